# Optimizing a Trainium2 kernel written in Bass

```python
import math, functools
import jax, jax.numpy as jnp
from jax import lax
import numpy as np

D_MODEL = 1024
BATCH = 4
SEQ = 4096
DEPTH = 1
DEC_BATCH = 128
DEC_SEQ = 8
PAST_LEN = 2048
PAGE_SIZE = 128

SB_HEADS = 8
SB_HEAD_DIM = 64
SB_WIDTH = SB_HEADS * SB_HEAD_DIM
SB_BIAS_INIT = -6.0
SSM_GROUP = 16
SSM_WIDTH = D_MODEL // 2
SSM_GROUPS = SSM_WIDTH // SSM_GROUP
SSM_STATE = 64
D_FF = 2816
Q_BLOCK = 128
ADA_CHUNKS = 9
RMS_EPS = 1e-6
DT_MIN = 1e-3
DT_MAX = 1e-1
IN_SPLITS = (SB_WIDTH, 2 * SB_WIDTH, 3 * SB_WIDTH, 3 * SB_WIDTH + SSM_WIDTH,
             3 * SB_WIDTH + SSM_WIDTH + D_MODEL)
IN_WIDTH = 3 * SB_WIDTH + SSM_WIDTH + 2 * D_MODEL

kernel_name = 'stickbreak_s5_macaron_adaln_step'


def rmsnorm(x, gain):
    xf = x.astype(jnp.float32)
    n = xf * lax.rsqrt(jnp.mean(xf * xf, axis=-1, keepdims=True) + RMS_EPS)
    return (n * gain.astype(jnp.float32)).astype(x.dtype)


def rms_modulate(x, gain, shift, scale):
    xf = x.astype(jnp.float32)
    n = xf * lax.rsqrt(jnp.mean(xf * xf, axis=-1, keepdims=True) + RMS_EPS) * gain.astype(jnp.float32)
    n = n * (1.0 + scale[:, None, :].astype(jnp.float32)) + shift[:, None, :].astype(jnp.float32)
    return n.astype(x.dtype)


def swiglu(u, w_in, w_out):
    g, up = jnp.split(u @ w_in, 2, axis=-1)
    return (jax.nn.silu(g) * up) @ w_out


def stick_breaking(q, k, v, bias, q_pos, k_pos):
    z = (jnp.einsum('bqhd,bkhd->bhqk', q, k).astype(jnp.float32) * (SB_HEAD_DIM ** -0.5)
         + bias.astype(jnp.float32)[None, :, None, None])
    mask = k_pos[None, :] < q_pos[:, None]
    log_beta = jax.nn.log_sigmoid(z)
    log_keep = jnp.where(mask, log_beta - z, 0.0)
    after = lax.cumsum(log_keep, axis=3, reverse=True) - log_keep
    w = jnp.where(mask, jnp.exp(log_beta + after), 0.0)
    return jnp.einsum('bhqk,bkhd->bqhd', w.astype(v.dtype), v)


def sb_prompt(q, k, v, bias):
    b, s, h, d = q.shape
    nblk = s // Q_BLOCK
    qb = q.reshape(b, nblk, Q_BLOCK, h, d).transpose(1, 0, 2, 3, 4)
    k_pos = jnp.arange(s)

    def one_block(args):
        q_blk, i = args
        q_pos = i * Q_BLOCK + jnp.arange(Q_BLOCK)
        return stick_breaking(q_blk, k, v, bias, q_pos, k_pos)

    out = lax.map(one_block, (qb, jnp.arange(nblk)))
    return out.transpose(1, 0, 2, 3, 4).reshape(b, s, h, d)


def sb_sample(q, k, v, bias, past_k, past_v):
    past = past_k.shape[1]
    t = q.shape[1]
    k_all = jnp.concatenate([past_k, k], axis=1)
    v_all = jnp.concatenate([past_v, v], axis=1)
    return stick_breaking(q, k_all, v_all, bias, past + jnp.arange(t), jnp.arange(past + t))


def s5_ssm(u, lam_re, lam_im, log_dt, b_re, b_im, c_re, c_im, d_skip, s0_re, s0_im):
    f32 = jnp.float32
    bsz, L, _ = u.shape
    uf = u.astype(f32)
    ug = uf.reshape(bsz, L, SSM_GROUPS, SSM_GROUP)
    lr, li = lam_re.astype(f32), lam_im.astype(f32)
    dt = jnp.exp(log_dt.astype(f32))[:, None]
    mag = jnp.exp(lr * dt)
    ab_re, ab_im = mag * jnp.cos(li * dt), mag * jnp.sin(li * dt)
    den = lr * lr + li * li
    nr, ni = ab_re - 1.0, ab_im
    zr, zi = (nr * lr + ni * li) / den, (ni * lr - nr * li) / den
    br, bi = b_re.astype(f32), b_im.astype(f32)
    bb_re = zr[..., None] * br - zi[..., None] * bi
    bb_im = zr[..., None] * bi + zi[..., None] * br
    bu_re = jnp.einsum('blgh,gph->blgp', ug, bb_re)
    bu_im = jnp.einsum('blgh,gph->blgp', ug, bb_im)
    s0r, s0i = s0_re.astype(f32), s0_im.astype(f32)
    bu_re = bu_re.at[:, 0].add(ab_re * s0r - ab_im * s0i)
    bu_im = bu_im.at[:, 0].add(ab_re * s0i + ab_im * s0r)
    a_re = jnp.broadcast_to(ab_re, bu_re.shape)
    a_im = jnp.broadcast_to(ab_im, bu_im.shape)

    def combine(e1, e2):
        a1r, a1i, b1r, b1i = e1
        a2r, a2i, b2r, b2i = e2
        return (a1r * a2r - a1i * a2i,
                a1r * a2i + a1i * a2r,
                a2r * b1r - a2i * b1i + b2r,
                a2r * b1i + a2i * b1r + b2i)

    _, _, x_re, x_im = lax.associative_scan(combine, (a_re, a_im, bu_re, bu_im), axis=1)
    y = (jnp.einsum('blgp,ghp->blgh', x_re, c_re.astype(f32))
         - jnp.einsum('blgp,ghp->blgh', x_im, c_im.astype(f32)))
    y = y.reshape(bsz, L, SSM_WIDTH) + d_skip.astype(f32) * uf
    return y.astype(u.dtype), x_re[:, -1], x_im[:, -1]


def token_mixer(u, lp, attend, s0_re, s0_im):
    bsz, L, _ = u.shape
    q, k, v, xs, ga, gb = jnp.split(u @ lp['w_in'], IN_SPLITS, axis=-1)
    q = q.reshape(bsz, L, SB_HEADS, SB_HEAD_DIM)
    k = k.reshape(bsz, L, SB_HEADS, SB_HEAD_DIM)
    v = v.reshape(bsz, L, SB_HEADS, SB_HEAD_DIM)
    o_a = attend(q, k, v, lp['sb_bias']).reshape(bsz, L, SB_WIDTH)
    y, sr, si = s5_ssm(xs, lp['ssm_lambda_re'], lp['ssm_lambda_im'], lp['ssm_log_dt'],
                       lp['ssm_b_re'], lp['ssm_b_im'], lp['ssm_c_re'], lp['ssm_c_im'],
                       lp['ssm_d'], s0_re, s0_im)
    zg = jax.nn.gelu(y)
    o_b = zg * jax.nn.sigmoid(zg @ lp['glu_w'] + lp['glu_b'])
    merged = (jax.nn.sigmoid(ga) * (o_a @ lp['w_branch_a'])
              + jax.nn.sigmoid(gb) * (o_b @ lp['w_branch_b']))
    return merged @ lp['w_out'], k, v, sr, si


def decoder_layer(x, c, lp, attend, s0_re, s0_im):
    mod = jax.nn.silu(c) @ lp['ada_w'] + lp['ada_b']
    sh1, sc1, g1, sh2, sc2, g2, sh3, sc3, g3 = jnp.split(mod, ADA_CHUNKS, axis=-1)
    h = x + 0.5 * g1[:, None, :] * swiglu(rms_modulate(x, lp['norm_ffn1'], sh1, sc1),
                                          lp['ffn1_w_in'], lp['ffn1_w_out'])
    mix, k, v, sr, si = token_mixer(rms_modulate(h, lp['norm_mix'], sh2, sc2), lp, attend, s0_re, s0_im)
    h = h + g2[:, None, :] * mix
    h = h + 0.5 * g3[:, None, :] * swiglu(rms_modulate(h, lp['norm_ffn2'], sh3, sc3),
                                          lp['ffn2_w_in'], lp['ffn2_w_out'])
    return h, k, v, sr, si


def setup_inputs(seed: int = 0) -> dict:
    key = jax.random.key(seed)
    ks = iter(jax.random.split(key, 40))
    f32 = jnp.float32

    def nrm(shape, scale=1.0):
        return scale * jax.random.normal(next(ks), shape, f32)

    n_pages = PAST_LEN // PAGE_SIZE
    n_used = DEC_BATCH * n_pages
    n_pool = n_used + max(1, n_used // 4)
    cache_shape = (DEPTH, n_pool, PAGE_SIZE, SB_HEADS, SB_HEAD_DIM)
    state_shape = (DEPTH, DEC_BATCH, SSM_GROUPS, SSM_STATE)
    x_prompt = nrm((BATCH, SEQ, D_MODEL))
    x_sample = nrm((DEC_BATCH, DEC_SEQ, D_MODEL))
    c_prompt = nrm((BATCH, D_MODEL))
    c_sample = nrm((DEC_BATCH, D_MODEL))
    cache_k = nrm(cache_shape)
    cache_v = nrm(cache_shape)
    state_ssm_re = nrm(state_shape, 0.1)
    state_ssm_im = nrm(state_shape, 0.1)
    page_table = jax.random.permutation(next(ks), n_pool)[:n_used].reshape(DEC_BATCH, n_pages).astype(jnp.int32)
    gs = (DEPTH, SSM_GROUPS, SSM_STATE)
    lam_im_base = jnp.broadcast_to(jnp.pi * jnp.arange(SSM_STATE, dtype=f32), gs)
    return {
        'x_prompt': x_prompt, 'x_sample': x_sample,
        'c_prompt': c_prompt, 'c_sample': c_sample,
        'cache_k': cache_k, 'cache_v': cache_v,
        'state_ssm_re': state_ssm_re, 'state_ssm_im': state_ssm_im,
        'page_table': page_table,
        'ada_w': nrm((DEPTH, D_MODEL, ADA_CHUNKS * D_MODEL), D_MODEL ** -0.5),
        'ada_b': nrm((DEPTH, ADA_CHUNKS * D_MODEL), 0.01),
        'norm_ffn1': 1.0 + nrm((DEPTH, D_MODEL), 0.01),
        'ffn1_w_in': nrm((DEPTH, D_MODEL, 2 * D_FF), D_MODEL ** -0.5),
        'ffn1_w_out': nrm((DEPTH, D_FF, D_MODEL), D_FF ** -0.5),
        'norm_mix': 1.0 + nrm((DEPTH, D_MODEL), 0.01),
        'w_in': nrm((DEPTH, D_MODEL, IN_WIDTH), D_MODEL ** -0.5),
        'sb_bias': SB_BIAS_INIT + nrm((DEPTH, SB_HEADS), 0.1),
        'ssm_lambda_re': -0.5 + nrm(gs, 0.01),
        'ssm_lambda_im': lam_im_base + nrm(gs, 0.01),
        'ssm_log_dt': jax.random.uniform(next(ks), (DEPTH, SSM_GROUPS), f32, math.log(DT_MIN), math.log(DT_MAX)),
        'ssm_b_re': nrm((DEPTH, SSM_GROUPS, SSM_STATE, SSM_GROUP), (2 * SSM_GROUP) ** -0.5),
        'ssm_b_im': nrm((DEPTH, SSM_GROUPS, SSM_STATE, SSM_GROUP), (2 * SSM_GROUP) ** -0.5),
        'ssm_c_re': nrm((DEPTH, SSM_GROUPS, SSM_GROUP, SSM_STATE), SSM_STATE ** -0.5),
        'ssm_c_im': nrm((DEPTH, SSM_GROUPS, SSM_GROUP, SSM_STATE), SSM_STATE ** -0.5),
        'ssm_d': nrm((DEPTH, SSM_WIDTH)),
        'glu_w': nrm((DEPTH, SSM_WIDTH, SSM_WIDTH), SSM_WIDTH ** -0.5),
        'glu_b': nrm((DEPTH, SSM_WIDTH), 0.01),
        'w_branch_a': nrm((DEPTH, SB_WIDTH, D_MODEL), SB_WIDTH ** -0.5),
        'w_branch_b': nrm((DEPTH, SSM_WIDTH, D_MODEL), SSM_WIDTH ** -0.5),
        'w_out': nrm((DEPTH, D_MODEL, D_MODEL), D_MODEL ** -0.5),
        'norm_ffn2': 1.0 + nrm((DEPTH, D_MODEL), 0.01),
        'ffn2_w_in': nrm((DEPTH, D_MODEL, 2 * D_FF), D_MODEL ** -0.5),
        'ffn2_w_out': nrm((DEPTH, D_FF, D_MODEL), D_FF ** -0.5),
        'final_norm': 1.0 + nrm((D_MODEL,), 0.01),
    }


def reference(x_prompt, x_sample, c_prompt, c_sample, cache_k, cache_v, state_ssm_re, state_ssm_im,
              page_table, ada_w, ada_b, norm_ffn1, ffn1_w_in, ffn1_w_out, norm_mix, w_in, sb_bias,
              ssm_lambda_re, ssm_lambda_im, ssm_log_dt, ssm_b_re, ssm_b_im, ssm_c_re, ssm_c_im,
              ssm_d, glu_w, glu_b, w_branch_a, w_branch_b, w_out, norm_ffn2, ffn2_w_in,
              ffn2_w_out, final_norm):
    dec_b, n_pages = page_table.shape
    past_len = n_pages * PAGE_SIZE
    hp, hs = x_prompt, x_sample
    zero_state = jnp.zeros((x_prompt.shape[0], SSM_GROUPS, SSM_STATE), x_prompt.dtype)
    kp_l, vp_l, srp_l, sip_l, ks_l, vs_l, srs_l, sis_l = [], [], [], [], [], [], [], []
    for l in range(DEPTH):
        lp = dict(ada_w=ada_w[l], ada_b=ada_b[l], norm_ffn1=norm_ffn1[l], ffn1_w_in=ffn1_w_in[l],
                  ffn1_w_out=ffn1_w_out[l], norm_mix=norm_mix[l], w_in=w_in[l], sb_bias=sb_bias[l],
                  ssm_lambda_re=ssm_lambda_re[l], ssm_lambda_im=ssm_lambda_im[l],
                  ssm_log_dt=ssm_log_dt[l], ssm_b_re=ssm_b_re[l], ssm_b_im=ssm_b_im[l],
                  ssm_c_re=ssm_c_re[l], ssm_c_im=ssm_c_im[l], ssm_d=ssm_d[l], glu_w=glu_w[l],
                  glu_b=glu_b[l], w_branch_a=w_branch_a[l], w_branch_b=w_branch_b[l], w_out=w_out[l],
                  norm_ffn2=norm_ffn2[l], ffn2_w_in=ffn2_w_in[l], ffn2_w_out=ffn2_w_out[l])
        past_k = cache_k[l][page_table].reshape(dec_b, past_len, SB_HEADS, SB_HEAD_DIM)
        past_v = cache_v[l][page_table].reshape(dec_b, past_len, SB_HEADS, SB_HEAD_DIM)
        hp, kp, vp, srp, sip = decoder_layer(hp, c_prompt, lp, sb_prompt, zero_state, zero_state)
        hs, ks, vs, srs, sis = decoder_layer(hs, c_sample, lp,
                                             functools.partial(sb_sample, past_k=past_k, past_v=past_v),
                                             state_ssm_re[l], state_ssm_im[l])
        kp_l.append(kp); vp_l.append(vp); srp_l.append(srp); sip_l.append(sip)
        ks_l.append(ks); vs_l.append(vs); srs_l.append(srs); sis_l.append(sis)
    y_prompt = rmsnorm(hp, final_norm)
    y_sample = rmsnorm(hs, final_norm)
    return (y_prompt, y_sample,
            jnp.stack(kp_l), jnp.stack(vp_l), jnp.stack(srp_l), jnp.stack(sip_l),
            jnp.stack(ks_l), jnp.stack(vs_l), jnp.stack(srs_l), jnp.stack(sis_l))
```

```python
import numpy as np
from contextlib import ExitStack
import concourse.bass as bass
import concourse.mybir as mybir
from concourse.bass_utils import run_bass_kernel_spmd

F32 = mybir.dt.float32
BF16 = mybir.dt.bfloat16
I32 = mybir.dt.int32
AF = mybir.ActivationFunctionType
ALU = mybir.AluOpType

D = 1024
DFF = 2816
NF = 22
SEQ = 4096
NBLK = 32
NOWN = 16
PI = 3.14159265358979
TWO_PI = 6.283185307179586


import types


def _snap(fn):
    if fn.__closure__ is None:
        return fn
    cells = []
    for c in fn.__closure__:
        try:
            cells.append(types.CellType(c.cell_contents))
        except ValueError:
            cells.append(c)
    return types.FunctionType(fn.__code__, fn.__globals__, fn.__name__, fn.__defaults__, tuple(cells))


class Prog:
    CE = ("pe", "act", "dve", "pool")
    QE = ("sp", "act", "pool")

    def __init__(self, nc, sems, n_dma_sems=(28, 4, 24)):
        self.nc = nc
        self.streams = {e: [] for e in ("pe", "act", "dve", "pool", "sp")}
        self.count = {e: 0 for e in self.CE}
        self.res = {}
        self.waited = {e: {} for e in self.streams}
        self.sems = {}
        it = iter(sems)
        for e in self.CE:
            self.sems[e] = next(it)
        self.dma_pool = {}
        for q, n in zip(self.QE, n_dma_sems):
            self.dma_pool[q] = [[next(it), 0] for _ in range(n)]
        self.dma_rr = {q: 0 for q in self.QE}
        for q in self.QE:
            for i, sv in enumerate(self.dma_pool[q]):
                self.sems[("d", q, i)] = sv[0]
        self.final_events = {}

    def _deps(self, eng, reads, writes):
        evs = {}

        def add(k, v):
            if v > evs.get(k, 0):
                evs[k] = v

        for r in reads:
            st = self.res.get(r)
            if st and st["w"]:
                add(*st["w"])
        for w in writes:
            st = self.res.get(w)
            if st:
                if st["w"]:
                    add(*st["w"])
                for k, v in st["r"].items():
                    if k == eng:
                        continue
                    add(k, v)
        if eng == "pe":
            evs.pop("pe", None)
        return evs

    def _commit(self, ev, reads, writes):
        k, v = ev
        for r in reads:
            st = self.res.setdefault(r, {"w": None, "r": {}})
            if v > st["r"].get(k, 0):
                st["r"][k] = v
        for w in writes:
            self.res[w] = {"w": ev, "r": {}}

    def _waits(self, eng, evs):
        out = []
        wd = self.waited[eng]
        for k, v in evs.items():
            if wd.get(k, 0) >= v:
                continue
            wd[k] = v
            out.append((k, v))
        return out

    def op(self, eng, fn, reads=(), writes=()):
        fn = _snap(fn)
        evs = self._deps(eng, reads, writes)
        waits = self._waits(eng, evs)
        self.count[eng] += 1
        ev = (eng, self.count[eng])
        self.streams[eng].append(("op", waits, fn, ev))
        self._commit(ev, reads, writes)
        return ev

    def dma(self, q, fn, reads=(), writes=()):
        fn = _snap(fn)
        evs = self._deps(None, reads, writes)
        pool = self.dma_pool[q]
        i = self.dma_rr[q]
        self.dma_rr[q] = (i + 1) % len(pool)
        key = ("d", q, i)
        prev = pool[i][1]
        if prev > 0:
            evs[key] = max(evs.get(key, 0), prev)
        waits = self._waits(q, evs)
        pool[i][1] = prev + 16
        ev = (key, prev + 16)
        self.streams[q].append(("dma", waits, fn, ev))
        self._commit(ev, reads, writes)
        self.final_events[key] = prev + 16
        return ev

    def barrier(self):
        evs = {e: self.count[e] for e in self.CE if self.count[e] > 0}
        evs.update(self.final_events)
        for e in self.streams:
            w = self._waits(e, {k: v for k, v in evs.items() if k != e or e != "pe"})
            if w:
                self.streams[e].append(("wait", w, None, None))
        self.res = {}

    def emit(self):
        nc = self.nc
        handles = {"pe": "tensor", "act": "scalar", "dve": "vector", "pool": "gpsimd", "sp": "sync"}
        with nc.Block() as block:
            for e, hname in handles.items():
                stream = self.streams[e]

                def body(engine, stream=stream, e=e):
                    for kind, waits, fn, ev in stream:
                        for k, v in waits:
                            engine.wait_ge(self.sems[k], v)
                        if kind == "wait":
                            continue
                        ins = fn(engine)
                        if kind == "op":
                            ins.then_inc(self.sems[e], 1)
                        else:
                            ins.then_inc(self.sems[ev[0]], 16)

                getattr(block, hname)(body)


class Arena:
    def __init__(self, big, nbytes):
        self.f = big
        self.b = big.bitcast(BF16)
        self.i = big.bitcast(I32)
        self.n = nbytes
        self.top = 0
        self.marks = []

    def alloc(self, nbytes, dt=F32, shape=None):
        nbytes = (nbytes + 15) // 16 * 16
        o = self.top
        self.top += nbytes
        assert self.top <= self.n, f"SBUF arena overflow {self.top} > {self.n}"
        if dt == F32:
            v = self.f[:, o // 4:(o + nbytes) // 4]
        elif dt == BF16:
            v = self.b[:, o // 2:(o + nbytes) // 2]
        else:
            v = self.i[:, o // 4:(o + nbytes) // 4]
        return v

    def f32(self, *shape):
        n = int(np.prod(shape))
        v = self.alloc(n * 4, F32)[:, 0:n]
        return _shape(v, shape)

    def bf16(self, *shape):
        n = int(np.prod(shape))
        v = self.alloc(n * 2, BF16)[:, 0:n]
        return _shape(v, shape)

    def i32(self, *shape):
        n = int(np.prod(shape))
        v = self.alloc(n * 4, I32)[:, 0:n]
        return _shape(v, shape)

    def mark(self):
        self.marks.append(self.top)

    def release(self):
        self.top = self.marks.pop()


def _shape(v, shape):
    if len(shape) == 1:
        return v
    if len(shape) == 2:
        return v.rearrange("p (a b) -> p a b", a=shape[0])
    if len(shape) == 3:
        return v.rearrange("p (a b c) -> p a b c", a=shape[0], b=shape[1])
    raise ValueError


_uid = [0]


def uid(s):
    _uid[0] += 1
    return f"{s}#{_uid[0]}"


def build_program():
    nc = bass.Bass("TRN2", target_bir_lowering=False)
    di = {}

    def inp(name, shape, dt=F32):
        di[name] = nc.dram_tensor(name, list(shape), dt, kind="ExternalInput").ap()
        return di[name]

    def outp(name, shape, dt=F32):
        di[name] = nc.dram_tensor(name, list(shape), dt, kind="ExternalOutput").ap()
        return di[name]

    def scratch(name, shape, dt=F32):
        di[name] = nc.dram_tensor(name, list(shape), dt, kind="Internal").ap()
        return di[name]

    xp = inp("xp", [SEQ, D]); xs = inp("xs", [128, D])
    cpr = inp("cpr", [128, D]); csr = inp("csr", [128, D])
    cache_k = inp("cache_k", [2560 * 128, 512]); cache_v = inp("cache_v", [2560 * 128, 512])
    ptab = inp("ptab", [1, 256], I32)
    st_re = inp("st_re", [128, 16, 16]); st_im = inp("st_im", [128, 16, 16])
    ada_w = inp("ada_w", [D, 9 * D]); ada_b = inp("ada_b", [1, 9 * D])
    n1 = inp("norm_ffn1", [1, D]); n2 = inp("norm_mix", [1, D]); n3 = inp("norm_ffn2", [1, D]); nfin = inp("final_norm", [1, D])
    f1_in = inp("ffn1_w_in", [D, 2 * DFF]); f1_out = inp("ffn1_w_out", [DFF, D])
    f2_in = inp("ffn2_w_in", [D, 2 * DFF]); f2_out = inp("ffn2_w_out", [DFF, D])
    w_in = inp("w_in", [D, 4096]); sbb_d = inp("sb_bias", [1, 8])
    lam_re = inp("lam_re", [128, 16]); lam_im = inp("lam_im", [128, 16]); logdt = inp("logdt", [128, 16])
    bre_d = inp("bfull_re", [16, 128, 128]); bim_d = inp("bfull_im", [16, 128, 128])
    cre_d = inp("cfull_re", [16, 128, 128]); cim_d = inp("cfull_im", [16, 128, 128])
    dsk_d = inp("ssm_d", [128, 4]); glu_w = inp("glu_w", [512, 512]); glub_d = inp("glu_b", [128, 4])
    wba = inp("w_branch_a", [512, D]); wbb = inp("w_branch_b", [512, D]); w_o = inp("w_out", [D, D])
    ident_d = inp("ident", [128, 128]); iotap_d = inp("iota_p", [128, 1]); iotaj_d = inp("iota_j", [128, 512])
    ltri_d = inp("ltri", [128, 128]); amask_d = inp("amask", [8, 128, 512]); smask_d = inp("smask", [128, 16, 128])
    idxown_d = inp("idx_own", [128, 16], I32); p0p1_d = inp("p0p1", [128, 2]); inp("sb_rows", [128, 1])

    y_p = outp("y_p", [NOWN * 128, D]); y_s = outp("y_s", [128, D])
    kp_o = outp("kp", [SEQ, 512]); vp_o = outp("vp", [SEQ, 512])
    spre_o = outp("sp_re", [128, 16]); spim_o = outp("sp_im", [128, 16])
    ks_o = outp("ks", [128, 512]); vs_o = outp("vs", [128, 512])
    ssre_o = outp("ss_re", [128, 16, 16]); ssim_o = outp("ss_im", [128, 16, 16])

    MOD = scratch("MOD", [2, 128, 9 * D])
    H1 = scratch("H1", [33 * 128, D]); XS2 = scratch("XS2", [33 * 128, D], BF16)
    SGA = scratch("SGA", [17, 128, 1024], BF16); SGB = scratch("SGB", [17, 128, 1024], BF16)
    YT = scratch("YT", [17, 128, 512])
    H2 = scratch("H2", [17 * 128, D]); XS3 = scratch("XS3", [17 * 128, D], BF16)

    with ExitStack() as es:
        sems = [es.enter_context(nc.semaphore(f"s{i}")) for i in range(64)]
        ARENA_BYTES = 207 * 1024
        big = es.enter_context(nc.sbuf_tensor("big", [128, ARENA_BYTES // 4], F32))
        psf = [es.enter_context(nc.psum_tensor(f"ps{i}", [128, 512], F32)) for i in range(8)]
        psb = [p.bitcast(BF16) for p in psf]
        P = Prog(nc, sems)
        A = Arena(big, ARENA_BYTES)

        def PSN(i):
            return f"ps{i}"

        wstage = {"n": 0, "tiles": None, "on": False}
        ident_b = A.bf16(128); ones_b = A.bf16(128); ltri_b = A.bf16(128)
        ones_f = A.f32(128)
        iota_p = A.f32(1); p0p1 = A.f32(2); sbb = A.f32(8); epsb = A.f32(1)
        P.dma("pool", lambda e: e.dma_start(out=ident_b, in_=ident_d), writes=["ident_b"])
        P.dma("pool", lambda e: e.dma_start(out=ltri_b, in_=ltri_d), writes=["ltri_b"])
        P.dma("sp", lambda e: e.dma_start(out=iota_p, in_=iotap_d), writes=["iota_p"])
        P.dma("sp", lambda e: e.dma_start(out=p0p1, in_=p0p1_d), writes=["p0p1"])
        P.dma("sp", lambda e: e.dma_start(out=sbb, in_=sbb_d.partition_broadcast(128)), writes=["sbb"])
        P.op("dve", lambda e: e.memset(ones_b, 1.0), writes=["ones_b"])
        P.op("dve", lambda e: e.memset(ones_f, 1.0), writes=["ones_f"])
        P.op("dve", lambda e: e.memset(epsb, 1e-6), writes=["epsb"])
        CONST = ["ident_b", "ones_b", "ltri_b", "ones_f", "iota_p", "p0p1", "sbb", "epsb"]
        WST_BYTES = 3 * 4096
        wstage["tiles"] = [A.f[:, (ARENA_BYTES - WST_BYTES) // 4 + 1024 * i:(ARENA_BYTES - WST_BYTES) // 4 + 1024 * (i + 1)] for i in range(3)]

        def after_barrier():
            pass


        def rstd_from(x_ap, x_res, junk, ss, rstd, tag):
            P.op("act", lambda e: e.activation(junk, x_ap, AF.Square, accum_out=ss), reads=[x_res], writes=[tag + "junk", tag + "ss"])
            P.op("act", lambda e: e.activation(rstd, ss, AF.Sqrt, scale=1.0 / D, bias=epsb), reads=[tag + "ss"], writes=[tag + "rs0"])
            P.op("dve", lambda e: e.reciprocal(rstd, rstd), reads=[tag + "rs0"], writes=[tag + "rstd"])

        def transpose_block(src_b, src_res, dstT, dst_res, pbank):
            for k in range(8):
                P.op("pe", lambda e, k=k: e.transpose(psb[pbank][:, k * 128:(k + 1) * 128], src_b[:, k * 128:(k + 1) * 128], ident_b),
                     reads=[src_res, "ident_b"], writes=[PSN(pbank)])
            P.op("act", lambda e: e.copy(dstT, psb[pbank][:, 0:1024].rearrange("p (k n) -> p k n", k=8)), reads=[PSN(pbank)], writes=[dst_res])

        def _stage_cast(dst_view, src_view, res, fine):
            if not wstage["on"]:
                P.dma("pool", lambda e: e.dma_start(out=dst_view, in_=src_view), writes=[res])
                return
            i = wstage["n"]; wstage["n"] += 1
            sl = i % len(wstage["tiles"])
            st = wstage["tiles"][sl]
            shp = dst_view.shape
            stv = st[:, 0:shp[1] * shp[2]].rearrange("p (k n) -> p k n", k=shp[1])
            P.dma("sp", lambda e: e.dma_start(out=stv, in_=src_view), writes=[f"wst{sl}"])
            if fine and i % 2 == 1:
                P.op("act", lambda e: e.copy(dst_view, stv), reads=[f"wst{sl}"], writes=[res])
            else:
                P.op("pool", lambda e: e.tensor_copy(dst_view, stv), reads=[f"wst{sl}"], writes=[res])

        def load_w_cols(dst, src, c0, c1, res, step=128, order=None, fine=False):
            starts = order if order is not None else list(range(c0, c1, step))
            for a in starts:
                b = min(a + step, c1)
                _stage_cast(dst[:, :, a - c0:b - c0], src[:, a:b].rearrange("(k p) n -> p k n", p=128), f"{res}:{a - c0}" if fine else res, fine)

        def load_w_rows(dst, src, nchunk, res, step=1, fine=False):
            ncols = dst.shape[2]
            per = max(1, 1024 // ncols)
            for a in range(0, nchunk, per):
                b = min(a + per, nchunk)
                _stage_cast(dst[:, a:b, :], src[a * 128:b * 128, :].rearrange("(k p) n -> p k n", p=128), f"{res}:{a}" if fine else res, fine)

        def ffn_mm1(uT, uT_res, W1, actT, tag):
            for f in range(NF):
                pb = 2 + (f % 2)
                for k in range(8):
                    P.op("pe", lambda e, f=f, k=k, pb=pb: e.matmul(psf[pb][:, 0:128], W1[:, k, f * 128:(f + 1) * 128], uT[:, k, :], start=(k == 0), stop=(k == 7)),
                         reads=[uT_res, f"{tag}W1:{f * 128}"], writes=[PSN(pb)])
                for k in range(8):
                    P.op("pe", lambda e, f=f, k=k, pb=pb: e.matmul(psf[pb][:, 128:256], W1[:, k, DFF + f * 128:DFF + (f + 1) * 128], uT[:, k, :], start=(k == 0), stop=(k == 7)),
                         reads=[uT_res, f"{tag}W1:{DFF + f * 128}"], writes=[PSN(pb)])
                sg = sgs[f % 2]
                P.op("act", lambda e, pb=pb, sg=sg: e.activation(sg, psf[pb][:, 0:128], AF.Silu), reads=[PSN(pb)], writes=[f"sg{f % 2}"])
                P.op("dve", lambda e, f=f, pb=pb, sg=sg: e.tensor_tensor(actT[:, f, :], sg, psf[pb][:, 128:256], ALU.mult), reads=[f"sg{f % 2}", PSN(pb)], writes=[f"{tag}actT{f}"])

        def ffn_mm2(actT, W2, tag):
            for half in range(2):
                pb = 4 + half
                for f in range(NF):
                    P.op("pe", lambda e, f=f, half=half, pb=pb: e.matmul(psf[pb][:, 0:512], actT[:, f, :], W2[:, f, half * 512:(half + 1) * 512], start=(f == 0), stop=(f == NF - 1)),
                         reads=[f"{tag}actT{f}", f"{tag}W2:{f}"], writes=[PSN(pb)])

        def ffn_update(hres_in, h_ap, ghalf, tag):
            for half in range(2):
                pb = 4 + half
                hs = h_ap[:, half * 512:(half + 1) * 512]
                tm = tmp512
                P.op("dve", lambda e, half=half, pb=pb, tm=tm: e.tensor_tensor(tm, psf[pb][:, 0:512], ghalf[:, half * 512:(half + 1) * 512], ALU.mult), reads=[PSN(pb), tag + "mod"], writes=["tmp512"])
                P.op("dve", lambda e, hs=hs, tm=tm: e.tensor_tensor(hs, hs, tm, ALU.add), reads=["tmp512", hres_in], writes=[hres_in])

        def norm_mod(x_ap, x_res, Arow, shrow, out_b, out_res, tag):
            rstd_from(x_ap, x_res, junkD, ss1, rstd1, tag)
            P.op("dve", lambda e: e.scalar_tensor_tensor(tmpD, x_ap, rstd1, Arow, ALU.mult, ALU.mult), reads=[x_res, tag + "rstd", tag + "mod"], writes=["tmpD"])
            P.op("dve", lambda e: e.tensor_tensor(out_b, tmpD, shrow, ALU.add), reads=["tmpD", tag + "mod"], writes=[out_res])

        wstage["on"] = True
        A.n = ARENA_BYTES - WST_BYTES
        A.mark()
        scT = [A.bf16(8, 128), A.bf16(8, 128)]
        ctile = A.f32(D); cb = A.bf16(D)
        for g, src in enumerate((cpr, csr)):
            P.dma("sp", lambda e, src=src: e.dma_start(out=ctile, in_=src), writes=["ctile"])
            P.op("act", lambda e: e.activation(cb, ctile, AF.Silu), reads=["ctile"], writes=["cb"])
            transpose_block(cb, "cb", scT[g], f"scT{g}", 0)
        wA = [A.bf16(8, 512), A.bf16(8, 512)]
        brow = [A.f32(512), A.f32(512)]
        stg = [A.f32(512), A.f32(512)]
        for cg in range(18):
            s = cg % 2
            load_w_cols(wA[s], ada_w, cg * 512, (cg + 1) * 512, f"wA{s}", fine=True)
            wa_res = [f"wA{s}:{128 * q_}" for q_ in range(4)]
            P.dma("sp", lambda e, cg=cg, s=s: e.dma_start(out=brow[s][0:1, :], in_=ada_b[0:1, cg * 512:(cg + 1) * 512]), writes=[f"brow{s}"])
            for g in range(2):
                pb = 2 + g
                for k in range(8):
                    P.op("pe", lambda e, k=k, g=g, s=s, pb=pb: e.matmul(psf[pb][:, 0:512], scT[g][:, k, :], wA[s][:, k, :], start=(k == 0), stop=False),
                         reads=[f"scT{g}"] + wa_res, writes=[PSN(pb)])
                P.op("pe", lambda e, s=s, pb=pb: e.matmul(psf[pb][:, 0:512], ones_f[0:1, :], brow[s][0:1, :], start=False, stop=True),
                     reads=["ones_f", f"brow{s}"], writes=[PSN(pb)])
                P.op("act", lambda e, g=g, pb=pb: e.copy(stg[g], psf[pb][:, 0:512]), reads=[PSN(pb)], writes=[f"stg{g}"])
                P.dma("sp", lambda e, g=g, cg=cg: e.dma_start(out=MOD[g, :, cg * 512:(cg + 1) * 512], in_=stg[g]), reads=[f"stg{g}"], writes=[f"MOD{g}"])
        P.barrier()
        A.release()

        def load_mod(dst, g, chunk, res):
            P.dma("sp", lambda e: e.dma_start(out=dst, in_=MOD[g, :, chunk * D:(chunk + 1) * D]), writes=[res])

        def load_row_bcast(dst, src, res):
            P.dma("sp", lambda e: e.dma_start(out=dst, in_=src.partition_broadcast(128)), writes=[res])

        A.mark()
        W1 = A.bf16(8, 2 * DFF); W2 = A.bf16(NF, D)
        load_w_cols(W1, f1_in, 0, 2 * DFF, "f1W1", order=[x for f_ in range(NF) for x in (f_ * 128, DFF + f_ * 128)], fine=True)
        load_w_rows(W2, f1_out, NF, "f1W2", fine=True)
        A1 = A.f32(D); sh1 = A.f32(D); g1h = A.f32(D); A2 = A.f32(D); sh2 = A.f32(D); grow = A.f32(D)
        xt = [A.f32(D), A.f32(D)]
        xsb = [A.bf16(D), A.bf16(D)]; uT = [A.bf16(8, 128), A.bf16(8, 128)]; actT = A.bf16(NF, 128)
        sgs = [A.f32(128), A.f32(128)]
        tmp512 = A.f32(512); tmpD = A.f32(D); junkD = A.bf16(D)
        ss1 = A.f32(1); rstd1 = A.f32(1)
        xs2b = [A.bf16(D), A.bf16(D)]
        for g in range(2):
            load_mod(sh1, g, 0, "f1mod"); load_mod(A1, g, 1, "f1mod"); load_mod(g1h, g, 2, "f1mod")
            load_mod(sh2, g, 3, "f1mod"); load_mod(A2, g, 4, "f1mod")
            load_row_bcast(grow, n1, "grow")
            P.op("dve", lambda e: e.scalar_tensor_tensor(A1, A1, 1.0, grow, ALU.add, ALU.mult), reads=["f1mod", "grow"], writes=["f1mod"])
            P.op("dve", lambda e: e.tensor_scalar(g1h, g1h, 0.5, None, ALU.mult), reads=["f1mod"], writes=["f1mod"])
            load_row_bcast(grow, n2, "grow")
            P.op("dve", lambda e: e.scalar_tensor_tensor(A2, A2, 1.0, grow, ALU.add, ALU.mult), reads=["f1mod", "grow"], writes=["f1mod"])
            blocks = list(range(NBLK)) if g == 0 else [NBLK]

            def prep_norm(b, s):
                src = xp[b * 128:(b + 1) * 128, :] if b < NBLK else xs
                P.dma("sp", lambda e: e.dma_start(out=xt[s], in_=src), writes=[f"x{s}"])
                norm_mod(xt[s], f"x{s}", A1, sh1, xsb[s], f"xsb{s}", "f1")

            def prep_tr(s):
                transpose_block(xsb[s], f"xsb{s}", uT[s], f"uT{s}", 0)

            prep_norm(blocks[0], 0)
            prep_tr(0)
            for i, b in enumerate(blocks):
                s = i % 2
                x = xt[s]; xr = f"x{s}"
                ffn_mm1(uT[s], f"uT{s}", W1, actT, "f1")
                if i + 1 < len(blocks):
                    prep_norm(blocks[i + 1], 1 - s)
                ffn_mm2(actT, W2, "f1")
                if i + 1 < len(blocks):
                    prep_tr(1 - s)
                ffn_update(xr, x, g1h, "f1")
                P.dma("sp", lambda e, x=x, b=b: e.dma_start(out=H1[b * 128:(b + 1) * 128, :], in_=x), reads=[xr], writes=["H1"])
                norm_mod(x, xr, A2, sh2, xs2b[s], f"xs2b{s}", "f1")
                P.dma("sp", lambda e, s=s, b=b: e.dma_start(out=XS2[b * 128:(b + 1) * 128, :], in_=xs2b[s]), reads=[f"xs2b{s}"], writes=["XS2"])
        P.barrier()
        A.release()

        A.mark()
        OT_BYTES = 4 * (NOWN * 128 + 128) * 2
        OT = A.b[:, (ARENA_BYTES - OT_BYTES) // 2:ARENA_BYTES // 2].rearrange("p (a b) -> p a b", a=4)
        QTs = A.bf16(4, 128); KTn = A.bf16(4, 128); VBn = A.f32(512)
        ident_f = A.f32(128)
        P.dma("sp", lambda e: e.dma_start(out=ident_f, in_=ident_d), writes=["ident_f"])
        A.mark()
        KT = A.bf16(4, SEQ)
        VB = A.bf16(NBLK, 512)
        QT = A.bf16(4, NOWN * 128)
        A.mark()
        ST = A.bf16(4, SEQ + 128)
        A.mark()
        WinA = A.bf16(8, 1536)
        load_w_cols(WinA, w_in, 512, 2048, "Win")
        xin = [A.bf16(D), A.bf16(D)]
        u2T = A.bf16(8, 128)
        kf = A.f32(512); vf = A.f32(512)
        idxo = A.i32(16)
        sgat = A.bf16(8, 128); sgbt = A.bf16(8, 128)
        P.dma("sp", lambda e: e.dma_start(out=idxo, in_=idxown_d), writes=["idxo"])
        for b in range(NBLK + 1):
            s = b % 2
            P.dma("sp", lambda e, s=s, b=b: e.dma_start(out=xin[s], in_=XS2[b * 128:(b + 1) * 128, :]), writes=[f"xin{s}"])
            transpose_block(xin[s], f"xin{s}", u2T, "u2T", 0)
            for (c0, dstf, pb, nm) in ((512, kf, 2, "kf"), (1024, vf, 3, "vf")):
                for k in range(8):
                    P.op("pe", lambda e, k=k, c0=c0, pb=pb: e.matmul(psf[pb][:, 0:512], u2T[:, k, :], WinA[:, k, c0 - 512:c0], start=(k == 0), stop=(k == 7)),
                         reads=["u2T", "Win"], writes=[PSN(pb)])
                P.op("act", lambda e, dstf=dstf, pb=pb: e.copy(dstf, psf[pb][:, 0:512]), reads=[PSN(pb)], writes=[nm])
            ko = kp_o[b * 128:(b + 1) * 128, :] if b < NBLK else ks_o
            vo = vp_o[b * 128:(b + 1) * 128, :] if b < NBLK else vs_o
            P.dma("sp", lambda e, ko=ko: e.dma_start(out=ko, in_=kf), reads=["kf"], writes=[uid("ko")])
            P.dma("sp", lambda e, vo=vo: e.dma_start(out=vo, in_=vf), reads=["vf"], writes=[uid("vo")])
            if b < NBLK:
                P.op("pool", lambda e, b=b: e.tensor_copy(VB[:, b, :], vf), reads=["vf"], writes=[f"VB{b}"])
            else:
                P.op("pool", lambda e: e.tensor_copy(VBn, vf), reads=["vf"], writes=["VBn"])
            for (c0, dst, pb, nm) in ((512, KT, 4, "KT"), (1536, ST, 5, "ST")):
                for c in range(4):
                    for k in range(8):
                        P.op("pe", lambda e, k=k, c=c, c0=c0, pb=pb: e.matmul(psf[pb][:, c * 128:(c + 1) * 128], WinA[:, k, c0 - 512 + c * 128:c0 - 512 + (c + 1) * 128], u2T[:, k, :], start=(k == 0), stop=(k == 7)),
                             reads=["u2T", "Win"], writes=[PSN(pb)])
                dd = KTn if (nm == "KT" and b == NBLK) else dst[:, :, b * 128:(b + 1) * 128]
                P.op("act", lambda e, dd=dd, pb=pb: e.copy(dd, psf[pb][:, 0:512].rearrange("p (c n) -> p c n", c=4)),
                     reads=[PSN(pb)], writes=[f"{nm}{b}"])
        P.barrier()
        A.release()
        A.mark()
        WinQ = A.bf16(8, 512); WinG = A.bf16(8, 2048)
        load_w_cols(WinQ, w_in, 0, 512, "Win"); load_w_cols(WinG, w_in, 2048, 4096, "Win")
        xin = [A.bf16(D), A.bf16(D)]
        u2T = A.bf16(8, 128)
        idxo = A.i32(16)
        sgat = A.bf16(8, 128); sgbt = A.bf16(8, 128)
        P.dma("sp", lambda e: e.dma_start(out=idxo, in_=idxown_d), writes=["idxo"])
        for j in range(NOWN + 1):
            s = j % 2
            if j < NOWN:
                P.dma("pool", lambda e, s=s, j=j: e.indirect_dma_start(out=xin[s], out_offset=None, in_=XS2, in_offset=bass.IndirectOffsetOnAxis(ap=idxo[:, j:j + 1], axis=0)),
                      reads=["idxo"], writes=[f"xin{s}"])
            else:
                P.dma("sp", lambda e, s=s: e.dma_start(out=xin[s], in_=XS2[NBLK * 128:(NBLK + 1) * 128, :]), writes=[f"xin{s}"])
            transpose_block(xin[s], f"xin{s}", u2T, "u2T", 0)
            for c in range(4):
                for k in range(8):
                    P.op("pe", lambda e, k=k, c=c: e.matmul(psf[2][:, c * 128:(c + 1) * 128], WinQ[:, k, c * 128:(c + 1) * 128], u2T[:, k, :], start=(k == 0), stop=(k == 7)),
                         reads=["u2T", "Win"], writes=[PSN(2)])
            qd = QT[:, :, j * 128:(j + 1) * 128] if j < NOWN else QTs
            P.op("act", lambda e, qd=qd: e.copy(qd, psf[2][:, 0:512].rearrange("p (c n) -> p c n", c=4)), reads=[PSN(2)], writes=[f"QT{j}"])
            for (c0, dst, scr, nm) in ((2048, sgat, SGA, "sga"), (3072, sgbt, SGB, "sgb")):
                for hh in range(2):
                    pb = 4 + hh
                    for c in range(4):
                        cc = hh * 4 + c
                        for k in range(8):
                            P.op("pe", lambda e, k=k, c=c, cc=cc, c0=c0, pb=pb: e.matmul(psf[pb][:, c * 128:(c + 1) * 128], WinG[:, k, c0 - 2048 + cc * 128:c0 - 2048 + (cc + 1) * 128], u2T[:, k, :], start=(k == 0), stop=(k == 7)),
                                 reads=["u2T", "Win"], writes=[PSN(pb)])
                    P.op("act", lambda e, dst=dst, hh=hh, pb=pb: e.activation(dst[:, hh * 4:(hh + 1) * 4, :], psf[pb][:, 0:512].rearrange("p (c n) -> p c n", c=4), AF.Sigmoid),
                         reads=[PSN(pb)], writes=[nm])
                P.dma("sp", lambda e, dst=dst, scr=scr, j=j: e.dma_start(out=scr[j].rearrange("p (c n) -> p c n", c=8), in_=dst), reads=[nm], writes=[uid("sg")])
        P.barrier()
        A.release()

        wstage["on"] = False
        A.n = ARENA_BYTES
        A.mark()
        def pt16():
            return A.f32(16)
        lr = pt16(); li = pt16(); ldt = pt16(); dt_ = pt16(); rr = pt16(); th = pt16(); cth = pt16(); sth = pt16()
        abre = pt16(); abim = pt16(); nabim = pt16(); zr = pt16(); zi = pt16(); den = pt16(); q1 = pt16(); q2 = pt16(); nr = pt16()
        Fr = pt16(); Fi = pt16(); a512 = pt16(); init_re = pt16(); init_im = pt16(); fin_re = pt16(); fin_im = pt16()
        dsk = A.f32(4)
        Bre = A.bf16(16, 128); Bim = A.bf16(16, 128); Cre = A.bf16(16, 128); Cim = A.bf16(16, 128)
        iotaj = A.f32(512)
        tabs = [[A.f32(512) for _ in range(4)] for _ in range(4)]
        rt = [A.f32(512) for _ in range(4)]
        wi_re = A.f32(512); wi_im = A.f32(512); w_re = A.f32(512); w_im = A.f32(512); u1 = A.f32(512); u2 = A.f32(512)
        xre = A.bf16(4, 512); ximn = A.bf16(4, 512)
        ysb = wi_re; yown = wi_im[:, 0:256].rearrange("p (a n) -> p a n", a=2)
        _o = A.top; ki = A.i32(512); p1 = A.f[:, _o // 4:_o // 4 + 512]; kfl = u2; ang = u1; ang2 = wi_re
        sre0 = A.f32(16, 16); sim0 = A.f32(16, 16); sso_re = A.f32(16, 16); sso_im = A.f32(16, 16)
        bzr = A.f32(128); bzi = A.f32(128); xsr = w_re[:, 0:128]; xsi = w_im[:, 0:128]; st4 = [A.f32(16) for _ in range(4)]
        tq = A.f32(16); tq2 = A.f32(16)
        for dst, src, nm in ((lr, lam_re, "lr"), (li, lam_im, "li"), (ldt, logdt, "ldt"), (dsk, dsk_d, "dsk"), (iotaj, iotaj_d, "iotaj"),
                             (sre0, st_re, "sre0"), (sim0, st_im, "sim0")):
            P.dma("sp", lambda e, dst=dst, src=src: e.dma_start(out=dst, in_=src), writes=[nm])
        for dst, src, nm in ((Bre, bre_d, "Bre"), (Bim, bim_d, "Bim"), (Cre, cre_d, "Cre"), (Cim, cim_d, "Cim")):
            for hlf in range(2):
                P.dma("pool", lambda e, dst=dst, src=src, hlf=hlf: e.dma_start(out=dst[:, hlf * 8:(hlf + 1) * 8, :], in_=src[hlf * 8:(hlf + 1) * 8].rearrange("t p n -> p t n")), writes=[nm])

        def dve(fn, reads, writes):
            P.op("dve", fn, reads=reads, writes=writes)

        dve(lambda e: e.tensor_scalar(Cim, Cim, -1.0, None, ALU.mult), ["Cim"], ["Cim"])

        def sin_of(a_ap, a_res, out, out_res, n, shift=0.0):
            kiv, kfv, av = ki[:, 0:n], kfl[:, 0:n], ang2[:, 0:n]
            dve(lambda e: e.tensor_scalar(av, a_ap, shift, None, ALU.add), [a_res], ["wi_re"])
            dve(lambda e: e.tensor_scalar(kiv, av, 1.0 / TWO_PI, None, ALU.mult), ["wi_re"], ["ki"])
            dve(lambda e: e.tensor_copy(kfv, kiv), ["ki"], ["u2"])
            dve(lambda e: e.scalar_tensor_tensor(av, kfv, -TWO_PI, av, ALU.mult, ALU.add), ["u2", "wi_re"], ["wi_re"])
            dve(lambda e: e.tensor_scalar(av, av, -PI, PI, ALU.max, ALU.min), ["wi_re"], ["wi_re"])
            P.op("act", lambda e: e.activation(out, av, AF.Sin), reads=["wi_re"], writes=[out_res])

        P.op("act", lambda e: e.activation(dt_, ldt, AF.Exp), reads=["ldt"], writes=["dt"])
        dve(lambda e: e.tensor_tensor(q1, lr, dt_, ALU.mult), ["lr", "dt"], ["q1"])
        P.op("act", lambda e: e.activation(rr, q1, AF.Exp), reads=["q1"], writes=["rr"])
        dve(lambda e: e.tensor_tensor(th, li, dt_, ALU.mult), ["li", "dt"], ["th"])
        dve(lambda e: e.tensor_scalar(ki[:, 0:16], th, 1.0 / TWO_PI, None, ALU.mult), ["th"], ["ki"])
        dve(lambda e: e.tensor_copy(kfl[:, 0:16], ki[:, 0:16]), ["ki"], ["u2"])
        dve(lambda e: e.scalar_tensor_tensor(th, kfl[:, 0:16], -TWO_PI, th, ALU.mult, ALU.add), ["u2", "th"], ["th"])
        sin_of(th, "th", sth, "sth", 16)
        sin_of(th, "th", cth, "cth", 16, shift=PI / 2)
        dve(lambda e: e.tensor_tensor(abre, rr, cth, ALU.mult), ["rr", "cth"], ["abre"])
        dve(lambda e: e.tensor_tensor(abim, rr, sth, ALU.mult), ["rr", "sth"], ["abim"])
        dve(lambda e: e.tensor_scalar(nabim, abim, -1.0, None, ALU.mult), ["abim"], ["nabim"])
        dve(lambda e: e.tensor_tensor(den, lr, lr, ALU.mult), ["lr"], ["den"])
        dve(lambda e: e.tensor_tensor(q2, li, li, ALU.mult), ["li"], ["q2"])
        dve(lambda e: e.tensor_tensor(den, den, q2, ALU.add), ["den", "q2"], ["den"])
        dve(lambda e: e.reciprocal(den, den), ["den"], ["den"])
        dve(lambda e: e.tensor_scalar(nr, abre, -1.0, None, ALU.add), ["abre"], ["nr"])
        dve(lambda e: e.tensor_tensor(q1, nr, lr, ALU.mult), ["nr", "lr"], ["q1"])
        dve(lambda e: e.tensor_tensor(q2, abim, li, ALU.mult), ["abim", "li"], ["q2"])
        dve(lambda e: e.tensor_tensor(q1, q1, q2, ALU.add), ["q1", "q2"], ["q1"])
        dve(lambda e: e.tensor_tensor(zr, q1, den, ALU.mult), ["q1", "den"], ["zr"])
        dve(lambda e: e.tensor_tensor(q1, abim, lr, ALU.mult), ["abim", "lr"], ["q1"])
        dve(lambda e: e.tensor_tensor(q2, nr, li, ALU.mult), ["nr", "li"], ["q2"])
        dve(lambda e: e.tensor_tensor(q1, q1, q2, ALU.subtract), ["q1", "q2"], ["q1"])
        dve(lambda e: e.tensor_tensor(zi, q1, den, ALU.mult), ["q1", "den"], ["zi"])
        dve(lambda e: e.tensor_scalar(a512, th, 512.0, None, ALU.mult), ["th"], ["a512"])
        sin_of(a512, "a512", Fi, "Fi", 16)
        sin_of(a512, "a512", Fr, "Fr", 16, shift=PI / 2)

        for cc in range(4):
            for tl in range(4):
                ti = cc * 4 + tl
                c_t, s_t, er_t, ei_t = tabs[tl]
                tr = f"tab{tl}"
                dve(lambda e, ti=ti: e.tensor_scalar(ang, iotaj, th[:, ti:ti + 1], None, ALU.mult), ["iotaj", "th"], ["u1"])
                sin_of(ang, "u1", s_t, tr + "s", 512)
                sin_of(ang, "u1", c_t, tr + "c", 512, shift=PI / 2)
                dve(lambda e, ti=ti, c_t=c_t: e.tensor_scalar(u1, c_t, zr[:, ti:ti + 1], None, ALU.mult), [tr + "c", "zr"], ["u1"])
                dve(lambda e, ti=ti, s_t=s_t, er_t=er_t: e.scalar_tensor_tensor(er_t, s_t, zi[:, ti:ti + 1], u1, ALU.mult, ALU.add), [tr + "s", "zi", "u1"], [tr + "er"])
                dve(lambda e, ti=ti, s_t=s_t: e.tensor_scalar(u1, s_t, zr[:, ti:ti + 1], None, ALU.mult), [tr + "s", "zr"], ["u1"])
                dve(lambda e, ti=ti, c_t=c_t, ei_t=ei_t: e.scalar_tensor_tensor(ei_t, c_t, zi[:, ti:ti + 1], u1, ALU.mult, ALU.subtract), [tr + "c", "zi", "u1"], [tr + "ei"])
                dve(lambda e, ti=ti, tl=tl: e.tensor_scalar(rt[tl], ones_f.to_broadcast([128, 512]) if False else iotaj, 0.0, rr[:, ti:ti + 1], ALU.mult, ALU.add), ["iotaj", "rr"], [f"rt{tl}"])
                dve(lambda e, ti=ti: e.memset(init_re[:, ti:ti + 1], 0.0), [], [f"ire{ti}"])
                dve(lambda e, ti=ti: e.memset(init_im[:, ti:ti + 1], 0.0), [], [f"iim{ti}"])
            for c8 in range(8):
                t0 = c8 * 512
                for tl in range(4):
                    ti = cc * 4 + tl
                    c_t, s_t, er_t, ei_t = tabs[tl]
                    tr = f"tab{tl}"
                    P.op("pe", lambda e, ti=ti, t0=t0, cc=cc: e.matmul(psf[2][:, 0:512], Bre[:, ti, :], ST[:, cc, t0:t0 + 512], start=True, stop=True), reads=["Bre"], writes=[PSN(2)])
                    P.op("pe", lambda e, ti=ti, t0=t0, cc=cc: e.matmul(psf[3][:, 0:512], Bim[:, ti, :], ST[:, cc, t0:t0 + 512], start=True, stop=True), reads=["Bim"], writes=[PSN(3)])
                    dve(lambda e, er_t=er_t: e.tensor_tensor(u1, psf[2][:, 0:512], er_t, ALU.mult), [PSN(2), tr + "er"], ["u1"])
                    dve(lambda e, ei_t=ei_t: e.tensor_tensor(u2, psf[3][:, 0:512], ei_t, ALU.mult), [PSN(3), tr + "ei"], ["u2"])
                    dve(lambda e: e.tensor_tensor(wi_re, u1, u2, ALU.subtract), ["u1", "u2"], ["wi_re"])
                    dve(lambda e, er_t=er_t: e.tensor_tensor(u1, psf[3][:, 0:512], er_t, ALU.mult), [PSN(3), tr + "er"], ["u1"])
                    dve(lambda e, ei_t=ei_t: e.tensor_tensor(u2, psf[2][:, 0:512], ei_t, ALU.mult), [PSN(2), tr + "ei"], ["u2"])
                    dve(lambda e: e.tensor_tensor(wi_im, u1, u2, ALU.add), ["u1", "u2"], ["wi_im"])
                    dve(lambda e, ti=ti, tl=tl: e.tensor_tensor_scan(w_re, rt[tl], wi_re, init_re[:, ti:ti + 1], ALU.mult, ALU.add), [f"rt{tl}", "wi_re", f"ire{ti}"], ["w_re"])
                    dve(lambda e, ti=ti, tl=tl: e.tensor_tensor_scan(w_im, rt[tl], wi_im, init_im[:, ti:ti + 1], ALU.mult, ALU.add), [f"rt{tl}", "wi_im", f"iim{ti}"], ["w_im"])
                    if c8 < 7:
                        dve(lambda e, ti=ti: e.tensor_scalar(tq[:, 0:1], w_im[:, 511:512], Fi[:, ti:ti + 1], None, ALU.mult), ["w_im", "Fi"], ["tq"])
                        dve(lambda e, ti=ti: e.scalar_tensor_tensor(init_re[:, ti:ti + 1], w_re[:, 511:512], Fr[:, ti:ti + 1], tq[:, 0:1], ALU.mult, ALU.subtract), ["w_re", "Fr", "tq"], [f"ire{ti}"])
                        dve(lambda e, ti=ti: e.tensor_scalar(tq[:, 0:1], w_re[:, 511:512], Fi[:, ti:ti + 1], None, ALU.mult), ["w_re", "Fi"], ["tq"])
                        dve(lambda e, ti=ti: e.scalar_tensor_tensor(init_im[:, ti:ti + 1], w_im[:, 511:512], Fr[:, ti:ti + 1], tq[:, 0:1], ALU.mult, ALU.add), ["w_im", "Fr", "tq"], [f"iim{ti}"])
                    dve(lambda e, c_t=c_t: e.tensor_tensor(u1, c_t, w_re, ALU.mult), [tr + "c", "w_re"], ["u1"])
                    dve(lambda e, s_t=s_t: e.tensor_tensor(u2, s_t, w_im, ALU.mult), [tr + "s", "w_im"], ["u2"])
                    dve(lambda e, tl=tl: e.tensor_tensor(xre[:, tl, :], u1, u2, ALU.subtract), ["u1", "u2"], [f"xre{tl}"])
                    if c8 == 7:
                        dve(lambda e, ti=ti: e.tensor_tensor(fin_re[:, ti:ti + 1], u1[:, 511:512], u2[:, 511:512], ALU.subtract), ["u1", "u2"], ["fin_re"])
                    P.op("pool", lambda e, s_t=s_t: e.tensor_tensor(p1, s_t, w_re, ALU.mult), reads=[tr + "s", "w_re"], writes=["ki"])
                    P.op("pool", lambda e, c_t=c_t, tl=tl: e.tensor_tensor(ximn[:, tl, :], c_t, w_im, ALU.mult), reads=[tr + "c", "w_im"], writes=[f"ximn{tl}"])
                    P.op("pool", lambda e, tl=tl: e.tensor_tensor(ximn[:, tl, :], ximn[:, tl, :], p1, ALU.add), reads=["ki", f"ximn{tl}"], writes=[f"ximn{tl}"])
                    if c8 == 7:
                        dve(lambda e, ti=ti, s_t=s_t: e.tensor_tensor(tq[:, 0:1], s_t[:, 511:512], w_re[:, 511:512], ALU.mult), [tr + "s", "w_re"], ["tq"])
                        dve(lambda e, ti=ti, c_t=c_t: e.scalar_tensor_tensor(fin_im[:, ti:ti + 1], c_t[:, 511:512], w_im[:, 511:512], tq[:, 0:1], ALU.mult, ALU.add), [tr + "c", "w_im", "tq"], ["fin_im"])
                for tl in range(4):
                    ti = cc * 4 + tl
                    P.op("pe", lambda e, ti=ti, tl=tl: e.matmul(psf[4][:, 0:512], Cre[:, ti, :], xre[:, tl, :], start=(tl == 0), stop=False), reads=["Cre", f"xre{tl}"], writes=[PSN(4)])
                    P.op("pe", lambda e, ti=ti, tl=tl: e.matmul(psf[4][:, 0:512], Cim[:, ti, :], ximn[:, tl, :], start=False, stop=(tl == 3)), reads=["Cim", f"ximn{tl}"], writes=[PSN(4)])
                dve(lambda e, cc=cc, t0=t0: e.scalar_tensor_tensor(ysb, ST[:, cc, t0:t0 + 512], dsk[:, cc:cc + 1], psf[4][:, 0:512], ALU.mult, ALU.add), ["dsk", PSN(4)], ["wi_re"])
                yv = ysb.rearrange("p (a b n) -> p a b n", a=2, b=2)
                dve(lambda e, yv=yv: e.tensor_scalar(u1[:, 0:256].rearrange("p (a n) -> p a n", a=2), yv[:, :, 0, :], p0p1[:, 0:1], None, ALU.mult), ["wi_re"], ["u1"])
                dve(lambda e, yv=yv: e.scalar_tensor_tensor(yown, yv[:, :, 1, :], p0p1[:, 1:2], u1[:, 0:256].rearrange("p (a n) -> p a n", a=2), ALU.mult, ALU.add), ["wi_re", "u1"], ["wi_im"])
                for jj in range(2):
                    P.dma("sp", lambda e, jj=jj, c8=c8, cc=cc: e.dma_start(out=YT[2 * c8 + jj, :, cc * 128:(cc + 1) * 128], in_=yown[:, jj, :]), reads=["wi_im"], writes=[uid("YT")])
            for tl in range(4):
                ti = cc * 4 + tl
                P.op("pe", lambda e, ti=ti, cc=cc: e.matmul(psf[2][:, 0:128], Bre[:, ti, :], ST[:, cc, SEQ:SEQ + 128], start=True, stop=True), reads=["Bre"], writes=[PSN(2)])
                P.op("pe", lambda e, ti=ti, cc=cc: e.matmul(psf[3][:, 0:128], Bim[:, ti, :], ST[:, cc, SEQ:SEQ + 128], start=True, stop=True), reads=["Bim"], writes=[PSN(3)])
                dve(lambda e, ti=ti: e.tensor_scalar(u1[:, 0:128], psf[3][:, 0:128], zi[:, ti:ti + 1], None, ALU.mult), [PSN(3), "zi"], ["u1"])
                dve(lambda e, ti=ti: e.scalar_tensor_tensor(bzr, psf[2][:, 0:128], zr[:, ti:ti + 1], u1[:, 0:128], ALU.mult, ALU.subtract), [PSN(2), "zr", "u1"], ["bzr"])
                dve(lambda e, ti=ti: e.tensor_scalar(u1[:, 0:128], psf[2][:, 0:128], zi[:, ti:ti + 1], None, ALU.mult), [PSN(2), "zi"], ["u1"])
                dve(lambda e, ti=ti: e.scalar_tensor_tensor(bzi, psf[3][:, 0:128], zr[:, ti:ti + 1], u1[:, 0:128], ALU.mult, ALU.add), [PSN(3), "zr", "u1"], ["bzi"])
                bzr_v = bzr.rearrange("p (n i) -> p n i", i=8); bzi_v = bzi.rearrange("p (n i) -> p n i", i=8)
                xsr_v = xsr.rearrange("p (n i) -> p n i", i=8); xsi_v = xsi.rearrange("p (n i) -> p n i", i=8)
                for i in range(8):
                    pre = sre0[:, ti, :] if i == 0 else xsr_v[:, :, i - 1]
                    pim = sim0[:, ti, :] if i == 0 else xsi_v[:, :, i - 1]
                    rd = ["sre0", "sim0", "w_re", "w_im", "bzr", "bzi", "abre", "abim", "nabim"]
                    dve(lambda e, ti=ti, i=i, pim=pim: e.scalar_tensor_tensor(tq, pim, nabim[:, ti:ti + 1], bzr_v[:, :, i], ALU.mult, ALU.add), rd, ["tq"])
                    dve(lambda e, ti=ti, i=i, pre=pre: e.scalar_tensor_tensor(tq2, pre, abim[:, ti:ti + 1], bzi_v[:, :, i], ALU.mult, ALU.add), rd, ["tq2"])
                    dve(lambda e, ti=ti, i=i, pre=pre: e.scalar_tensor_tensor(xsr_v[:, :, i], pre, abre[:, ti:ti + 1], tq, ALU.mult, ALU.add), rd + ["tq"], ["w_re"])
                    dve(lambda e, ti=ti, i=i, pim=pim: e.scalar_tensor_tensor(xsi_v[:, :, i], pim, abre[:, ti:ti + 1], tq2, ALU.mult, ALU.add), rd + ["tq2"], ["w_im"])
                dve(lambda e, ti=ti: e.tensor_copy(sso_re[:, ti, :], xsr_v[:, :, 7]), ["w_re"], ["sso_re"])
                dve(lambda e, ti=ti: e.tensor_copy(sso_im[:, ti, :], xsi_v[:, :, 7]), ["w_im"], ["sso_im"])
                dve(lambda e, tl=tl: e.tensor_copy(xre[:, tl, 0:128], xsr), ["w_re"], [f"xre{tl}"])
                dve(lambda e, tl=tl: e.tensor_copy(ximn[:, tl, 0:128], xsi), ["w_im"], [f"ximn{tl}"])
            for tl in range(4):
                ti = cc * 4 + tl
                P.op("pe", lambda e, ti=ti, tl=tl: e.matmul(psf[4][:, 0:128], Cre[:, ti, :], xre[:, tl, 0:128], start=(tl == 0), stop=False), reads=["Cre", f"xre{tl}"], writes=[PSN(4)])
                P.op("pe", lambda e, ti=ti, tl=tl: e.matmul(psf[4][:, 0:128], Cim[:, ti, :], ximn[:, tl, 0:128], start=False, stop=(tl == 3)), reads=["Cim", f"ximn{tl}"], writes=[PSN(4)])
            dve(lambda e, cc=cc: e.scalar_tensor_tensor(ysb[:, 0:128], ST[:, cc, SEQ:SEQ + 128], dsk[:, cc:cc + 1], psf[4][:, 0:128], ALU.mult, ALU.add), ["dsk", PSN(4)], ["wi_re"])
            P.dma("sp", lambda e, cc=cc: e.dma_start(out=YT[16, :, cc * 128:(cc + 1) * 128], in_=ysb[:, 0:128]), reads=["wi_re"], writes=[uid("YT")])
        P.dma("sp", lambda e: e.dma_start(out=spre_o, in_=fin_re), reads=["fin_re"], writes=[uid("o")])
        P.dma("sp", lambda e: e.dma_start(out=spim_o, in_=fin_im), reads=["fin_im"], writes=[uid("o")])
        P.dma("sp", lambda e: e.dma_start(out=ssre_o, in_=sso_re), reads=["sso_re"], writes=[uid("o")])
        P.dma("sp", lambda e: e.dma_start(out=ssim_o, in_=sso_im), reads=["sso_im"], writes=[uid("o")])
        P.barrier()
        A.release()
        A.release()

        A.n = ARENA_BYTES - OT_BYTES
        A.mark()
        amask = A.bf16(8, 512)
        for m in range(8):
            P.dma("pool", lambda e, m=m: e.dma_start(out=amask[:, m, :], in_=amask_d[m]), writes=["amask"])
        HC = []
        for hi in range(2):
            HC.append(dict(E=[A.f32(512) for _ in range(3)], SP=[A.bf16(512), A.bf16(512)], SPf=A.f32(512), R=A.bf16(512),
                           T=[A.f32(512), A.f32(512)], W=[A.bf16(512), A.bf16(512)], Wf=A.f32(512), psS=hi, psC=[2 + 2 * hi, 3 + 2 * hi], hi=hi))

        def stA(hc, g, h, kb, k):
            hi = hc["hi"]
            c, hb = h // 2, (h % 2) * 64
            ps_s = hc["psS"]
            E = hc["E"][k % 3]; SP = hc["SP"][k % 2]; SPf = hc["SPf"]
            en = f"E{hi}{k % 3}"; spn = f"SP{hi}{k % 2}"
            P.op("pe", lambda e: e.matmul(psf[ps_s][:, 0:512], KT[hb:hb + 64, c, kb * 128:(kb + 1) * 128], QT[hb:hb + 64, c, g * 512:(g + 1) * 512], start=True, stop=True),
                 reads=[], writes=[PSN(ps_s)])
            P.op("act", lambda e: e.activation(E, psf[ps_s][:, 0:512], AF.Exp, scale=0.125, bias=sbb[:, h:h + 1]), reads=[PSN(ps_s)], writes=[en])
            if kb >= 8 * g:
                m = kb - 8 * g
                P.op("act", lambda e: e.activation(SPf, E, AF.Ln, bias=1.0), reads=[en], writes=[f"SPf{hi}"])
                P.op("dve", lambda e: e.tensor_tensor(SP, SPf, amask[:, m, :], ALU.mult), reads=[f"SPf{hi}", "amask"], writes=[spn])
            else:
                P.op("act", lambda e: e.activation(SP, E, AF.Ln, bias=1.0), reads=[en], writes=[spn])

        def stB(hc, k, first, last):
            hi = hc["hi"]
            ps_c = hc["psC"][k % 2]
            SP = hc["SP"][k % 2]; R = hc["R"]; spn = f"SP{hi}{k % 2}"
            P.op("pe", lambda e: e.matmul(psf[ps_c][:, 0:512], ltri_b, SP, start=True, stop=first), reads=[spn], writes=[PSN(ps_c)])
            if not first:
                P.op("pe", lambda e: e.matmul(psf[ps_c][:, 0:512], ones_b, R, start=False, stop=True), reads=[f"R{hi}"], writes=[PSN(ps_c)])
            if not last:
                if first:
                    P.op("pool", lambda e: e.tensor_copy(R, SP), reads=[spn], writes=[f"R{hi}"])
                else:
                    P.op("pool", lambda e: e.tensor_tensor(R, R, SP, ALU.add), reads=[spn, f"R{hi}"], writes=[f"R{hi}"])

        def stC(hc, g, h, kb, k):
            hi = hc["hi"]
            ps_c = hc["psC"][k % 2]
            E = hc["E"][k % 3]; T = hc["T"][k % 2]; W = hc["W"][k % 2]; Wf = hc["Wf"]
            en = f"E{hi}{k % 3}"; tn = f"T{hi}{k % 2}"; wn = f"W{hi}{k % 2}"
            P.op("act", lambda e: e.activation(T, psf[ps_c][:, 0:512], AF.Exp, scale=-1.0), reads=[PSN(ps_c)], writes=[tn])
            if kb >= 8 * g:
                m = kb - 8 * g
                P.op("dve", lambda e: e.tensor_tensor(Wf, E, T, ALU.mult), reads=[en, tn], writes=[f"Wf{hi}"])
                P.op("dve", lambda e: e.tensor_tensor(W, Wf, amask[:, m, :], ALU.mult), reads=[f"Wf{hi}", "amask"], writes=[wn])
            else:
                P.op("dve", lambda e: e.tensor_tensor(W, E, T, ALU.mult), reads=[en, tn], writes=[wn])

        def stD(hc, h, kb, k, first, last):
            hi = hc["hi"]
            hb = (h % 2) * 64
            W = hc["W"][k % 2]; wn = f"W{hi}{k % 2}"
            P.op("pe", lambda e: e.matmul(psf[6][hb:hb + 64, 0:512], VB[:, kb, h * 64:(h + 1) * 64], W, start=first, stop=last),
                 reads=[wn], writes=[f"psO{hi}"])

        for g in range(4):
            nkb = 8 * (g + 1)
            kbs = list(range(nkb - 1, -1, -1))
            n = len(kbs)
            for hp in range(4):
                hs = [2 * hp, 2 * hp + 1]
                for i in range(n + 2):
                    if 0 <= i - 2 < n:
                        for hi in range(2):
                            stC(HC[hi], g, hs[hi], kbs[i - 2], i - 2)
                    if i < n:
                        for hi in range(2):
                            stA(HC[hi], g, hs[hi], kbs[i], i)
                    if 0 <= i - 1 < n:
                        for hi in range(2):
                            stB(HC[hi], i - 1, i - 1 == 0, i - 1 == n - 1)
                    if 0 <= i - 2 < n:
                        for hi in range(2):
                            stD(HC[hi], hs[hi], kbs[i - 2], i - 2, i - 2 == 0, i - 2 == n - 1)
                for hi in range(2):
                    h = hs[hi]; c, hb = h // 2, (h % 2) * 64
                    P.op("act", lambda e, c=c, hb=hb, g=g: e.copy(OT[hb:hb + 64, c, g * 512:(g + 1) * 512], psf[6][hb:hb + 64, 0:512]), reads=[f"psO{hi}"], writes=[uid("OT")])
        P.barrier()
        A.release()
        A.release()

        A.mark()
        sbrow = A.f32(1); smask = A.f32(16, 128)
        pti = A.i32(256); ptf = A.f32(256); idx = A.i32(256)
        Qbd = A.bf16(4, 16, 16)
        NR = 4
        Kf = A.f32(16, 512); Vf = A.f32(16, 512); Kb = A.bf16(NR, 512); Vb = A.bf16(16, 512)
        KTs = A.bf16(4, 2176)
        NK = 2176
        Es = A.f32(NK); SPs = A.f32(NK); Fs = A.f32(NK); zer = A.f32(NK); Wsb = A.bf16(NK)
        WT = A.bf16(17, 128)
        ntot = A.f32(1); VBnb = A.bf16(512)
        P.dma("sp", lambda e: e.dma_start(out=sbrow, in_=di["sb_rows"]), writes=["sbrow"])
        P.dma("sp", lambda e: e.dma_start(out=smask, in_=smask_d), writes=["smask"])
        P.dma("sp", lambda e: e.dma_start(out=pti, in_=ptab.partition_broadcast(128)), writes=["pti"])
        dve(lambda e: e.tensor_copy(ptf, pti), ["pti"], ["ptf"])
        dve(lambda e: e.tensor_scalar(ptf, ptf, 128.0, iota_p[:, 0:1], ALU.mult, ALU.add), ["ptf"], ["ptf"])
        dve(lambda e: e.tensor_copy(idx, ptf), ["ptf"], ["idx"])
        dve(lambda e: e.memset(zer, 0.0), [], ["zer"])
        dve(lambda e: e.memset(Qbd, 0.0), [], ["Qbd"])
        dve(lambda e: e.tensor_copy(VBnb, VBn), [], ["VBnb"])
        for c in range(4):
            for hh in range(2):
                dve(lambda e, c=c, hh=hh: e.tensor_copy(Qbd[hh * 64:(hh + 1) * 64, c, :, hh * 8:(hh + 1) * 8],
                                                        QTs[hh * 64:(hh + 1) * 64, c, :].rearrange("p (n i) -> p n i", i=8)), ["Qbd"], ["Qbd"])
        dve(lambda e: e.tensor_copy(KTs[:, :, 2048:2176], KTn), [], ["KTnew"])
        gcount = {"k": 0, "v": 0}

        def k_gather(n):
            for pg in range(16):
                col = n * 16 + pg
                sl = pg
                P.dma("pool", lambda e, sl=sl, col=col: e.indirect_dma_start(out=Kf[:, sl, :], out_offset=None, in_=cache_k, in_offset=bass.IndirectOffsetOnAxis(ap=idx[:, col:col + 1], axis=0)),
                      reads=["idx"], writes=[f"Kf{sl}"])

        def v_gather(n):
            for pg in range(16):
                col = n * 16 + pg
                sl = pg
                P.dma("pool", lambda e, sl=sl, col=col: e.indirect_dma_start(out=Vf[:, sl, :], out_offset=None, in_=cache_v, in_offset=bass.IndirectOffsetOnAxis(ap=idx[:, col:col + 1], axis=0)),
                      reads=["idx"], writes=[f"Vf{sl}"])

        def k_process(n):
            for pg in range(16):
                sl = pg % NR
                dve(lambda e, sl=sl, pg=pg: e.tensor_copy(Kb[:, sl, :], Kf[:, pg, :]), [f"Kf{pg}"], [f"Kb{sl}"])
                for c in range(4):
                    P.op("pe", lambda e, sl=sl, c=c: e.transpose(psb[7][:, c * 128:(c + 1) * 128], Kb[:, sl, c * 128:(c + 1) * 128], ident_b), reads=[f"Kb{sl}"], writes=[PSN(7)])
                P.op("act", lambda e, pg=pg: e.copy(KTs[:, :, pg * 128:(pg + 1) * 128], psb[7][:, 0:512].rearrange("p (c n) -> p c n", c=4)), reads=[PSN(7)], writes=[f"KTs{pg}"])

        def v_process(n):
            for pg in range(16):
                P.op("act", lambda e, pg=pg: e.copy(Vb[:, pg, :], Vf[:, pg, :]), reads=[f"Vf{pg}"], writes=[f"Vb{pg}"])

        ktr = [f"KTs{pg}" for pg in range(16)] + ["KTnew"]
        k_gather(0); v_gather(0)
        k_process(0); v_process(0)
        for n in range(16):
            for cb in range(5):
                w = 512 if cb < 4 else 128
                for c in range(4):
                    P.op("pe", lambda e, c=c, cb=cb, w=w, n=n: e.matmul(psf[cb][32 * c:32 * c + 16, 0:w], Qbd[:, c, n, :], KTs[:, c, cb * 512:cb * 512 + w], start=True, stop=True, tile_position=(0, 32 * c)),
                         reads=ktr + ["Qbd"], writes=[PSN(cb)])
                P.op("act", lambda e, cb=cb, w=w: e.activation(Es[:, cb * 512:cb * 512 + w], psf[cb][:, 0:w], AF.Exp, scale=0.125, bias=sbrow), reads=[PSN(cb), "sbrow"], writes=["Es"])
            if n + 1 < 16:
                k_gather(n + 1); v_gather(n + 1)
            P.op("act", lambda e: e.activation(SPs, Es, AF.Ln, bias=1.0), reads=["Es"], writes=["SPs"])
            dve(lambda e, n=n: e.tensor_tensor(SPs[:, 2048:2176], SPs[:, 2048:2176], smask[:, n, :], ALU.mult), ["SPs", "smask"], ["SPs"])
            dve(lambda e: e.tensor_tensor_scan(Fs, zer, SPs, 0.0, ALU.add, ALU.add), ["zer", "SPs"], ["Fs"])
            dve(lambda e: e.tensor_scalar(ntot, Fs[:, NK - 1:NK], -1.0, None, ALU.mult), ["Fs"], ["ntot"])
            dve(lambda e: e.tensor_tensor(Fs, Fs, SPs, ALU.subtract), ["Fs", "SPs"], ["Fs"])
            P.op("act", lambda e: e.activation(SPs, Fs, AF.Exp, bias=ntot), reads=["Fs", "ntot"], writes=["SPs"])
            dve(lambda e: e.tensor_tensor(Wsb[:, 0:2048], Es[:, 0:2048], SPs[:, 0:2048], ALU.mult), ["Es", "SPs"], ["Wsb"])
            dve(lambda e: e.tensor_tensor(Fs[:, 0:128], Es[:, 2048:2176], SPs[:, 2048:2176], ALU.mult), ["Es", "SPs"], ["Fs"])
            dve(lambda e, n=n: e.tensor_tensor(Wsb[:, 2048:2176], Fs[:, 0:128], smask[:, n, :], ALU.mult), ["Fs", "smask"], ["Wsb"])
            if n + 1 < 16:
                k_process(n + 1)
            for blk in range(17):
                P.op("pe", lambda e, blk=blk: e.transpose(psb[5][:, (blk % 8) * 128:(blk % 8 + 1) * 128], Wsb[:, blk * 128:(blk + 1) * 128], ident_b), reads=["Wsb"], writes=[PSN(5)])
                if blk % 8 == 7 or blk == 16:
                    b0 = blk - (blk % 8); nb = blk % 8 + 1
                    P.op("act", lambda e, b0=b0, nb=nb: e.copy(WT[:, b0:b0 + nb, :], psb[5][:, 0:nb * 128].rearrange("p (b n) -> p b n", b=nb)), reads=[PSN(5)], writes=["WT"])
            for c in range(4):
                for blk in range(17):
                    vsrc = Vb[:, blk, c * 128:(c + 1) * 128] if blk < 16 else VBnb[:, c * 128:(c + 1) * 128]
                    P.op("pe", lambda e, c=c, blk=blk, vsrc=vsrc: e.matmul(psf[6][:, c * 16:(c + 1) * 16], vsrc, WT[:, blk, 32 * c:32 * c + 16], start=(blk == 0), stop=(blk == 16)),
                         reads=["WT"] + ([f"Vb{blk}"] if blk < 16 else ["VBnb"]), writes=[PSN(6)])
            for hh in range(2):
                P.op("act", lambda e, hh=hh, n=n: e.copy(OT[hh * 64:(hh + 1) * 64, :, NOWN * 128 + 8 * n:NOWN * 128 + 8 * n + 8],
                                                         psf[6][hh * 64:(hh + 1) * 64, 0:64].rearrange("p (c k) -> p c k", c=4)[:, :, hh * 8:(hh + 1) * 8]), reads=[PSN(6)], writes=[uid("OTs")])
            if n + 1 < 16:
                v_process(n + 1)
        P.barrier()
        A.release()

        A.mark()
        Wa = A.bf16(4, D); Wbb = A.bf16(4, D); Wg = A.bf16(4, 512); Wo = A.bf16(8, D)
        load_w_rows(Wa, wba, 4, "Wa"); load_w_rows(Wbb, wbb, 4, "Wbb"); load_w_rows(Wg, glu_w, 4, "Wg"); load_w_rows(Wo, w_o, 8, "Wo")
        glub = A.f32(4); idxo = A.i32(16)
        P.dma("sp", lambda e: e.dma_start(out=glub, in_=glub_d), writes=["glub"])
        P.dma("sp", lambda e: e.dma_start(out=idxo, in_=idxown_d), writes=["idxo"])
        g2r = A.f32(D); sh3 = A.f32(D); A3 = A.f32(D); grow = A.f32(D)
        yt = A.f32(512); y2 = A.f32(512); zg = A.f32(512); sgm = A.f32(512); zgb = A.bf16(4, 128); obT = A.bf16(4, 128)
        sga = A.bf16(8, 128); sgb = A.bf16(8, 128); m1 = A.f32(512); m2 = A.f32(512); mgT = A.bf16(8, 128)
        h1t = A.f32(D); tmp512 = A.f32(512); tmpD = A.f32(D); junkD = A.bf16(D); ss1 = A.f32(1); rstd1 = A.f32(1)
        xs3b = A.bf16(D)
        for g in range(2):
            load_mod(g2r, g, 5, "p5mod"); load_mod(sh3, g, 6, "p5mod"); load_mod(A3, g, 7, "p5mod")
            load_row_bcast(grow, n3, "grow")
            dve(lambda e: e.scalar_tensor_tensor(A3, A3, 1.0, grow, ALU.add, ALU.mult), ["p5mod", "grow"], ["p5mod"])
            for j in (range(NOWN) if g == 0 else [NOWN]):
                P.dma("sp", lambda e, j=j: e.dma_start(out=yt, in_=YT[j]), writes=["yt"])
                dve(lambda e: e.tensor_tensor(y2, yt, yt, ALU.mult), ["yt"], ["y2"])
                dve(lambda e: e.tensor_scalar(y2, y2, 0.044715, 1.0, ALU.mult, ALU.add), ["y2"], ["y2"])
                dve(lambda e: e.tensor_tensor(y2, y2, yt, ALU.mult), ["y2", "yt"], ["y2"])
                P.op("act", lambda e: e.activation(sgm, y2, AF.Sigmoid, scale=1.5957691216057308), reads=["y2"], writes=["sgm"])
                dve(lambda e: e.tensor_tensor(zg, yt, sgm, ALU.mult), ["yt", "sgm"], ["zg"])
                dve(lambda e: e.tensor_copy(zgb, zg.rearrange("p (c n) -> p c n", c=4)), ["zg"], ["zgb"])
                for c in range(4):
                    for k in range(4):
                        P.op("pe", lambda e, c=c, k=k: e.matmul(psf[1][:, c * 128:(c + 1) * 128], Wg[:, k, c * 128:(c + 1) * 128], zgb[:, k, :], start=(k == 0), stop=(k == 3)), reads=["Wg", "zgb"], writes=[PSN(1)])
                for c in range(4):
                    P.op("act", lambda e, c=c: e.activation(sgm[:, c * 128:(c + 1) * 128], psf[1][:, c * 128:(c + 1) * 128], AF.Sigmoid, bias=glub[:, c:c + 1]), reads=[PSN(1), "glub", "zg"], writes=["sgm"])
                dve(lambda e: e.tensor_tensor(obT, zg.rearrange("p (c n) -> p c n", c=4), sgm.rearrange("p (c n) -> p c n", c=4), ALU.mult), ["zg", "sgm"], ["obT"])
                P.dma("sp", lambda e, j=j: e.dma_start(out=sga, in_=SGA[j].rearrange("p (c n) -> p c n", c=8)), writes=["sga"])
                P.dma("sp", lambda e, j=j: e.dma_start(out=sgb, in_=SGB[j].rearrange("p (c n) -> p c n", c=8)), writes=["sgb"])
                for hh in range(2):
                    for c4 in range(4):
                        dc = hh * 4 + c4
                        for k in range(4):
                            P.op("pe", lambda e, hh=hh, c4=c4, dc=dc, k=k, j=j: e.matmul(psf[2 + hh][:, c4 * 128:(c4 + 1) * 128], Wa[:, k, dc * 128:(dc + 1) * 128], OT[:, k, j * 128:(j + 1) * 128], start=(k == 0), stop=(k == 3)),
                                 reads=["Wa"], writes=[PSN(2 + hh)])
                        for k in range(4):
                            P.op("pe", lambda e, hh=hh, c4=c4, dc=dc, k=k: e.matmul(psf[4 + hh][:, c4 * 128:(c4 + 1) * 128], Wbb[:, k, dc * 128:(dc + 1) * 128], obT[:, k, :], start=(k == 0), stop=(k == 3)),
                                 reads=["Wbb", "obT"], writes=[PSN(4 + hh)])
                    dve(lambda e, hh=hh: e.tensor_tensor(m1, psf[2 + hh][:, 0:512], sga[:, hh * 4:(hh + 1) * 4, :].rearrange("p c n -> p (c n)"), ALU.mult), [PSN(2 + hh), "sga"], ["m1"])
                    dve(lambda e, hh=hh: e.tensor_tensor(m2, psf[4 + hh][:, 0:512], sgb[:, hh * 4:(hh + 1) * 4, :].rearrange("p c n -> p (c n)"), ALU.mult), [PSN(4 + hh), "sgb"], ["m2"])
                    dve(lambda e, hh=hh: e.tensor_tensor(mgT[:, hh * 4:(hh + 1) * 4, :], m1.rearrange("p (c n) -> p c n", c=4), m2.rearrange("p (c n) -> p c n", c=4), ALU.add), ["m1", "m2"], ["mgT"])
                if j < NOWN:
                    P.dma("pool", lambda e, j=j: e.indirect_dma_start(out=h1t, out_offset=None, in_=H1, in_offset=bass.IndirectOffsetOnAxis(ap=idxo[:, j:j + 1], axis=0)), reads=["idxo"], writes=["h1t"])
                else:
                    P.dma("sp", lambda e: e.dma_start(out=h1t, in_=H1[NBLK * 128:(NBLK + 1) * 128, :]), writes=["h1t"])
                for half in range(2):
                    for k in range(8):
                        P.op("pe", lambda e, half=half, k=k: e.matmul(psf[6 + half][:, 0:512], mgT[:, k, :], Wo[:, k, half * 512:(half + 1) * 512], start=(k == 0), stop=(k == 7)), reads=["mgT", "Wo"], writes=[PSN(6 + half)])
                    hs = h1t[:, half * 512:(half + 1) * 512]
                    dve(lambda e, half=half: e.tensor_tensor(tmp512, psf[6 + half][:, 0:512], g2r[:, half * 512:(half + 1) * 512], ALU.mult), [PSN(6 + half), "p5mod"], ["tmp512"])
                    dve(lambda e, hs=hs: e.tensor_tensor(hs, hs, tmp512, ALU.add), ["tmp512", "h1t"], ["h1t"])
                P.dma("sp", lambda e, j=j: e.dma_start(out=H2[j * 128:(j + 1) * 128, :], in_=h1t), reads=["h1t"], writes=[uid("H2")])
                norm_mod(h1t, "h1t", A3, sh3, xs3b, "xs3b", "p5")
                P.dma("sp", lambda e, j=j: e.dma_start(out=XS3[j * 128:(j + 1) * 128, :], in_=xs3b), reads=["xs3b"], writes=[uid("XS3")])
        P.barrier()
        A.release()
        A.release()

        wstage["on"] = True
        A.n = ARENA_BYTES - WST_BYTES
        A.mark()
        W1 = A.bf16(8, 2 * DFF); W2 = A.bf16(NF, D)
        load_w_cols(W1, f2_in, 0, 2 * DFF, "f2W1", order=[x for f_ in range(NF) for x in (f_ * 128, DFF + f_ * 128)], fine=True)
        load_w_rows(W2, f2_out, NF, "f2W2", fine=True)
        g3h = A.f32(D); fnr = A.f32(D)
        ht = [A.f32(D), A.f32(D)]; xin = [A.bf16(D), A.bf16(D)]
        uT = [A.bf16(8, 128), A.bf16(8, 128)]; actT = A.bf16(NF, 128)
        sgs = [A.f32(128), A.f32(128)]
        tmp512 = A.f32(512); junkD = A.bf16(D); ss1 = A.f32(1); rstd1 = A.f32(1)
        yo = [A.f32(D), A.f32(D)]
        load_row_bcast(fnr, nfin, "fnr")
        for g in range(2):
            load_mod(g3h, g, 8, "f2mod")
            dve(lambda e: e.tensor_scalar(g3h, g3h, 0.5, None, ALU.mult), ["f2mod"], ["f2mod"])
            blocks = list(range(NOWN)) if g == 0 else [NOWN]

            def prep5(j, s):
                P.dma("sp", lambda e: e.dma_start(out=ht[s], in_=H2[j * 128:(j + 1) * 128, :]), writes=[f"ht{s}"])
                P.dma("sp", lambda e: e.dma_start(out=xin[s], in_=XS3[j * 128:(j + 1) * 128, :]), writes=[f"xin{s}"])

            prep5(blocks[0], 0)
            transpose_block(xin[0], "xin0", uT[0], "uT0", 0)
            for i, j in enumerate(blocks):
                s = i % 2
                ffn_mm1(uT[s], f"uT{s}", W1, actT, "f2")
                if i + 1 < len(blocks):
                    prep5(blocks[i + 1], 1 - s)
                ffn_mm2(actT, W2, "f2")
                if i + 1 < len(blocks):
                    transpose_block(xin[1 - s], f"xin{1 - s}", uT[1 - s], f"uT{1 - s}", 0)
                ffn_update(f"ht{s}", ht[s], g3h, "f2")
                rstd_from(ht[s], f"ht{s}", junkD, ss1, rstd1, "fin")
                dve(lambda e, s=s: e.scalar_tensor_tensor(yo[s], ht[s], rstd1, fnr, ALU.mult, ALU.mult), [f"ht{s}", "finrstd", "fnr"], [f"yo{s}"])
                dst = y_p[j * 128:(j + 1) * 128, :] if j < NOWN else y_s
                P.dma("sp", lambda e, s=s, dst=dst: e.dma_start(out=dst, in_=yo[s]), reads=[f"yo{s}"], writes=[uid("y")])
        P.barrier()
        P.emit()
    return nc


_CACHE = {}


def _consts(par):
    c = {}
    c["ident"] = np.eye(128, dtype=np.float32)
    c["iota_p"] = np.arange(128, dtype=np.float32).reshape(128, 1)
    c["iota_j"] = np.ascontiguousarray(np.broadcast_to(np.arange(512, dtype=np.float32)[None, :], (128, 512)))
    jj, ss = np.meshgrid(np.arange(128), np.arange(128), indexing="ij")
    c["ltri"] = (jj >= ss).astype(np.float32)
    am = np.zeros((8, 128, 512), np.float32)
    p = np.arange(128)[:, None]
    t = np.arange(128)[None, :]
    for m in range(8):
        for q in range(4):
            am[m, :, q * 128:(q + 1) * 128] = ((m * 128 + p) < ((2 * q + par) * 128 + t)).astype(np.float32)
    c["amask"] = am
    sm = np.zeros((128, 16, 128), np.float32)
    for cc in range(4):
        for hh in range(2):
            for i in range(8):
                row = 32 * cc + 8 * hh + i
                for n in range(16):
                    sm[row, n, 8 * n:8 * n + i] = 1.0
    c["smask"] = sm
    c["idx_own"] = ((2 * np.arange(16)[None, :] + par) * 128 + np.arange(128)[:, None]).astype(np.int32)
    c["p0p1"] = np.ascontiguousarray(np.broadcast_to(np.array([[1.0 - par, float(par)]], np.float32), (128, 2)))
    return c


def kernel(**inp):
    f = lambda k: np.asarray(inp[k])
    if "nc" not in _CACHE:
        _CACHE["nc"] = build_program()
    nc = _CACHE["nc"]
    x_prompt, x_sample = f("x_prompt"), f("x_sample")
    c_prompt, c_sample = f("c_prompt"), f("c_sample")
    ck = np.ascontiguousarray(f("cache_k")[0].reshape(-1, 512)); cv = np.ascontiguousarray(f("cache_v")[0].reshape(-1, 512))
    page_table = f("page_table").astype(np.int32)
    s_re, s_im = f("state_ssm_re")[0], f("state_ssm_im")[0]

    def st_layout(a):
        return np.ascontiguousarray(a.reshape(16, 16, 2, 64).transpose(2, 3, 1, 0).reshape(128, 16, 16))

    def gp_layout(a):
        return np.ascontiguousarray(a.reshape(16, 2, 64).transpose(1, 2, 0).reshape(128, 16))

    b_re, b_im, c_re, c_im = f("ssm_b_re")[0], f("ssm_b_im")[0], f("ssm_c_re")[0], f("ssm_c_im")[0]

    def bfull(b):
        o = np.zeros((16, 128, 128), np.float32)
        for ti in range(16):
            for g2 in range(2):
                r0 = 32 * (ti % 4) + 16 * g2
                o[ti, r0:r0 + 16, 64 * g2:64 * g2 + 64] = b[2 * ti + g2].T
        return o

    def cfull(cm):
        o = np.zeros((16, 128, 128), np.float32)
        for ti in range(16):
            for g2 in range(2):
                c0 = 32 * (ti % 4) + 16 * g2
                o[ti, 64 * g2:64 * g2 + 64, c0:c0 + 16] = cm[2 * ti + g2].T
        return o

    ld = f("ssm_log_dt")[0]
    shared = {
        "cache_k": ck, "cache_v": cv,
        "ada_w": f("ada_w")[0], "ada_b": f("ada_b")[0].reshape(1, -1),
        "norm_ffn1": f("norm_ffn1")[0].reshape(1, -1), "norm_mix": f("norm_mix")[0].reshape(1, -1),
        "norm_ffn2": f("norm_ffn2")[0].reshape(1, -1), "final_norm": f("final_norm").reshape(1, -1),
        "ffn1_w_in": f("ffn1_w_in")[0], "ffn1_w_out": f("ffn1_w_out")[0],
        "ffn2_w_in": f("ffn2_w_in")[0], "ffn2_w_out": f("ffn2_w_out")[0],
        "w_in": f("w_in")[0], "sb_bias": f("sb_bias")[0].reshape(1, 8),
        "lam_re": gp_layout(f("ssm_lambda_re")[0]), "lam_im": gp_layout(f("ssm_lambda_im")[0]),
        "logdt": np.ascontiguousarray(np.broadcast_to(ld.reshape(16, 2).T[:, None, :], (2, 64, 16)).reshape(128, 16)),
        "bfull_re": bfull(b_re), "bfull_im": bfull(b_im), "cfull_re": cfull(c_re), "cfull_im": cfull(c_im),
        "ssm_d": np.ascontiguousarray(f("ssm_d")[0].reshape(4, 128).T), "glu_w": f("glu_w")[0],
        "glu_b": np.ascontiguousarray(f("glu_b")[0].reshape(4, 128).T),
        "w_branch_a": f("w_branch_a")[0], "w_branch_b": f("w_branch_b")[0], "w_out": f("w_out")[0],
    }
    sbr = np.zeros((128, 1), np.float32)
    for cc in range(4):
        for hh in range(2):
            sbr[32 * cc + 8 * hh:32 * cc + 8 * hh + 8, 0] = f("sb_bias")[0, 2 * cc + hh]
    shared["sb_rows"] = sbr
    shared = {k: np.ascontiguousarray(v, dtype=np.float32) for k, v in shared.items()}
    cst = [_consts(0), _consts(1)]
    in_maps = []
    for core in range(8):
        seq, par = core // 2, core % 2
        m = dict(shared)
        m.update(cst[par])
        m["xp"] = np.ascontiguousarray(x_prompt[seq])
        m["xs"] = np.ascontiguousarray(x_sample[16 * core:16 * core + 16].reshape(128, D))
        m["cpr"] = np.ascontiguousarray(np.repeat(c_prompt[seq:seq + 1], 128, axis=0))
        m["csr"] = np.ascontiguousarray(np.repeat(c_sample[16 * core:16 * core + 16], 8, axis=0))
        m["ptab"] = np.ascontiguousarray(page_table[16 * core:16 * core + 16].reshape(1, 256))
        m["st_re"] = st_layout(s_re[16 * core:16 * core + 16]); m["st_im"] = st_layout(s_im[16 * core:16 * core + 16])
        in_maps.append(m)
    res = run_bass_kernel_spmd(nc, in_maps, core_ids=list(range(8))).results
    B = 4
    y_prompt = np.zeros((B, SEQ, D), np.float32); y_sample = np.zeros((128, 8, D), np.float32)
    nkp = np.zeros((1, B, SEQ, 8, 64), np.float32); nvp = np.zeros_like(nkp)
    srp = np.zeros((1, B, 32, 64), np.float32); sip = np.zeros_like(srp)
    nks = np.zeros((1, 128, 8, 8, 64), np.float32); nvs = np.zeros_like(nks)
    srs = np.zeros((1, 128, 32, 64), np.float32); sis = np.zeros_like(srs)

    def gp_back(a):
        return a.reshape(2, 64, 16).transpose(2, 0, 1).reshape(32, 64)

    def st_back(a):
        return a.reshape(2, 64, 16, 16).transpose(3, 2, 0, 1).reshape(16, 32, 64)

    for core in range(8):
        seq, par = core // 2, core % 2
        r = res[core]
        yp = r["y_p"].reshape(16, 128, D)
        y_prompt[seq].reshape(16, 2, 128, D)[:, par] = yp
        y_sample[16 * core:16 * core + 16] = r["y_s"].reshape(16, 8, D)
        if par == 0:
            nkp[0, seq] = r["kp"].reshape(SEQ, 8, 64); nvp[0, seq] = r["vp"].reshape(SEQ, 8, 64)
            srp[0, seq] = gp_back(r["sp_re"]); sip[0, seq] = gp_back(r["sp_im"])
        nks[0, 16 * core:16 * core + 16] = r["ks"].reshape(16, 8, 8, 64); nvs[0, 16 * core:16 * core + 16] = r["vs"].reshape(16, 8, 8, 64)
        srs[0, 16 * core:16 * core + 16] = st_back(r["ss_re"]); sis[0, 16 * core:16 * core + 16] = st_back(r["ss_im"])
    return (y_prompt, y_sample, nkp, nvp, srp, sip, nks, nvs, srs, sis)
```

```python
import numpy as np
from contextlib import ExitStack
import concourse.bass as bass
import concourse.mybir as mybir
from concourse.bass_utils import run_bass_kernel_spmd

F32 = mybir.dt.float32
BF16 = mybir.dt.bfloat16
I32 = mybir.dt.int32
AF = mybir.ActivationFunctionType
ALU = mybir.AluOpType

D = 1024
DFF = 2816
NF = 22
SEQ = 4096
NBLK = 32
NOWN = 16
PI = 3.14159265358979
TWO_PI = 6.283185307179586


import types


def _snap(fn):
    if fn.__closure__ is None:
        return fn
    cells = []
    for c in fn.__closure__:
        try:
            cells.append(types.CellType(c.cell_contents))
        except ValueError:
            cells.append(c)
    return types.FunctionType(fn.__code__, fn.__globals__, fn.__name__, fn.__defaults__, tuple(cells))


class Prog:
    CE = ("pe", "act", "dve", "pool")
    QE = ("sp", "act", "pool")

    def __init__(self, nc, sems, n_dma_sems=(28, 4, 24)):
        self.nc = nc
        self.streams = {e: [] for e in ("pe", "act", "dve", "pool", "sp")}
        self.count = {e: 0 for e in self.CE}
        self.res = {}
        self.waited = {e: {} for e in self.streams}
        self.sems = {}
        it = iter(sems)
        for e in self.CE:
            self.sems[e] = next(it)
        self.dma_pool = {}
        for q, n in zip(self.QE, n_dma_sems):
            self.dma_pool[q] = [[next(it), 0] for _ in range(n)]
        self.dma_rr = {q: 0 for q in self.QE}
        for q in self.QE:
            for i, sv in enumerate(self.dma_pool[q]):
                self.sems[("d", q, i)] = sv[0]
        self.final_events = {}

    def _deps(self, eng, reads, writes):
        evs = {}

        def add(k, v):
            if v > evs.get(k, 0):
                evs[k] = v

        for r in reads:
            st = self.res.get(r)
            if st and st["w"]:
                add(*st["w"])
        for w in writes:
            st = self.res.get(w)
            if st:
                if st["w"]:
                    add(*st["w"])
                for k, v in st["r"].items():
                    if k == eng:
                        continue
                    add(k, v)
        if eng == "pe":
            evs.pop("pe", None)
        return evs

    def _commit(self, ev, reads, writes):
        k, v = ev
        for r in reads:
            st = self.res.setdefault(r, {"w": None, "r": {}})
            if v > st["r"].get(k, 0):
                st["r"][k] = v
        for w in writes:
            self.res[w] = {"w": ev, "r": {}}

    def _waits(self, eng, evs):
        out = []
        wd = self.waited[eng]
        for k, v in evs.items():
            if wd.get(k, 0) >= v:
                continue
            wd[k] = v
            out.append((k, v))
        return out

    def op(self, eng, fn, reads=(), writes=()):
        fn = _snap(fn)
        evs = self._deps(eng, reads, writes)
        waits = self._waits(eng, evs)
        self.count[eng] += 1
        ev = (eng, self.count[eng])
        self.streams[eng].append(("op", waits, fn, ev))
        self._commit(ev, reads, writes)
        return ev

    def dma(self, q, fn, reads=(), writes=()):
        fn = _snap(fn)
        evs = self._deps(None, reads, writes)
        pool = self.dma_pool[q]
        i = self.dma_rr[q]
        self.dma_rr[q] = (i + 1) % len(pool)
        key = ("d", q, i)
        prev = pool[i][1]
        if prev > 0:
            evs[key] = max(evs.get(key, 0), prev)
        waits = self._waits(q, evs)
        pool[i][1] = prev + 16
        ev = (key, prev + 16)
        self.streams[q].append(("dma", waits, fn, ev))
        self._commit(ev, reads, writes)
        self.final_events[key] = prev + 16
        return ev

    def barrier(self):
        evs = {e: self.count[e] for e in self.CE if self.count[e] > 0}
        evs.update(self.final_events)
        for e in self.streams:
            w = self._waits(e, {k: v for k, v in evs.items() if k != e or e != "pe"})
            if w:
                self.streams[e].append(("wait", w, None, None))
        self.res = {}

    def emit(self):
        nc = self.nc
        handles = {"pe": "tensor", "act": "scalar", "dve": "vector", "pool": "gpsimd", "sp": "sync"}
        with nc.Block() as block:
            for e, hname in handles.items():
                stream = self.streams[e]

                def body(engine, stream=stream, e=e):
                    for kind, waits, fn, ev in stream:
                        for k, v in waits:
                            engine.wait_ge(self.sems[k], v)
                        if kind == "wait":
                            continue
                        ins = fn(engine)
                        if kind == "op":
                            ins.then_inc(self.sems[e], 1)
                        else:
                            ins.then_inc(self.sems[ev[0]], 16)

                getattr(block, hname)(body)


class Arena:
    def __init__(self, big, nbytes):
        self.f = big
        self.b = big.bitcast(BF16)
        self.i = big.bitcast(I32)
        self.n = nbytes
        self.top = 0
        self.marks = []

    def alloc(self, nbytes, dt=F32, shape=None):
        nbytes = (nbytes + 15) // 16 * 16
        o = self.top
        self.top += nbytes
        assert self.top <= self.n, f"SBUF arena overflow {self.top} > {self.n}"
        if dt == F32:
            v = self.f[:, o // 4:(o + nbytes) // 4]
        elif dt == BF16:
            v = self.b[:, o // 2:(o + nbytes) // 2]
        else:
            v = self.i[:, o // 4:(o + nbytes) // 4]
        return v

    def f32(self, *shape):
        n = int(np.prod(shape))
        v = self.alloc(n * 4, F32)[:, 0:n]
        return _shape(v, shape)

    def bf16(self, *shape):
        n = int(np.prod(shape))
        v = self.alloc(n * 2, BF16)[:, 0:n]
        return _shape(v, shape)

    def i32(self, *shape):
        n = int(np.prod(shape))
        v = self.alloc(n * 4, I32)[:, 0:n]
        return _shape(v, shape)

    def mark(self):
        self.marks.append(self.top)

    def release(self):
        self.top = self.marks.pop()


def _shape(v, shape):
    if len(shape) == 1:
        return v
    if len(shape) == 2:
        return v.rearrange("p (a b) -> p a b", a=shape[0])
    if len(shape) == 3:
        return v.rearrange("p (a b c) -> p a b c", a=shape[0], b=shape[1])
    raise ValueError


_uid = [0]


def uid(s):
    _uid[0] += 1
    return f"{s}#{_uid[0]}"


def build_program():
    nc = bass.Bass("TRN2", target_bir_lowering=False)
    di = {}

    def inp(name, shape, dt=F32):
        di[name] = nc.dram_tensor(name, list(shape), dt, kind="ExternalInput").ap()
        return di[name]

    def outp(name, shape, dt=F32):
        di[name] = nc.dram_tensor(name, list(shape), dt, kind="ExternalOutput").ap()
        return di[name]

    def scratch(name, shape, dt=F32):
        di[name] = nc.dram_tensor(name, list(shape), dt, kind="Internal").ap()
        return di[name]

    xp = inp("xp", [SEQ, D]); xs = inp("xs", [128, D])
    cpr = inp("cpr", [128, D]); csr = inp("csr", [128, D])
    cache_k = inp("cache_k", [2560 * 128, 512]); cache_v = inp("cache_v", [2560 * 128, 512])
    ptab = inp("ptab", [1, 256], I32)
    st_re = inp("st_re", [128, 16, 16]); st_im = inp("st_im", [128, 16, 16])
    ada_w = inp("ada_w", [D, 9 * D]); ada_b = inp("ada_b", [1, 9 * D])
    n1 = inp("norm_ffn1", [1, D]); n2 = inp("norm_mix", [1, D]); n3 = inp("norm_ffn2", [1, D]); nfin = inp("final_norm", [1, D])
    f1_in = inp("ffn1_w_in", [D, 2 * DFF]); f1_out = inp("ffn1_w_out", [DFF, D])
    f2_in = inp("ffn2_w_in", [D, 2 * DFF]); f2_out = inp("ffn2_w_out", [DFF, D])
    w_in = inp("w_in", [D, 4096]); sbb_d = inp("sb_bias", [1, 8])
    lam_re = inp("lam_re", [128, 16]); lam_im = inp("lam_im", [128, 16]); logdt = inp("logdt", [128, 16])
    bre_d = inp("bfull_re", [16, 128, 128]); bim_d = inp("bfull_im", [16, 128, 128])
    cre_d = inp("cfull_re", [16, 128, 128]); cim_d = inp("cfull_im", [16, 128, 128])
    dsk_d = inp("ssm_d", [128, 4]); glu_w = inp("glu_w", [512, 512]); glub_d = inp("glu_b", [128, 4])
    wba = inp("w_branch_a", [512, D]); wbb = inp("w_branch_b", [512, D]); w_o = inp("w_out", [D, D])
    ident_d = inp("ident", [128, 128]); iotap_d = inp("iota_p", [128, 1]); iotaj_d = inp("iota_j", [128, 512])
    ltri_d = inp("ltri", [128, 128]); amask_d = inp("amask", [8, 128, 512]); smask_d = inp("smask", [128, 16, 128])
    idxown_d = inp("idx_own", [128, 16], I32); p0p1_d = inp("p0p1", [128, 2]); inp("sb_rows", [128, 1])

    y_p = outp("y_p", [NOWN * 128, D]); y_s = outp("y_s", [128, D])
    kp_o = outp("kp", [SEQ, 512]); vp_o = outp("vp", [SEQ, 512])
    spre_o = outp("sp_re", [128, 16]); spim_o = outp("sp_im", [128, 16])
    ks_o = outp("ks", [128, 512]); vs_o = outp("vs", [128, 512])
    ssre_o = outp("ss_re", [128, 16, 16]); ssim_o = outp("ss_im", [128, 16, 16])

    MOD = scratch("MOD", [2, 128, 9 * D])
    H1 = scratch("H1", [33 * 128, D]); XS2 = scratch("XS2", [33 * 128, D], BF16)
    SGA = scratch("SGA", [17, 128, 1024], BF16); SGB = scratch("SGB", [17, 128, 1024], BF16)
    YT = scratch("YT", [17, 128, 512])
    H2 = scratch("H2", [17 * 128, D]); XS3 = scratch("XS3", [17 * 128, D], BF16)

    with ExitStack() as es:
        sems = [es.enter_context(nc.semaphore(f"s{i}")) for i in range(64)]
        ARENA_BYTES = 207 * 1024
        big = es.enter_context(nc.sbuf_tensor("big", [128, ARENA_BYTES // 4], F32))
        psf = [es.enter_context(nc.psum_tensor(f"ps{i}", [128, 512], F32)) for i in range(8)]
        psb = [p.bitcast(BF16) for p in psf]
        P = Prog(nc, sems)
        A = Arena(big, ARENA_BYTES)

        def PSN(i):
            return f"ps{i}"

        wstage = {"n": 0, "tiles": None, "on": False}
        ident_b = A.bf16(128); ones_b = A.bf16(128); ltri_b = A.bf16(128)
        ones_f = A.f32(128)
        iota_p = A.f32(1); p0p1 = A.f32(2); sbb = A.f32(8); epsb = A.f32(1)
        P.dma("pool", lambda e: e.dma_start(out=ident_b, in_=ident_d), writes=["ident_b"])
        P.dma("pool", lambda e: e.dma_start(out=ltri_b, in_=ltri_d), writes=["ltri_b"])
        P.dma("sp", lambda e: e.dma_start(out=iota_p, in_=iotap_d), writes=["iota_p"])
        P.dma("sp", lambda e: e.dma_start(out=p0p1, in_=p0p1_d), writes=["p0p1"])
        P.dma("sp", lambda e: e.dma_start(out=sbb, in_=sbb_d.partition_broadcast(128)), writes=["sbb"])
        P.op("dve", lambda e: e.memset(ones_b, 1.0), writes=["ones_b"])
        P.op("dve", lambda e: e.memset(ones_f, 1.0), writes=["ones_f"])
        P.op("dve", lambda e: e.memset(epsb, 1e-6), writes=["epsb"])
        CONST = ["ident_b", "ones_b", "ltri_b", "ones_f", "iota_p", "p0p1", "sbb", "epsb"]
        WST_BYTES = 3 * 4096
        wstage["tiles"] = [A.f[:, (ARENA_BYTES - WST_BYTES) // 4 + 1024 * i:(ARENA_BYTES - WST_BYTES) // 4 + 1024 * (i + 1)] for i in range(3)]

        def after_barrier():
            pass


        def rstd_from(x_ap, x_res, junk, ss, rstd, tag):
            P.op("act", lambda e: e.activation(junk, x_ap, AF.Square, accum_out=ss), reads=[x_res], writes=[tag + "junk", tag + "ss"])
            P.op("act", lambda e: e.activation(rstd, ss, AF.Sqrt, scale=1.0 / D, bias=epsb), reads=[tag + "ss"], writes=[tag + "rs0"])
            P.op("dve", lambda e: e.reciprocal(rstd, rstd), reads=[tag + "rs0"], writes=[tag + "rstd"])

        def transpose_block(src_b, src_res, dstT, dst_res, pbank):
            for k in range(8):
                P.op("pe", lambda e, k=k: e.transpose(psb[pbank][:, k * 128:(k + 1) * 128], src_b[:, k * 128:(k + 1) * 128], ident_b),
                     reads=[src_res, "ident_b"], writes=[PSN(pbank)])
            P.op("act", lambda e: e.copy(dstT, psb[pbank][:, 0:1024].rearrange("p (k n) -> p k n", k=8)), reads=[PSN(pbank)], writes=[dst_res])

        def _stage_cast(dst_view, src_view, res, fine):
            if not wstage["on"]:
                P.dma("pool", lambda e: e.dma_start(out=dst_view, in_=src_view), writes=[res])
                return
            i = wstage["n"]; wstage["n"] += 1
            sl = i % len(wstage["tiles"])
            st = wstage["tiles"][sl]
            shp = dst_view.shape
            stv = st[:, 0:shp[1] * shp[2]].rearrange("p (k n) -> p k n", k=shp[1])
            P.dma("sp", lambda e: e.dma_start(out=stv, in_=src_view), writes=[f"wst{sl}"])
            if fine and i % 2 == 1:
                P.op("act", lambda e: e.copy(dst_view, stv), reads=[f"wst{sl}"], writes=[res])
            else:
                P.op("pool", lambda e: e.tensor_copy(dst_view, stv), reads=[f"wst{sl}"], writes=[res])

        def load_w_cols(dst, src, c0, c1, res, step=512, order=None, fine=False):
            if fine:
                step = 128
            starts = order if order is not None else list(range(c0, c1, step))
            for a in starts:
                b = min(a + step, c1)
                _stage_cast(dst[:, :, a - c0:b - c0], src[:, a:b].rearrange("(k p) n -> p k n", p=128), f"{res}:{a - c0}" if fine else res, fine)

        def load_w_rows(dst, src, nchunk, res, step=1, fine=False):
            ncols = dst.shape[2]
            per = max(1, 1024 // ncols) if fine else max(1, 4096 // ncols)
            for a in range(0, nchunk, per):
                b = min(a + per, nchunk)
                _stage_cast(dst[:, a:b, :], src[a * 128:b * 128, :].rearrange("(k p) n -> p k n", p=128), f"{res}:{a}" if fine else res, fine)

        def ffn_mm1(uT, uT_res, W1, actT, tag):
            for f in range(NF):
                pb = 2 + (f % 2)
                for k in range(8):
                    P.op("pe", lambda e, f=f, k=k, pb=pb: e.matmul(psf[pb][:, 0:128], W1[:, k, f * 128:(f + 1) * 128], uT[:, k, :], start=(k == 0), stop=(k == 7)),
                         reads=[uT_res, f"{tag}W1:{f * 128}"], writes=[PSN(pb)])
                for k in range(8):
                    P.op("pe", lambda e, f=f, k=k, pb=pb: e.matmul(psf[pb][:, 128:256], W1[:, k, DFF + f * 128:DFF + (f + 1) * 128], uT[:, k, :], start=(k == 0), stop=(k == 7)),
                         reads=[uT_res, f"{tag}W1:{DFF + f * 128}"], writes=[PSN(pb)])
                sg = sgs[f % 2]
                P.op("act", lambda e, pb=pb, sg=sg: e.activation(sg, psf[pb][:, 0:128], AF.Silu), reads=[PSN(pb)], writes=[f"sg{f % 2}"])
                P.op("dve", lambda e, f=f, pb=pb, sg=sg: e.tensor_tensor(actT[:, f, :], sg, psf[pb][:, 128:256], ALU.mult), reads=[f"sg{f % 2}", PSN(pb)], writes=[f"{tag}actT{f}"])

        def ffn_mm2(actT, W2, tag):
            for half in range(2):
                pb = 4 + half
                for f in range(NF):
                    P.op("pe", lambda e, f=f, half=half, pb=pb: e.matmul(psf[pb][:, 0:512], actT[:, f, :], W2[:, f, half * 512:(half + 1) * 512], start=(f == 0), stop=(f == NF - 1)),
                         reads=[f"{tag}actT{f}", f"{tag}W2:{f}"], writes=[PSN(pb)])

        def ffn_update(hres_in, h_ap, ghalf, tag):
            for half in range(2):
                pb = 4 + half
                hs = h_ap[:, half * 512:(half + 1) * 512]
                tm = tmp512
                P.op("dve", lambda e, half=half, pb=pb, tm=tm: e.tensor_tensor(tm, psf[pb][:, 0:512], ghalf[:, half * 512:(half + 1) * 512], ALU.mult), reads=[PSN(pb), tag + "mod"], writes=["tmp512"])
                P.op("dve", lambda e, hs=hs, tm=tm: e.tensor_tensor(hs, hs, tm, ALU.add), reads=["tmp512", hres_in], writes=[hres_in])

        def norm_mod(x_ap, x_res, Arow, shrow, out_b, out_res, tag):
            rstd_from(x_ap, x_res, junkD, ss1, rstd1, tag)
            P.op("dve", lambda e: e.scalar_tensor_tensor(tmpD, x_ap, rstd1, Arow, ALU.mult, ALU.mult), reads=[x_res, tag + "rstd", tag + "mod"], writes=["tmpD"])
            P.op("dve", lambda e: e.tensor_tensor(out_b, tmpD, shrow, ALU.add), reads=["tmpD", tag + "mod"], writes=[out_res])

        wstage["on"] = False
        A.n = ARENA_BYTES
        A.mark()
        scT = [A.bf16(8, 128), A.bf16(8, 128)]
        ctile = A.f32(D); cb = A.bf16(D)
        for g, src in enumerate((cpr, csr)):
            P.dma("sp", lambda e, src=src: e.dma_start(out=ctile, in_=src), writes=["ctile"])
            P.op("act", lambda e: e.activation(cb, ctile, AF.Silu), reads=["ctile"], writes=["cb"])
            transpose_block(cb, "cb", scT[g], f"scT{g}", 0)
        wA = [A.bf16(8, 512), A.bf16(8, 512)]
        brow = [A.f32(512), A.f32(512)]
        stg = [A.f32(512), A.f32(512)]
        for cg in range(18):
            s = cg % 2
            load_w_cols(wA[s], ada_w, cg * 512, (cg + 1) * 512, f"wA{s}")
            wa_res = [f"wA{s}"]
            P.dma("sp", lambda e, cg=cg, s=s: e.dma_start(out=brow[s][0:1, :], in_=ada_b[0:1, cg * 512:(cg + 1) * 512]), writes=[f"brow{s}"])
            for g in range(2):
                pb = 2 + g
                for k in range(8):
                    P.op("pe", lambda e, k=k, g=g, s=s, pb=pb: e.matmul(psf[pb][:, 0:512], scT[g][:, k, :], wA[s][:, k, :], start=(k == 0), stop=False),
                         reads=[f"scT{g}"] + wa_res, writes=[PSN(pb)])
                P.op("pe", lambda e, s=s, pb=pb: e.matmul(psf[pb][:, 0:512], ones_f[0:1, :], brow[s][0:1, :], start=False, stop=True),
                     reads=["ones_f", f"brow{s}"], writes=[PSN(pb)])
                P.op("act", lambda e, g=g, pb=pb: e.copy(stg[g], psf[pb][:, 0:512]), reads=[PSN(pb)], writes=[f"stg{g}"])
                P.dma("sp", lambda e, g=g, cg=cg: e.dma_start(out=MOD[g, :, cg * 512:(cg + 1) * 512], in_=stg[g]), reads=[f"stg{g}"], writes=[f"MOD{g}"])
        P.barrier()
        A.release()

        def load_mod(dst, g, chunk, res):
            P.dma("sp", lambda e: e.dma_start(out=dst, in_=MOD[g, :, chunk * D:(chunk + 1) * D]), writes=[res])

        def load_row_bcast(dst, src, res):
            P.dma("sp", lambda e: e.dma_start(out=dst, in_=src.partition_broadcast(128)), writes=[res])

        A.mark()
        W1 = A.bf16(8, 2 * DFF); W2 = A.bf16(NF, D)
        load_w_cols(W1, f1_in, 0, 2 * DFF, "f1W1", order=[x for f_ in range(NF) for x in (f_ * 128, DFF + f_ * 128)], fine=True)
        load_w_rows(W2, f1_out, NF, "f1W2", fine=True)
        A1 = A.f32(D); sh1 = A.f32(D); g1h = A.f32(D); A2 = A.f32(D); sh2 = A.f32(D); grow = A.f32(D)
        xt = [A.f32(D), A.f32(D)]
        xsb = [A.bf16(D), A.bf16(D)]; uT = [A.bf16(8, 128), A.bf16(8, 128)]; actT = A.bf16(NF, 128)
        sgs = [A.f32(128), A.f32(128)]
        tmp512 = A.f32(512); tmpD = A.f32(D); junkD = A.bf16(D)
        ss1 = A.f32(1); rstd1 = A.f32(1)
        xs2b = [A.bf16(D), A.bf16(D)]
        for g in range(2):
            load_mod(sh1, g, 0, "f1mod"); load_mod(A1, g, 1, "f1mod"); load_mod(g1h, g, 2, "f1mod")
            load_mod(sh2, g, 3, "f1mod"); load_mod(A2, g, 4, "f1mod")
            load_row_bcast(grow, n1, "grow")
            P.op("dve", lambda e: e.scalar_tensor_tensor(A1, A1, 1.0, grow, ALU.add, ALU.mult), reads=["f1mod", "grow"], writes=["f1mod"])
            P.op("dve", lambda e: e.tensor_scalar(g1h, g1h, 0.5, None, ALU.mult), reads=["f1mod"], writes=["f1mod"])
            load_row_bcast(grow, n2, "grow")
            P.op("dve", lambda e: e.scalar_tensor_tensor(A2, A2, 1.0, grow, ALU.add, ALU.mult), reads=["f1mod", "grow"], writes=["f1mod"])
            blocks = list(range(NBLK)) if g == 0 else [NBLK]

            def prep_norm(b, s):
                src = xp[b * 128:(b + 1) * 128, :] if b < NBLK else xs
                P.dma("sp", lambda e: e.dma_start(out=xt[s], in_=src), writes=[f"x{s}"])
                norm_mod(xt[s], f"x{s}", A1, sh1, xsb[s], f"xsb{s}", "f1")

            def prep_tr(s):
                transpose_block(xsb[s], f"xsb{s}", uT[s], f"uT{s}", 0)

            prep_norm(blocks[0], 0)
            prep_tr(0)
            for i, b in enumerate(blocks):
                s = i % 2
                x = xt[s]; xr = f"x{s}"
                ffn_mm1(uT[s], f"uT{s}", W1, actT, "f1")
                if i + 1 < len(blocks):
                    prep_norm(blocks[i + 1], 1 - s)
                ffn_mm2(actT, W2, "f1")
                if i + 1 < len(blocks):
                    prep_tr(1 - s)
                ffn_update(xr, x, g1h, "f1")
                P.dma("sp", lambda e, x=x, b=b: e.dma_start(out=H1[b * 128:(b + 1) * 128, :], in_=x), reads=[xr], writes=["H1"])
                norm_mod(x, xr, A2, sh2, xs2b[s], f"xs2b{s}", "f1")
                P.dma("sp", lambda e, s=s, b=b: e.dma_start(out=XS2[b * 128:(b + 1) * 128, :], in_=xs2b[s]), reads=[f"xs2b{s}"], writes=["XS2"])
        P.barrier()
        A.release()

        A.mark()
        OT_BYTES = 4 * (NOWN * 128 + 128) * 2
        OT = A.b[:, (ARENA_BYTES - OT_BYTES) // 2:ARENA_BYTES // 2].rearrange("p (a b) -> p a b", a=4)
        QTs = A.bf16(4, 128); KTn = A.bf16(4, 128); VBn = A.f32(512)
        ident_f = A.f32(128)
        P.dma("sp", lambda e: e.dma_start(out=ident_f, in_=ident_d), writes=["ident_f"])
        A.mark()
        KT = A.bf16(4, SEQ)
        VB = A.bf16(NBLK, 512)
        QT = A.bf16(4, NOWN * 128)
        A.mark()
        ST = A.bf16(4, SEQ + 128)
        A.mark()
        WinA = A.bf16(8, 1536)
        load_w_cols(WinA, w_in, 512, 2048, "Win")
        xin = [A.bf16(D), A.bf16(D)]
        u2T = A.bf16(8, 128)
        kf = A.f32(512); vf = A.f32(512)
        idxo = A.i32(16)
        sgat = A.bf16(8, 128); sgbt = A.bf16(8, 128)
        P.dma("sp", lambda e: e.dma_start(out=idxo, in_=idxown_d), writes=["idxo"])
        for b in range(NBLK + 1):
            s = b % 2
            P.dma("sp", lambda e, s=s, b=b: e.dma_start(out=xin[s], in_=XS2[b * 128:(b + 1) * 128, :]), writes=[f"xin{s}"])
            transpose_block(xin[s], f"xin{s}", u2T, "u2T", 0)
            for (c0, dstf, pb, nm) in ((512, kf, 2, "kf"), (1024, vf, 3, "vf")):
                for k in range(8):
                    P.op("pe", lambda e, k=k, c0=c0, pb=pb: e.matmul(psf[pb][:, 0:512], u2T[:, k, :], WinA[:, k, c0 - 512:c0], start=(k == 0), stop=(k == 7)),
                         reads=["u2T", "Win"], writes=[PSN(pb)])
                P.op("act", lambda e, dstf=dstf, pb=pb: e.copy(dstf, psf[pb][:, 0:512]), reads=[PSN(pb)], writes=[nm])
            ko = kp_o[b * 128:(b + 1) * 128, :] if b < NBLK else ks_o
            vo = vp_o[b * 128:(b + 1) * 128, :] if b < NBLK else vs_o
            P.dma("sp", lambda e, ko=ko: e.dma_start(out=ko, in_=kf), reads=["kf"], writes=[uid("ko")])
            P.dma("sp", lambda e, vo=vo: e.dma_start(out=vo, in_=vf), reads=["vf"], writes=[uid("vo")])
            if b < NBLK:
                P.op("pool", lambda e, b=b: e.tensor_copy(VB[:, b, :], vf), reads=["vf"], writes=[f"VB{b}"])
            else:
                P.op("pool", lambda e: e.tensor_copy(VBn, vf), reads=["vf"], writes=["VBn"])
            for (c0, dst, pb, nm) in ((512, KT, 4, "KT"), (1536, ST, 5, "ST")):
                for c in range(4):
                    for k in range(8):
                        P.op("pe", lambda e, k=k, c=c, c0=c0, pb=pb: e.matmul(psf[pb][:, c * 128:(c + 1) * 128], WinA[:, k, c0 - 512 + c * 128:c0 - 512 + (c + 1) * 128], u2T[:, k, :], start=(k == 0), stop=(k == 7)),
                             reads=["u2T", "Win"], writes=[PSN(pb)])
                dd = KTn if (nm == "KT" and b == NBLK) else dst[:, :, b * 128:(b + 1) * 128]
                P.op("act", lambda e, dd=dd, pb=pb: e.copy(dd, psf[pb][:, 0:512].rearrange("p (c n) -> p c n", c=4)),
                     reads=[PSN(pb)], writes=[f"{nm}{b}"])
        P.barrier()
        A.release()
        A.mark()
        WinQ = A.bf16(8, 512); WinG = A.bf16(8, 2048)
        load_w_cols(WinQ, w_in, 0, 512, "Win"); load_w_cols(WinG, w_in, 2048, 4096, "Win")
        xin = [A.bf16(D), A.bf16(D)]
        u2T = A.bf16(8, 128)
        idxo = A.i32(16)
        sgat = A.bf16(8, 128); sgbt = A.bf16(8, 128)
        P.dma("sp", lambda e: e.dma_start(out=idxo, in_=idxown_d), writes=["idxo"])
        for j in range(NOWN + 1):
            s = j % 2
            if j < NOWN:
                P.dma("pool", lambda e, s=s, j=j: e.indirect_dma_start(out=xin[s], out_offset=None, in_=XS2, in_offset=bass.IndirectOffsetOnAxis(ap=idxo[:, j:j + 1], axis=0)),
                      reads=["idxo"], writes=[f"xin{s}"])
            else:
                P.dma("sp", lambda e, s=s: e.dma_start(out=xin[s], in_=XS2[NBLK * 128:(NBLK + 1) * 128, :]), writes=[f"xin{s}"])
            transpose_block(xin[s], f"xin{s}", u2T, "u2T", 0)
            for c in range(4):
                for k in range(8):
                    P.op("pe", lambda e, k=k, c=c: e.matmul(psf[2][:, c * 128:(c + 1) * 128], WinQ[:, k, c * 128:(c + 1) * 128], u2T[:, k, :], start=(k == 0), stop=(k == 7)),
                         reads=["u2T", "Win"], writes=[PSN(2)])
            qd = QT[:, :, j * 128:(j + 1) * 128] if j < NOWN else QTs
            P.op("act", lambda e, qd=qd: e.copy(qd, psf[2][:, 0:512].rearrange("p (c n) -> p c n", c=4)), reads=[PSN(2)], writes=[f"QT{j}"])
            for (c0, dst, scr, nm) in ((2048, sgat, SGA, "sga"), (3072, sgbt, SGB, "sgb")):
                for hh in range(2):
                    pb = 4 + hh
                    for c in range(4):
                        cc = hh * 4 + c
                        for k in range(8):
                            P.op("pe", lambda e, k=k, c=c, cc=cc, c0=c0, pb=pb: e.matmul(psf[pb][:, c * 128:(c + 1) * 128], WinG[:, k, c0 - 2048 + cc * 128:c0 - 2048 + (cc + 1) * 128], u2T[:, k, :], start=(k == 0), stop=(k == 7)),
                                 reads=["u2T", "Win"], writes=[PSN(pb)])
                    P.op("act", lambda e, dst=dst, hh=hh, pb=pb: e.activation(dst[:, hh * 4:(hh + 1) * 4, :], psf[pb][:, 0:512].rearrange("p (c n) -> p c n", c=4), AF.Sigmoid),
                         reads=[PSN(pb)], writes=[nm])
                P.dma("sp", lambda e, dst=dst, scr=scr, j=j: e.dma_start(out=scr[j].rearrange("p (c n) -> p c n", c=8), in_=dst), reads=[nm], writes=[uid("sg")])
        P.barrier()
        A.release()

        wstage["on"] = False
        A.n = ARENA_BYTES
        A.mark()
        def pt16():
            return A.f32(16)
        lr = pt16(); li = pt16(); ldt = pt16(); dt_ = pt16(); rr = pt16(); th = pt16(); cth = pt16(); sth = pt16()
        abre = pt16(); abim = pt16(); nabim = pt16(); zr = pt16(); zi = pt16(); den = pt16(); q1 = pt16(); q2 = pt16(); nr = pt16()
        Fr = pt16(); Fi = pt16(); a512 = pt16(); init_re = pt16(); init_im = pt16(); fin_re = pt16(); fin_im = pt16()
        dsk = A.f32(4)
        Bre = A.bf16(16, 128); Bim = A.bf16(16, 128); Cre = A.bf16(16, 128); Cim = A.bf16(16, 128)
        iotaj = A.f32(512)
        tabs = [[A.f32(512) for _ in range(4)] for _ in range(4)]
        rt = [A.f32(512) for _ in range(4)]
        wi_re = A.f32(512); wi_im = A.f32(512); w_re = A.f32(512); w_im = A.f32(512); u1 = A.f32(512); u2 = A.f32(512)
        xre = A.bf16(4, 512); ximn = A.bf16(4, 512)
        ysb = wi_re; yown = wi_im[:, 0:256].rearrange("p (a n) -> p a n", a=2)
        _o = A.top; ki = A.i32(512); p1 = A.f[:, _o // 4:_o // 4 + 512]; kfl = u2; ang = u1; ang2 = wi_re
        sre0 = A.f32(16, 16); sim0 = A.f32(16, 16); sso_re = A.f32(16, 16); sso_im = A.f32(16, 16)
        bzr = A.f32(128); bzi = A.f32(128); xsr = w_re[:, 0:128]; xsi = w_im[:, 0:128]; st4 = [A.f32(16) for _ in range(4)]
        tq = A.f32(16); tq2 = A.f32(16)
        for dst, src, nm in ((lr, lam_re, "lr"), (li, lam_im, "li"), (ldt, logdt, "ldt"), (dsk, dsk_d, "dsk"), (iotaj, iotaj_d, "iotaj"),
                             (sre0, st_re, "sre0"), (sim0, st_im, "sim0")):
            P.dma("sp", lambda e, dst=dst, src=src: e.dma_start(out=dst, in_=src), writes=[nm])
        for dst, src, nm in ((Bre, bre_d, "Bre"), (Bim, bim_d, "Bim"), (Cre, cre_d, "Cre"), (Cim, cim_d, "Cim")):
            for hlf in range(2):
                P.dma("pool", lambda e, dst=dst, src=src, hlf=hlf: e.dma_start(out=dst[:, hlf * 8:(hlf + 1) * 8, :], in_=src[hlf * 8:(hlf + 1) * 8].rearrange("t p n -> p t n")), writes=[nm])

        def dve(fn, reads, writes):
            P.op("dve", fn, reads=reads, writes=writes)

        dve(lambda e: e.tensor_scalar(Cim, Cim, -1.0, None, ALU.mult), ["Cim"], ["Cim"])

        def sin_of(a_ap, a_res, out, out_res, n, shift=0.0):
            kiv, kfv, av = ki[:, 0:n], kfl[:, 0:n], ang2[:, 0:n]
            dve(lambda e: e.tensor_scalar(av, a_ap, shift, None, ALU.add), [a_res], ["wi_re"])
            dve(lambda e: e.tensor_scalar(kiv, av, 1.0 / TWO_PI, None, ALU.mult), ["wi_re"], ["ki"])
            dve(lambda e: e.tensor_copy(kfv, kiv), ["ki"], ["u2"])
            dve(lambda e: e.scalar_tensor_tensor(av, kfv, -TWO_PI, av, ALU.mult, ALU.add), ["u2", "wi_re"], ["wi_re"])
            dve(lambda e: e.tensor_scalar(av, av, -PI, PI, ALU.max, ALU.min), ["wi_re"], ["wi_re"])
            P.op("act", lambda e: e.activation(out, av, AF.Sin), reads=["wi_re"], writes=[out_res])

        P.op("act", lambda e: e.activation(dt_, ldt, AF.Exp), reads=["ldt"], writes=["dt"])
        dve(lambda e: e.tensor_tensor(q1, lr, dt_, ALU.mult), ["lr", "dt"], ["q1"])
        P.op("act", lambda e: e.activation(rr, q1, AF.Exp), reads=["q1"], writes=["rr"])
        dve(lambda e: e.tensor_tensor(th, li, dt_, ALU.mult), ["li", "dt"], ["th"])
        dve(lambda e: e.tensor_scalar(ki[:, 0:16], th, 1.0 / TWO_PI, None, ALU.mult), ["th"], ["ki"])
        dve(lambda e: e.tensor_copy(kfl[:, 0:16], ki[:, 0:16]), ["ki"], ["u2"])
        dve(lambda e: e.scalar_tensor_tensor(th, kfl[:, 0:16], -TWO_PI, th, ALU.mult, ALU.add), ["u2", "th"], ["th"])
        sin_of(th, "th", sth, "sth", 16)
        sin_of(th, "th", cth, "cth", 16, shift=PI / 2)
        dve(lambda e: e.tensor_tensor(abre, rr, cth, ALU.mult), ["rr", "cth"], ["abre"])
        dve(lambda e: e.tensor_tensor(abim, rr, sth, ALU.mult), ["rr", "sth"], ["abim"])
        dve(lambda e: e.tensor_scalar(nabim, abim, -1.0, None, ALU.mult), ["abim"], ["nabim"])
        dve(lambda e: e.tensor_tensor(den, lr, lr, ALU.mult), ["lr"], ["den"])
        dve(lambda e: e.tensor_tensor(q2, li, li, ALU.mult), ["li"], ["q2"])
        dve(lambda e: e.tensor_tensor(den, den, q2, ALU.add), ["den", "q2"], ["den"])
        dve(lambda e: e.reciprocal(den, den), ["den"], ["den"])
        dve(lambda e: e.tensor_scalar(nr, abre, -1.0, None, ALU.add), ["abre"], ["nr"])
        dve(lambda e: e.tensor_tensor(q1, nr, lr, ALU.mult), ["nr", "lr"], ["q1"])
        dve(lambda e: e.tensor_tensor(q2, abim, li, ALU.mult), ["abim", "li"], ["q2"])
        dve(lambda e: e.tensor_tensor(q1, q1, q2, ALU.add), ["q1", "q2"], ["q1"])
        dve(lambda e: e.tensor_tensor(zr, q1, den, ALU.mult), ["q1", "den"], ["zr"])
        dve(lambda e: e.tensor_tensor(q1, abim, lr, ALU.mult), ["abim", "lr"], ["q1"])
        dve(lambda e: e.tensor_tensor(q2, nr, li, ALU.mult), ["nr", "li"], ["q2"])
        dve(lambda e: e.tensor_tensor(q1, q1, q2, ALU.subtract), ["q1", "q2"], ["q1"])
        dve(lambda e: e.tensor_tensor(zi, q1, den, ALU.mult), ["q1", "den"], ["zi"])
        dve(lambda e: e.tensor_scalar(a512, th, 512.0, None, ALU.mult), ["th"], ["a512"])
        sin_of(a512, "a512", Fi, "Fi", 16)
        sin_of(a512, "a512", Fr, "Fr", 16, shift=PI / 2)

        for cc in range(4):
            for tl in range(4):
                ti = cc * 4 + tl
                c_t, s_t, er_t, ei_t = tabs[tl]
                tr = f"tab{tl}"
                dve(lambda e, ti=ti: e.tensor_scalar(ang, iotaj, th[:, ti:ti + 1], None, ALU.mult), ["iotaj", "th"], ["u1"])
                sin_of(ang, "u1", s_t, tr + "s", 512)
                sin_of(ang, "u1", c_t, tr + "c", 512, shift=PI / 2)
                dve(lambda e, ti=ti, c_t=c_t: e.tensor_scalar(u1, c_t, zr[:, ti:ti + 1], None, ALU.mult), [tr + "c", "zr"], ["u1"])
                dve(lambda e, ti=ti, s_t=s_t, er_t=er_t: e.scalar_tensor_tensor(er_t, s_t, zi[:, ti:ti + 1], u1, ALU.mult, ALU.add), [tr + "s", "zi", "u1"], [tr + "er"])
                dve(lambda e, ti=ti, s_t=s_t: e.tensor_scalar(u1, s_t, zr[:, ti:ti + 1], None, ALU.mult), [tr + "s", "zr"], ["u1"])
                dve(lambda e, ti=ti, c_t=c_t, ei_t=ei_t: e.scalar_tensor_tensor(ei_t, c_t, zi[:, ti:ti + 1], u1, ALU.mult, ALU.subtract), [tr + "c", "zi", "u1"], [tr + "ei"])
                dve(lambda e, ti=ti, tl=tl: e.tensor_scalar(rt[tl], ones_f.to_broadcast([128, 512]) if False else iotaj, 0.0, rr[:, ti:ti + 1], ALU.mult, ALU.add), ["iotaj", "rr"], [f"rt{tl}"])
                dve(lambda e, ti=ti: e.memset(init_re[:, ti:ti + 1], 0.0), [], [f"ire{ti}"])
                dve(lambda e, ti=ti: e.memset(init_im[:, ti:ti + 1], 0.0), [], [f"iim{ti}"])
            for c8 in range(8):
                t0 = c8 * 512
                for tl in range(4):
                    ti = cc * 4 + tl
                    c_t, s_t, er_t, ei_t = tabs[tl]
                    tr = f"tab{tl}"
                    P.op("pe", lambda e, ti=ti, t0=t0, cc=cc: e.matmul(psf[2][:, 0:512], Bre[:, ti, :], ST[:, cc, t0:t0 + 512], start=True, stop=True), reads=["Bre"], writes=[PSN(2)])
                    P.op("pe", lambda e, ti=ti, t0=t0, cc=cc: e.matmul(psf[3][:, 0:512], Bim[:, ti, :], ST[:, cc, t0:t0 + 512], start=True, stop=True), reads=["Bim"], writes=[PSN(3)])
                    dve(lambda e, er_t=er_t: e.tensor_tensor(u1, psf[2][:, 0:512], er_t, ALU.mult), [PSN(2), tr + "er"], ["u1"])
                    dve(lambda e, ei_t=ei_t: e.tensor_tensor(u2, psf[3][:, 0:512], ei_t, ALU.mult), [PSN(3), tr + "ei"], ["u2"])
                    dve(lambda e: e.tensor_tensor(wi_re, u1, u2, ALU.subtract), ["u1", "u2"], ["wi_re"])
                    dve(lambda e, er_t=er_t: e.tensor_tensor(u1, psf[3][:, 0:512], er_t, ALU.mult), [PSN(3), tr + "er"], ["u1"])
                    dve(lambda e, ei_t=ei_t: e.tensor_tensor(u2, psf[2][:, 0:512], ei_t, ALU.mult), [PSN(2), tr + "ei"], ["u2"])
                    dve(lambda e: e.tensor_tensor(wi_im, u1, u2, ALU.add), ["u1", "u2"], ["wi_im"])
                    dve(lambda e, ti=ti, tl=tl: e.tensor_tensor_scan(w_re, rt[tl], wi_re, init_re[:, ti:ti + 1], ALU.mult, ALU.add), [f"rt{tl}", "wi_re", f"ire{ti}"], ["w_re"])
                    dve(lambda e, ti=ti, tl=tl: e.tensor_tensor_scan(w_im, rt[tl], wi_im, init_im[:, ti:ti + 1], ALU.mult, ALU.add), [f"rt{tl}", "wi_im", f"iim{ti}"], ["w_im"])
                    if c8 < 7:
                        dve(lambda e, ti=ti: e.tensor_scalar(tq[:, 0:1], w_im[:, 511:512], Fi[:, ti:ti + 1], None, ALU.mult), ["w_im", "Fi"], ["tq"])
                        dve(lambda e, ti=ti: e.scalar_tensor_tensor(init_re[:, ti:ti + 1], w_re[:, 511:512], Fr[:, ti:ti + 1], tq[:, 0:1], ALU.mult, ALU.subtract), ["w_re", "Fr", "tq"], [f"ire{ti}"])
                        dve(lambda e, ti=ti: e.tensor_scalar(tq[:, 0:1], w_re[:, 511:512], Fi[:, ti:ti + 1], None, ALU.mult), ["w_re", "Fi"], ["tq"])
                        dve(lambda e, ti=ti: e.scalar_tensor_tensor(init_im[:, ti:ti + 1], w_im[:, 511:512], Fr[:, ti:ti + 1], tq[:, 0:1], ALU.mult, ALU.add), ["w_im", "Fr", "tq"], [f"iim{ti}"])
                    dve(lambda e, c_t=c_t: e.tensor_tensor(u1, c_t, w_re, ALU.mult), [tr + "c", "w_re"], ["u1"])
                    dve(lambda e, s_t=s_t: e.tensor_tensor(u2, s_t, w_im, ALU.mult), [tr + "s", "w_im"], ["u2"])
                    dve(lambda e, tl=tl: e.tensor_tensor(xre[:, tl, :], u1, u2, ALU.subtract), ["u1", "u2"], [f"xre{tl}"])
                    if c8 == 7:
                        dve(lambda e, ti=ti: e.tensor_tensor(fin_re[:, ti:ti + 1], u1[:, 511:512], u2[:, 511:512], ALU.subtract), ["u1", "u2"], ["fin_re"])
                    P.op("pool", lambda e, s_t=s_t: e.tensor_tensor(p1, s_t, w_re, ALU.mult), reads=[tr + "s", "w_re"], writes=["ki"])
                    P.op("pool", lambda e, c_t=c_t, tl=tl: e.tensor_tensor(ximn[:, tl, :], c_t, w_im, ALU.mult), reads=[tr + "c", "w_im"], writes=[f"ximn{tl}"])
                    P.op("pool", lambda e, tl=tl: e.tensor_tensor(ximn[:, tl, :], ximn[:, tl, :], p1, ALU.add), reads=["ki", f"ximn{tl}"], writes=[f"ximn{tl}"])
                    if c8 == 7:
                        dve(lambda e, ti=ti, s_t=s_t: e.tensor_tensor(tq[:, 0:1], s_t[:, 511:512], w_re[:, 511:512], ALU.mult), [tr + "s", "w_re"], ["tq"])
                        dve(lambda e, ti=ti, c_t=c_t: e.scalar_tensor_tensor(fin_im[:, ti:ti + 1], c_t[:, 511:512], w_im[:, 511:512], tq[:, 0:1], ALU.mult, ALU.add), [tr + "c", "w_im", "tq"], ["fin_im"])
                for tl in range(4):
                    ti = cc * 4 + tl
                    P.op("pe", lambda e, ti=ti, tl=tl: e.matmul(psf[4][:, 0:512], Cre[:, ti, :], xre[:, tl, :], start=(tl == 0), stop=False), reads=["Cre", f"xre{tl}"], writes=[PSN(4)])
                    P.op("pe", lambda e, ti=ti, tl=tl: e.matmul(psf[4][:, 0:512], Cim[:, ti, :], ximn[:, tl, :], start=False, stop=(tl == 3)), reads=["Cim", f"ximn{tl}"], writes=[PSN(4)])
                dve(lambda e, cc=cc, t0=t0: e.scalar_tensor_tensor(ysb, ST[:, cc, t0:t0 + 512], dsk[:, cc:cc + 1], psf[4][:, 0:512], ALU.mult, ALU.add), ["dsk", PSN(4)], ["wi_re"])
                yv = ysb.rearrange("p (a b n) -> p a b n", a=2, b=2)
                dve(lambda e, yv=yv: e.tensor_scalar(u1[:, 0:256].rearrange("p (a n) -> p a n", a=2), yv[:, :, 0, :], p0p1[:, 0:1], None, ALU.mult), ["wi_re"], ["u1"])
                dve(lambda e, yv=yv: e.scalar_tensor_tensor(yown, yv[:, :, 1, :], p0p1[:, 1:2], u1[:, 0:256].rearrange("p (a n) -> p a n", a=2), ALU.mult, ALU.add), ["wi_re", "u1"], ["wi_im"])
                for jj in range(2):
                    P.dma("sp", lambda e, jj=jj, c8=c8, cc=cc: e.dma_start(out=YT[2 * c8 + jj, :, cc * 128:(cc + 1) * 128], in_=yown[:, jj, :]), reads=["wi_im"], writes=[uid("YT")])
            for tl in range(4):
                ti = cc * 4 + tl
                P.op("pe", lambda e, ti=ti, cc=cc: e.matmul(psf[2][:, 0:128], Bre[:, ti, :], ST[:, cc, SEQ:SEQ + 128], start=True, stop=True), reads=["Bre"], writes=[PSN(2)])
                P.op("pe", lambda e, ti=ti, cc=cc: e.matmul(psf[3][:, 0:128], Bim[:, ti, :], ST[:, cc, SEQ:SEQ + 128], start=True, stop=True), reads=["Bim"], writes=[PSN(3)])
                dve(lambda e, ti=ti: e.tensor_scalar(u1[:, 0:128], psf[3][:, 0:128], zi[:, ti:ti + 1], None, ALU.mult), [PSN(3), "zi"], ["u1"])
                dve(lambda e, ti=ti: e.scalar_tensor_tensor(bzr, psf[2][:, 0:128], zr[:, ti:ti + 1], u1[:, 0:128], ALU.mult, ALU.subtract), [PSN(2), "zr", "u1"], ["bzr"])
                dve(lambda e, ti=ti: e.tensor_scalar(u1[:, 0:128], psf[2][:, 0:128], zi[:, ti:ti + 1], None, ALU.mult), [PSN(2), "zi"], ["u1"])
                dve(lambda e, ti=ti: e.scalar_tensor_tensor(bzi, psf[3][:, 0:128], zr[:, ti:ti + 1], u1[:, 0:128], ALU.mult, ALU.add), [PSN(3), "zr", "u1"], ["bzi"])
                bzr_v = bzr.rearrange("p (n i) -> p n i", i=8); bzi_v = bzi.rearrange("p (n i) -> p n i", i=8)
                xsr_v = xsr.rearrange("p (n i) -> p n i", i=8); xsi_v = xsi.rearrange("p (n i) -> p n i", i=8)
                for i in range(8):
                    pre = sre0[:, ti, :] if i == 0 else xsr_v[:, :, i - 1]
                    pim = sim0[:, ti, :] if i == 0 else xsi_v[:, :, i - 1]
                    rd = ["sre0", "sim0", "w_re", "w_im", "bzr", "bzi", "abre", "abim", "nabim"]
                    dve(lambda e, ti=ti, i=i, pim=pim: e.scalar_tensor_tensor(tq, pim, nabim[:, ti:ti + 1], bzr_v[:, :, i], ALU.mult, ALU.add), rd, ["tq"])
                    dve(lambda e, ti=ti, i=i, pre=pre: e.scalar_tensor_tensor(tq2, pre, abim[:, ti:ti + 1], bzi_v[:, :, i], ALU.mult, ALU.add), rd, ["tq2"])
                    dve(lambda e, ti=ti, i=i, pre=pre: e.scalar_tensor_tensor(xsr_v[:, :, i], pre, abre[:, ti:ti + 1], tq, ALU.mult, ALU.add), rd + ["tq"], ["w_re"])
                    dve(lambda e, ti=ti, i=i, pim=pim: e.scalar_tensor_tensor(xsi_v[:, :, i], pim, abre[:, ti:ti + 1], tq2, ALU.mult, ALU.add), rd + ["tq2"], ["w_im"])
                dve(lambda e, ti=ti: e.tensor_copy(sso_re[:, ti, :], xsr_v[:, :, 7]), ["w_re"], ["sso_re"])
                dve(lambda e, ti=ti: e.tensor_copy(sso_im[:, ti, :], xsi_v[:, :, 7]), ["w_im"], ["sso_im"])
                dve(lambda e, tl=tl: e.tensor_copy(xre[:, tl, 0:128], xsr), ["w_re"], [f"xre{tl}"])
                dve(lambda e, tl=tl: e.tensor_copy(ximn[:, tl, 0:128], xsi), ["w_im"], [f"ximn{tl}"])
            for tl in range(4):
                ti = cc * 4 + tl
                P.op("pe", lambda e, ti=ti, tl=tl: e.matmul(psf[4][:, 0:128], Cre[:, ti, :], xre[:, tl, 0:128], start=(tl == 0), stop=False), reads=["Cre", f"xre{tl}"], writes=[PSN(4)])
                P.op("pe", lambda e, ti=ti, tl=tl: e.matmul(psf[4][:, 0:128], Cim[:, ti, :], ximn[:, tl, 0:128], start=False, stop=(tl == 3)), reads=["Cim", f"ximn{tl}"], writes=[PSN(4)])
            dve(lambda e, cc=cc: e.scalar_tensor_tensor(ysb[:, 0:128], ST[:, cc, SEQ:SEQ + 128], dsk[:, cc:cc + 1], psf[4][:, 0:128], ALU.mult, ALU.add), ["dsk", PSN(4)], ["wi_re"])
            P.dma("sp", lambda e, cc=cc: e.dma_start(out=YT[16, :, cc * 128:(cc + 1) * 128], in_=ysb[:, 0:128]), reads=["wi_re"], writes=[uid("YT")])
        P.dma("sp", lambda e: e.dma_start(out=spre_o, in_=fin_re), reads=["fin_re"], writes=[uid("o")])
        P.dma("sp", lambda e: e.dma_start(out=spim_o, in_=fin_im), reads=["fin_im"], writes=[uid("o")])
        P.dma("sp", lambda e: e.dma_start(out=ssre_o, in_=sso_re), reads=["sso_re"], writes=[uid("o")])
        P.dma("sp", lambda e: e.dma_start(out=ssim_o, in_=sso_im), reads=["sso_im"], writes=[uid("o")])
        P.barrier()
        A.release()
        A.release()

        A.n = ARENA_BYTES - OT_BYTES
        A.mark()
        amask = A.bf16(8, 512)
        for m in range(8):
            P.dma("pool", lambda e, m=m: e.dma_start(out=amask[:, m, :], in_=amask_d[m]), writes=["amask"])
        HC = []
        for hi in range(2):
            HC.append(dict(E=[A.f32(512) for _ in range(3)], SP=[A.bf16(512), A.bf16(512)], SPf=A.f32(512), R=A.bf16(512),
                           T=[A.f32(512), A.f32(512)], W=[A.bf16(512), A.bf16(512)], Wf=A.f32(512), psS=hi, psC=[2 + 2 * hi, 3 + 2 * hi], hi=hi))

        def stA(hc, g, h, kb, k):
            hi = hc["hi"]
            c, hb = h // 2, (h % 2) * 64
            ps_s = hc["psS"]
            E = hc["E"][k % 3]; SP = hc["SP"][k % 2]; SPf = hc["SPf"]
            en = f"E{hi}{k % 3}"; spn = f"SP{hi}{k % 2}"
            P.op("pe", lambda e: e.matmul(psf[ps_s][:, 0:512], KT[hb:hb + 64, c, kb * 128:(kb + 1) * 128], QT[hb:hb + 64, c, g * 512:(g + 1) * 512], start=True, stop=True),
                 reads=[], writes=[PSN(ps_s)])
            P.op("act", lambda e: e.activation(E, psf[ps_s][:, 0:512], AF.Exp, scale=0.125, bias=sbb[:, h:h + 1]), reads=[PSN(ps_s)], writes=[en])
            if kb >= 8 * g:
                m = kb - 8 * g
                P.op("act", lambda e: e.activation(SPf, E, AF.Ln, bias=1.0), reads=[en], writes=[f"SPf{hi}"])
                P.op("dve", lambda e: e.tensor_tensor(SP, SPf, amask[:, m, :], ALU.mult), reads=[f"SPf{hi}", "amask"], writes=[spn])
            else:
                P.op("act", lambda e: e.activation(SP, E, AF.Ln, bias=1.0), reads=[en], writes=[spn])

        def stB(hc, k, first, last):
            hi = hc["hi"]
            ps_c = hc["psC"][k % 2]
            SP = hc["SP"][k % 2]; R = hc["R"]; spn = f"SP{hi}{k % 2}"
            P.op("pe", lambda e: e.matmul(psf[ps_c][:, 0:512], ltri_b, SP, start=True, stop=first), reads=[spn], writes=[PSN(ps_c)])
            if not first:
                P.op("pe", lambda e: e.matmul(psf[ps_c][:, 0:512], ones_b, R, start=False, stop=True), reads=[f"R{hi}"], writes=[PSN(ps_c)])
            if not last:
                if first:
                    P.op("pool", lambda e: e.tensor_copy(R, SP), reads=[spn], writes=[f"R{hi}"])
                else:
                    P.op("pool", lambda e: e.tensor_tensor(R, R, SP, ALU.add), reads=[spn, f"R{hi}"], writes=[f"R{hi}"])

        def stC(hc, g, h, kb, k):
            hi = hc["hi"]
            ps_c = hc["psC"][k % 2]
            E = hc["E"][k % 3]; T = hc["T"][k % 2]; W = hc["W"][k % 2]; Wf = hc["Wf"]
            en = f"E{hi}{k % 3}"; tn = f"T{hi}{k % 2}"; wn = f"W{hi}{k % 2}"
            P.op("act", lambda e: e.activation(T, psf[ps_c][:, 0:512], AF.Exp, scale=-1.0), reads=[PSN(ps_c)], writes=[tn])
            if kb >= 8 * g:
                m = kb - 8 * g
                P.op("dve", lambda e: e.tensor_tensor(Wf, E, T, ALU.mult), reads=[en, tn], writes=[f"Wf{hi}"])
                P.op("dve", lambda e: e.tensor_tensor(W, Wf, amask[:, m, :], ALU.mult), reads=[f"Wf{hi}", "amask"], writes=[wn])
            else:
                P.op("dve", lambda e: e.tensor_tensor(W, E, T, ALU.mult), reads=[en, tn], writes=[wn])

        def stD(hc, h, kb, k, first, last):
            hi = hc["hi"]
            hb = (h % 2) * 64
            W = hc["W"][k % 2]; wn = f"W{hi}{k % 2}"
            P.op("pe", lambda e: e.matmul(psf[6][hb:hb + 64, 0:512], VB[:, kb, h * 64:(h + 1) * 64], W, start=first, stop=last),
                 reads=[wn], writes=[f"psO{hi}"])

        for g in range(4):
            nkb = 8 * (g + 1)
            kbs = list(range(nkb - 1, -1, -1))
            n = len(kbs)
            for hp in range(4):
                hs = [2 * hp, 2 * hp + 1]
                for i in range(n + 2):
                    if 0 <= i - 2 < n:
                        for hi in range(2):
                            stC(HC[hi], g, hs[hi], kbs[i - 2], i - 2)
                    if i < n:
                        for hi in range(2):
                            stA(HC[hi], g, hs[hi], kbs[i], i)
                    if 0 <= i - 1 < n:
                        for hi in range(2):
                            stB(HC[hi], i - 1, i - 1 == 0, i - 1 == n - 1)
                    if 0 <= i - 2 < n:
                        for hi in range(2):
                            stD(HC[hi], hs[hi], kbs[i - 2], i - 2, i - 2 == 0, i - 2 == n - 1)
                for hi in range(2):
                    h = hs[hi]; c, hb = h // 2, (h % 2) * 64
                    P.op("act", lambda e, c=c, hb=hb, g=g: e.copy(OT[hb:hb + 64, c, g * 512:(g + 1) * 512], psf[6][hb:hb + 64, 0:512]), reads=[f"psO{hi}"], writes=[uid("OT")])
        P.barrier()
        A.release()
        A.release()

        A.mark()
        sbrow = A.f32(1); smask = A.f32(16, 128)
        pti = A.i32(256); ptf = A.f32(256); idx = A.i32(256)
        Qbd = A.bf16(4, 16, 16)
        NR = 4
        Kf = A.f32(16, 512); Vf = A.f32(16, 512); Kb = A.bf16(NR, 512); Vb = A.bf16(16, 512)
        KTs = A.bf16(4, 2176)
        NK = 2176
        Es = A.f32(NK); SPs = A.f32(NK); Fs = A.f32(NK); zer = A.f32(NK); Wsb = A.bf16(NK)
        WT = A.bf16(17, 128)
        ntot = A.f32(1); VBnb = A.bf16(512)
        P.dma("sp", lambda e: e.dma_start(out=sbrow, in_=di["sb_rows"]), writes=["sbrow"])
        P.dma("sp", lambda e: e.dma_start(out=smask, in_=smask_d), writes=["smask"])
        P.dma("sp", lambda e: e.dma_start(out=pti, in_=ptab.partition_broadcast(128)), writes=["pti"])
        dve(lambda e: e.tensor_copy(ptf, pti), ["pti"], ["ptf"])
        dve(lambda e: e.tensor_scalar(ptf, ptf, 128.0, iota_p[:, 0:1], ALU.mult, ALU.add), ["ptf"], ["ptf"])
        dve(lambda e: e.tensor_copy(idx, ptf), ["ptf"], ["idx"])
        dve(lambda e: e.memset(zer, 0.0), [], ["zer"])
        dve(lambda e: e.memset(Qbd, 0.0), [], ["Qbd"])
        dve(lambda e: e.tensor_copy(VBnb, VBn), [], ["VBnb"])
        for c in range(4):
            for hh in range(2):
                dve(lambda e, c=c, hh=hh: e.tensor_copy(Qbd[hh * 64:(hh + 1) * 64, c, :, hh * 8:(hh + 1) * 8],
                                                        QTs[hh * 64:(hh + 1) * 64, c, :].rearrange("p (n i) -> p n i", i=8)), ["Qbd"], ["Qbd"])
        dve(lambda e: e.tensor_copy(KTs[:, :, 2048:2176], KTn), [], ["KTnew"])
        gcount = {"k": 0, "v": 0}

        def k_gather(n):
            for pg in range(16):
                col = n * 16 + pg
                sl = pg
                P.dma("pool", lambda e, sl=sl, col=col: e.indirect_dma_start(out=Kf[:, sl, :], out_offset=None, in_=cache_k, in_offset=bass.IndirectOffsetOnAxis(ap=idx[:, col:col + 1], axis=0)),
                      reads=["idx"], writes=[f"Kf{sl}"])

        def v_gather(n):
            for pg in range(16):
                col = n * 16 + pg
                sl = pg
                P.dma("pool", lambda e, sl=sl, col=col: e.indirect_dma_start(out=Vf[:, sl, :], out_offset=None, in_=cache_v, in_offset=bass.IndirectOffsetOnAxis(ap=idx[:, col:col + 1], axis=0)),
                      reads=["idx"], writes=[f"Vf{sl}"])

        def k_process(n):
            for pg in range(16):
                sl = pg % NR
                dve(lambda e, sl=sl, pg=pg: e.tensor_copy(Kb[:, sl, :], Kf[:, pg, :]), [f"Kf{pg}"], [f"Kb{sl}"])
                for c in range(4):
                    P.op("pe", lambda e, sl=sl, c=c: e.transpose(psb[7][:, c * 128:(c + 1) * 128], Kb[:, sl, c * 128:(c + 1) * 128], ident_b), reads=[f"Kb{sl}"], writes=[PSN(7)])
                P.op("act", lambda e, pg=pg: e.copy(KTs[:, :, pg * 128:(pg + 1) * 128], psb[7][:, 0:512].rearrange("p (c n) -> p c n", c=4)), reads=[PSN(7)], writes=[f"KTs{pg}"])

        def v_process(n):
            for pg in range(16):
                P.op("act", lambda e, pg=pg: e.copy(Vb[:, pg, :], Vf[:, pg, :]), reads=[f"Vf{pg}"], writes=[f"Vb{pg}"])

        ktr = [f"KTs{pg}" for pg in range(16)] + ["KTnew"]
        k_gather(0); v_gather(0)
        k_process(0); v_process(0)
        for n in range(16):
            for cb in range(5):
                w = 512 if cb < 4 else 128
                for c in range(4):
                    P.op("pe", lambda e, c=c, cb=cb, w=w, n=n: e.matmul(psf[cb][32 * c:32 * c + 16, 0:w], Qbd[:, c, n, :], KTs[:, c, cb * 512:cb * 512 + w], start=True, stop=True, tile_position=(0, 32 * c)),
                         reads=ktr + ["Qbd"], writes=[PSN(cb)])
                P.op("act", lambda e, cb=cb, w=w: e.activation(Es[:, cb * 512:cb * 512 + w], psf[cb][:, 0:w], AF.Exp, scale=0.125, bias=sbrow), reads=[PSN(cb), "sbrow"], writes=["Es"])
            if n + 1 < 16:
                k_gather(n + 1); v_gather(n + 1)
            P.op("act", lambda e: e.activation(SPs, Es, AF.Ln, bias=1.0), reads=["Es"], writes=["SPs"])
            dve(lambda e, n=n: e.tensor_tensor(SPs[:, 2048:2176], SPs[:, 2048:2176], smask[:, n, :], ALU.mult), ["SPs", "smask"], ["SPs"])
            dve(lambda e: e.tensor_tensor_scan(Fs, zer, SPs, 0.0, ALU.add, ALU.add), ["zer", "SPs"], ["Fs"])
            dve(lambda e: e.tensor_scalar(ntot, Fs[:, NK - 1:NK], -1.0, None, ALU.mult), ["Fs"], ["ntot"])
            dve(lambda e: e.tensor_tensor(Fs, Fs, SPs, ALU.subtract), ["Fs", "SPs"], ["Fs"])
            P.op("act", lambda e: e.activation(SPs, Fs, AF.Exp, bias=ntot), reads=["Fs", "ntot"], writes=["SPs"])
            dve(lambda e: e.tensor_tensor(Wsb[:, 0:2048], Es[:, 0:2048], SPs[:, 0:2048], ALU.mult), ["Es", "SPs"], ["Wsb"])
            dve(lambda e: e.tensor_tensor(Fs[:, 0:128], Es[:, 2048:2176], SPs[:, 2048:2176], ALU.mult), ["Es", "SPs"], ["Fs"])
            dve(lambda e, n=n: e.tensor_tensor(Wsb[:, 2048:2176], Fs[:, 0:128], smask[:, n, :], ALU.mult), ["Fs", "smask"], ["Wsb"])
            if n + 1 < 16:
                k_process(n + 1)
            for blk in range(17):
                P.op("pe", lambda e, blk=blk: e.transpose(psb[5][:, (blk % 8) * 128:(blk % 8 + 1) * 128], Wsb[:, blk * 128:(blk + 1) * 128], ident_b), reads=["Wsb"], writes=[PSN(5)])
                if blk % 8 == 7 or blk == 16:
                    b0 = blk - (blk % 8); nb = blk % 8 + 1
                    P.op("act", lambda e, b0=b0, nb=nb: e.copy(WT[:, b0:b0 + nb, :], psb[5][:, 0:nb * 128].rearrange("p (b n) -> p b n", b=nb)), reads=[PSN(5)], writes=["WT"])
            for c in range(4):
                for blk in range(17):
                    vsrc = Vb[:, blk, c * 128:(c + 1) * 128] if blk < 16 else VBnb[:, c * 128:(c + 1) * 128]
                    P.op("pe", lambda e, c=c, blk=blk, vsrc=vsrc: e.matmul(psf[6][:, c * 16:(c + 1) * 16], vsrc, WT[:, blk, 32 * c:32 * c + 16], start=(blk == 0), stop=(blk == 16)),
                         reads=["WT"] + ([f"Vb{blk}"] if blk < 16 else ["VBnb"]), writes=[PSN(6)])
            for hh in range(2):
                P.op("act", lambda e, hh=hh, n=n: e.copy(OT[hh * 64:(hh + 1) * 64, :, NOWN * 128 + 8 * n:NOWN * 128 + 8 * n + 8],
                                                         psf[6][hh * 64:(hh + 1) * 64, 0:64].rearrange("p (c k) -> p c k", c=4)[:, :, hh * 8:(hh + 1) * 8]), reads=[PSN(6)], writes=[uid("OTs")])
            if n + 1 < 16:
                v_process(n + 1)
        P.barrier()
        A.release()

        A.mark()
        Wa = A.bf16(4, D); Wbb = A.bf16(4, D); Wg = A.bf16(4, 512); Wo = A.bf16(8, D)
        load_w_rows(Wa, wba, 4, "Wa"); load_w_rows(Wbb, wbb, 4, "Wbb"); load_w_rows(Wg, glu_w, 4, "Wg"); load_w_rows(Wo, w_o, 8, "Wo")
        glub = A.f32(4); idxo = A.i32(16)
        P.dma("sp", lambda e: e.dma_start(out=glub, in_=glub_d), writes=["glub"])
        P.dma("sp", lambda e: e.dma_start(out=idxo, in_=idxown_d), writes=["idxo"])
        g2r = A.f32(D); sh3 = A.f32(D); A3 = A.f32(D); grow = A.f32(D)
        yt = A.f32(512); y2 = A.f32(512); zg = A.f32(512); sgm = A.f32(512); zgb = A.bf16(4, 128); obT = A.bf16(4, 128)
        sga = A.bf16(8, 128); sgb = A.bf16(8, 128); m1 = A.f32(512); m2 = A.f32(512); mgT = A.bf16(8, 128)
        h1t = A.f32(D); tmp512 = A.f32(512); tmpD = A.f32(D); junkD = A.bf16(D); ss1 = A.f32(1); rstd1 = A.f32(1)
        xs3b = A.bf16(D)
        for g in range(2):
            load_mod(g2r, g, 5, "p5mod"); load_mod(sh3, g, 6, "p5mod"); load_mod(A3, g, 7, "p5mod")
            load_row_bcast(grow, n3, "grow")
            dve(lambda e: e.scalar_tensor_tensor(A3, A3, 1.0, grow, ALU.add, ALU.mult), ["p5mod", "grow"], ["p5mod"])
            for j in (range(NOWN) if g == 0 else [NOWN]):
                P.dma("sp", lambda e, j=j: e.dma_start(out=yt, in_=YT[j]), writes=["yt"])
                dve(lambda e: e.tensor_tensor(y2, yt, yt, ALU.mult), ["yt"], ["y2"])
                dve(lambda e: e.tensor_scalar(y2, y2, 0.044715, 1.0, ALU.mult, ALU.add), ["y2"], ["y2"])
                dve(lambda e: e.tensor_tensor(y2, y2, yt, ALU.mult), ["y2", "yt"], ["y2"])
                P.op("act", lambda e: e.activation(sgm, y2, AF.Sigmoid, scale=1.5957691216057308), reads=["y2"], writes=["sgm"])
                dve(lambda e: e.tensor_tensor(zg, yt, sgm, ALU.mult), ["yt", "sgm"], ["zg"])
                dve(lambda e: e.tensor_copy(zgb, zg.rearrange("p (c n) -> p c n", c=4)), ["zg"], ["zgb"])
                for c in range(4):
                    for k in range(4):
                        P.op("pe", lambda e, c=c, k=k: e.matmul(psf[1][:, c * 128:(c + 1) * 128], Wg[:, k, c * 128:(c + 1) * 128], zgb[:, k, :], start=(k == 0), stop=(k == 3)), reads=["Wg", "zgb"], writes=[PSN(1)])
                for c in range(4):
                    P.op("act", lambda e, c=c: e.activation(sgm[:, c * 128:(c + 1) * 128], psf[1][:, c * 128:(c + 1) * 128], AF.Sigmoid, bias=glub[:, c:c + 1]), reads=[PSN(1), "glub", "zg"], writes=["sgm"])
                dve(lambda e: e.tensor_tensor(obT, zg.rearrange("p (c n) -> p c n", c=4), sgm.rearrange("p (c n) -> p c n", c=4), ALU.mult), ["zg", "sgm"], ["obT"])
                P.dma("sp", lambda e, j=j: e.dma_start(out=sga, in_=SGA[j].rearrange("p (c n) -> p c n", c=8)), writes=["sga"])
                P.dma("sp", lambda e, j=j: e.dma_start(out=sgb, in_=SGB[j].rearrange("p (c n) -> p c n", c=8)), writes=["sgb"])
                for hh in range(2):
                    for c4 in range(4):
                        dc = hh * 4 + c4
                        for k in range(4):
                            P.op("pe", lambda e, hh=hh, c4=c4, dc=dc, k=k, j=j: e.matmul(psf[2 + hh][:, c4 * 128:(c4 + 1) * 128], Wa[:, k, dc * 128:(dc + 1) * 128], OT[:, k, j * 128:(j + 1) * 128], start=(k == 0), stop=(k == 3)),
                                 reads=["Wa"], writes=[PSN(2 + hh)])
                        for k in range(4):
                            P.op("pe", lambda e, hh=hh, c4=c4, dc=dc, k=k: e.matmul(psf[4 + hh][:, c4 * 128:(c4 + 1) * 128], Wbb[:, k, dc * 128:(dc + 1) * 128], obT[:, k, :], start=(k == 0), stop=(k == 3)),
                                 reads=["Wbb", "obT"], writes=[PSN(4 + hh)])
                    dve(lambda e, hh=hh: e.tensor_tensor(m1, psf[2 + hh][:, 0:512], sga[:, hh * 4:(hh + 1) * 4, :].rearrange("p c n -> p (c n)"), ALU.mult), [PSN(2 + hh), "sga"], ["m1"])
                    dve(lambda e, hh=hh: e.tensor_tensor(m2, psf[4 + hh][:, 0:512], sgb[:, hh * 4:(hh + 1) * 4, :].rearrange("p c n -> p (c n)"), ALU.mult), [PSN(4 + hh), "sgb"], ["m2"])
                    dve(lambda e, hh=hh: e.tensor_tensor(mgT[:, hh * 4:(hh + 1) * 4, :], m1.rearrange("p (c n) -> p c n", c=4), m2.rearrange("p (c n) -> p c n", c=4), ALU.add), ["m1", "m2"], ["mgT"])
                if j < NOWN:
                    P.dma("pool", lambda e, j=j: e.indirect_dma_start(out=h1t, out_offset=None, in_=H1, in_offset=bass.IndirectOffsetOnAxis(ap=idxo[:, j:j + 1], axis=0)), reads=["idxo"], writes=["h1t"])
                else:
                    P.dma("sp", lambda e: e.dma_start(out=h1t, in_=H1[NBLK * 128:(NBLK + 1) * 128, :]), writes=["h1t"])
                for half in range(2):
                    for k in range(8):
                        P.op("pe", lambda e, half=half, k=k: e.matmul(psf[6 + half][:, 0:512], mgT[:, k, :], Wo[:, k, half * 512:(half + 1) * 512], start=(k == 0), stop=(k == 7)), reads=["mgT", "Wo"], writes=[PSN(6 + half)])
                    hs = h1t[:, half * 512:(half + 1) * 512]
                    dve(lambda e, half=half: e.tensor_tensor(tmp512, psf[6 + half][:, 0:512], g2r[:, half * 512:(half + 1) * 512], ALU.mult), [PSN(6 + half), "p5mod"], ["tmp512"])
                    dve(lambda e, hs=hs: e.tensor_tensor(hs, hs, tmp512, ALU.add), ["tmp512", "h1t"], ["h1t"])
                P.dma("sp", lambda e, j=j: e.dma_start(out=H2[j * 128:(j + 1) * 128, :], in_=h1t), reads=["h1t"], writes=[uid("H2")])
                norm_mod(h1t, "h1t", A3, sh3, xs3b, "xs3b", "p5")
                P.dma("sp", lambda e, j=j: e.dma_start(out=XS3[j * 128:(j + 1) * 128, :], in_=xs3b), reads=["xs3b"], writes=[uid("XS3")])
        P.barrier()
        A.release()
        A.release()

        wstage["on"] = False
        A.n = ARENA_BYTES
        A.mark()
        W1 = A.bf16(8, 2 * DFF); W2 = A.bf16(NF, D)
        load_w_cols(W1, f2_in, 0, 2 * DFF, "f2W1", order=[x for f_ in range(NF) for x in (f_ * 128, DFF + f_ * 128)], fine=True)
        load_w_rows(W2, f2_out, NF, "f2W2", fine=True)
        g3h = A.f32(D); fnr = A.f32(D)
        ht = [A.f32(D), A.f32(D)]; xin = [A.bf16(D), A.bf16(D)]
        uT = [A.bf16(8, 128), A.bf16(8, 128)]; actT = A.bf16(NF, 128)
        sgs = [A.f32(128), A.f32(128)]
        tmp512 = A.f32(512); junkD = A.bf16(D); ss1 = A.f32(1); rstd1 = A.f32(1)
        yo = [A.f32(D), A.f32(D)]
        load_row_bcast(fnr, nfin, "fnr")
        for g in range(2):
            load_mod(g3h, g, 8, "f2mod")
            dve(lambda e: e.tensor_scalar(g3h, g3h, 0.5, None, ALU.mult), ["f2mod"], ["f2mod"])
            blocks = list(range(NOWN)) if g == 0 else [NOWN]

            def prep5(j, s):
                P.dma("sp", lambda e: e.dma_start(out=ht[s], in_=H2[j * 128:(j + 1) * 128, :]), writes=[f"ht{s}"])
                P.dma("sp", lambda e: e.dma_start(out=xin[s], in_=XS3[j * 128:(j + 1) * 128, :]), writes=[f"xin{s}"])

            prep5(blocks[0], 0)
            transpose_block(xin[0], "xin0", uT[0], "uT0", 0)
            for i, j in enumerate(blocks):
                s = i % 2
                ffn_mm1(uT[s], f"uT{s}", W1, actT, "f2")
                if i + 1 < len(blocks):
                    prep5(blocks[i + 1], 1 - s)
                ffn_mm2(actT, W2, "f2")
                if i + 1 < len(blocks):
                    transpose_block(xin[1 - s], f"xin{1 - s}", uT[1 - s], f"uT{1 - s}", 0)
                ffn_update(f"ht{s}", ht[s], g3h, "f2")
                rstd_from(ht[s], f"ht{s}", junkD, ss1, rstd1, "fin")
                dve(lambda e, s=s: e.scalar_tensor_tensor(yo[s], ht[s], rstd1, fnr, ALU.mult, ALU.mult), [f"ht{s}", "finrstd", "fnr"], [f"yo{s}"])
                dst = y_p[j * 128:(j + 1) * 128, :] if j < NOWN else y_s
                P.dma("sp", lambda e, s=s, dst=dst: e.dma_start(out=dst, in_=yo[s]), reads=[f"yo{s}"], writes=[uid("y")])
        P.barrier()
        P.emit()
    return nc


_CACHE = {}


def _consts(par):
    c = {}
    c["ident"] = np.eye(128, dtype=np.float32)
    c["iota_p"] = np.arange(128, dtype=np.float32).reshape(128, 1)
    c["iota_j"] = np.ascontiguousarray(np.broadcast_to(np.arange(512, dtype=np.float32)[None, :], (128, 512)))
    jj, ss = np.meshgrid(np.arange(128), np.arange(128), indexing="ij")
    c["ltri"] = (jj >= ss).astype(np.float32)
    am = np.zeros((8, 128, 512), np.float32)
    p = np.arange(128)[:, None]
    t = np.arange(128)[None, :]
    for m in range(8):
        for q in range(4):
            am[m, :, q * 128:(q + 1) * 128] = ((m * 128 + p) < ((2 * q + par) * 128 + t)).astype(np.float32)
    c["amask"] = am
    sm = np.zeros((128, 16, 128), np.float32)
    for cc in range(4):
        for hh in range(2):
            for i in range(8):
                row = 32 * cc + 8 * hh + i
                for n in range(16):
                    sm[row, n, 8 * n:8 * n + i] = 1.0
    c["smask"] = sm
    c["idx_own"] = ((2 * np.arange(16)[None, :] + par) * 128 + np.arange(128)[:, None]).astype(np.int32)
    c["p0p1"] = np.ascontiguousarray(np.broadcast_to(np.array([[1.0 - par, float(par)]], np.float32), (128, 2)))
    return c


def kernel(**inp):
    f = lambda k: np.asarray(inp[k])
    if "nc" not in _CACHE:
        _CACHE["nc"] = build_program()
    nc = _CACHE["nc"]
    x_prompt, x_sample = f("x_prompt"), f("x_sample")
    c_prompt, c_sample = f("c_prompt"), f("c_sample")
    ck = np.ascontiguousarray(f("cache_k")[0].reshape(-1, 512)); cv = np.ascontiguousarray(f("cache_v")[0].reshape(-1, 512))
    page_table = f("page_table").astype(np.int32)
    s_re, s_im = f("state_ssm_re")[0], f("state_ssm_im")[0]

    def st_layout(a):
        return np.ascontiguousarray(a.reshape(16, 16, 2, 64).transpose(2, 3, 1, 0).reshape(128, 16, 16))

    def gp_layout(a):
        return np.ascontiguousarray(a.reshape(16, 2, 64).transpose(1, 2, 0).reshape(128, 16))

    b_re, b_im, c_re, c_im = f("ssm_b_re")[0], f("ssm_b_im")[0], f("ssm_c_re")[0], f("ssm_c_im")[0]

    def bfull(b):
        o = np.zeros((16, 128, 128), np.float32)
        for ti in range(16):
            for g2 in range(2):
                r0 = 32 * (ti % 4) + 16 * g2
                o[ti, r0:r0 + 16, 64 * g2:64 * g2 + 64] = b[2 * ti + g2].T
        return o

    def cfull(cm):
        o = np.zeros((16, 128, 128), np.float32)
        for ti in range(16):
            for g2 in range(2):
                c0 = 32 * (ti % 4) + 16 * g2
                o[ti, 64 * g2:64 * g2 + 64, c0:c0 + 16] = cm[2 * ti + g2].T
        return o

    ld = f("ssm_log_dt")[0]
    shared = {
        "cache_k": ck, "cache_v": cv,
        "ada_w": f("ada_w")[0], "ada_b": f("ada_b")[0].reshape(1, -1),
        "norm_ffn1": f("norm_ffn1")[0].reshape(1, -1), "norm_mix": f("norm_mix")[0].reshape(1, -1),
        "norm_ffn2": f("norm_ffn2")[0].reshape(1, -1), "final_norm": f("final_norm").reshape(1, -1),
        "ffn1_w_in": f("ffn1_w_in")[0], "ffn1_w_out": f("ffn1_w_out")[0],
        "ffn2_w_in": f("ffn2_w_in")[0], "ffn2_w_out": f("ffn2_w_out")[0],
        "w_in": f("w_in")[0], "sb_bias": f("sb_bias")[0].reshape(1, 8),
        "lam_re": gp_layout(f("ssm_lambda_re")[0]), "lam_im": gp_layout(f("ssm_lambda_im")[0]),
        "logdt": np.ascontiguousarray(np.broadcast_to(ld.reshape(16, 2).T[:, None, :], (2, 64, 16)).reshape(128, 16)),
        "bfull_re": bfull(b_re), "bfull_im": bfull(b_im), "cfull_re": cfull(c_re), "cfull_im": cfull(c_im),
        "ssm_d": np.ascontiguousarray(f("ssm_d")[0].reshape(4, 128).T), "glu_w": f("glu_w")[0],
        "glu_b": np.ascontiguousarray(f("glu_b")[0].reshape(4, 128).T),
        "w_branch_a": f("w_branch_a")[0], "w_branch_b": f("w_branch_b")[0], "w_out": f("w_out")[0],
    }
    sbr = np.zeros((128, 1), np.float32)
    for cc in range(4):
        for hh in range(2):
            sbr[32 * cc + 8 * hh:32 * cc + 8 * hh + 8, 0] = f("sb_bias")[0, 2 * cc + hh]
    shared["sb_rows"] = sbr
    shared = {k: np.ascontiguousarray(v, dtype=np.float32) for k, v in shared.items()}
    cst = [_consts(0), _consts(1)]
    in_maps = []
    for core in range(8):
        seq, par = core // 2, core % 2
        m = dict(shared)
        m.update(cst[par])
        m["xp"] = np.ascontiguousarray(x_prompt[seq])
        m["xs"] = np.ascontiguousarray(x_sample[16 * core:16 * core + 16].reshape(128, D))
        m["cpr"] = np.ascontiguousarray(np.repeat(c_prompt[seq:seq + 1], 128, axis=0))
        m["csr"] = np.ascontiguousarray(np.repeat(c_sample[16 * core:16 * core + 16], 8, axis=0))
        m["ptab"] = np.ascontiguousarray(page_table[16 * core:16 * core + 16].reshape(1, 256))
        m["st_re"] = st_layout(s_re[16 * core:16 * core + 16]); m["st_im"] = st_layout(s_im[16 * core:16 * core + 16])
        in_maps.append(m)
    res = run_bass_kernel_spmd(nc, in_maps, core_ids=list(range(8))).results
    B = 4
    y_prompt = np.zeros((B, SEQ, D), np.float32); y_sample = np.zeros((128, 8, D), np.float32)
    nkp = np.zeros((1, B, SEQ, 8, 64), np.float32); nvp = np.zeros_like(nkp)
    srp = np.zeros((1, B, 32, 64), np.float32); sip = np.zeros_like(srp)
    nks = np.zeros((1, 128, 8, 8, 64), np.float32); nvs = np.zeros_like(nks)
    srs = np.zeros((1, 128, 32, 64), np.float32); sis = np.zeros_like(srs)

    def gp_back(a):
        return a.reshape(2, 64, 16).transpose(2, 0, 1).reshape(32, 64)

    def st_back(a):
        return a.reshape(2, 64, 16, 16).transpose(3, 2, 0, 1).reshape(16, 32, 64)

    for core in range(8):
        seq, par = core // 2, core % 2
        r = res[core]
        yp = r["y_p"].reshape(16, 128, D)
        y_prompt[seq].reshape(16, 2, 128, D)[:, par] = yp
        y_sample[16 * core:16 * core + 16] = r["y_s"].reshape(16, 8, D)
        if par == 0:
            nkp[0, seq] = r["kp"].reshape(SEQ, 8, 64); nvp[0, seq] = r["vp"].reshape(SEQ, 8, 64)
            srp[0, seq] = gp_back(r["sp_re"]); sip[0, seq] = gp_back(r["sp_im"])
        nks[0, 16 * core:16 * core + 16] = r["ks"].reshape(16, 8, 8, 64); nvs[0, 16 * core:16 * core + 16] = r["vs"].reshape(16, 8, 8, 64)
        srs[0, 16 * core:16 * core + 16] = st_back(r["ss_re"]); sis[0, 16 * core:16 * core + 16] = st_back(r["ss_im"])
    return (y_prompt, y_sample, nkp, nvp, srp, sip, nks, nvs, srs, sis)
```

```python
import numpy as np
from contextlib import ExitStack
import concourse.bass as bass
import concourse.mybir as mybir
from concourse.bass_utils import run_bass_kernel_spmd

F32 = mybir.dt.float32
BF16 = mybir.dt.bfloat16
I32 = mybir.dt.int32
AF = mybir.ActivationFunctionType
ALU = mybir.AluOpType

D = 1024
DFF = 2816
NF = 22
SEQ = 4096
NBLK = 32
NOWN = 16
PI = 3.14159265358979
TWO_PI = 6.283185307179586


import types


def _snap(fn):
    if fn.__closure__ is None:
        return fn
    cells = []
    for c in fn.__closure__:
        try:
            cells.append(types.CellType(c.cell_contents))
        except ValueError:
            cells.append(c)
    return types.FunctionType(fn.__code__, fn.__globals__, fn.__name__, fn.__defaults__, tuple(cells))


class Prog:
    CE = ("pe", "act", "dve", "pool")
    QE = ("sp", "act", "pool")

    def __init__(self, nc, sems, n_dma_sems=(28, 4, 24)):
        self.nc = nc
        self.streams = {e: [] for e in ("pe", "act", "dve", "pool", "sp")}
        self.count = {e: 0 for e in self.CE}
        self.res = {}
        self.waited = {e: {} for e in self.streams}
        self.sems = {}
        it = iter(sems)
        for e in self.CE:
            self.sems[e] = next(it)
        self.dma_pool = {}
        for q, n in zip(self.QE, n_dma_sems):
            self.dma_pool[q] = [[next(it), 0] for _ in range(n)]
        self.dma_rr = {q: 0 for q in self.QE}
        for q in self.QE:
            for i, sv in enumerate(self.dma_pool[q]):
                self.sems[("d", q, i)] = sv[0]
        self.final_events = {}

    def _deps(self, eng, reads, writes):
        evs = {}

        def add(k, v):
            if v > evs.get(k, 0):
                evs[k] = v

        for r in reads:
            st = self.res.get(r)
            if st and st["w"]:
                add(*st["w"])
        for w in writes:
            st = self.res.get(w)
            if st:
                if st["w"]:
                    add(*st["w"])
                for k, v in st["r"].items():
                    if k == eng:
                        continue
                    add(k, v)
        if eng == "pe":
            evs.pop("pe", None)
        return evs

    def _commit(self, ev, reads, writes):
        k, v = ev
        for r in reads:
            st = self.res.setdefault(r, {"w": None, "r": {}})
            if v > st["r"].get(k, 0):
                st["r"][k] = v
        for w in writes:
            self.res[w] = {"w": ev, "r": {}}

    def _waits(self, eng, evs):
        out = []
        wd = self.waited[eng]
        for k, v in evs.items():
            if wd.get(k, 0) >= v:
                continue
            wd[k] = v
            out.append((k, v))
        return out

    def op(self, eng, fn, reads=(), writes=()):
        fn = _snap(fn)
        evs = self._deps(eng, reads, writes)
        waits = self._waits(eng, evs)
        self.count[eng] += 1
        ev = (eng, self.count[eng])
        self.streams[eng].append(("op", waits, fn, ev))
        self._commit(ev, reads, writes)
        return ev

    def dma(self, q, fn, reads=(), writes=()):
        fn = _snap(fn)
        evs = self._deps(None, reads, writes)
        pool = self.dma_pool[q]
        i = self.dma_rr[q]
        self.dma_rr[q] = (i + 1) % len(pool)
        key = ("d", q, i)
        prev = pool[i][1]
        if prev > 0:
            evs[key] = max(evs.get(key, 0), prev)
        waits = self._waits(q, evs)
        pool[i][1] = prev + 16
        ev = (key, prev + 16)
        self.streams[q].append(("dma", waits, fn, ev))
        self._commit(ev, reads, writes)
        self.final_events[key] = prev + 16
        return ev

    def barrier(self):
        evs = {e: self.count[e] for e in self.CE if self.count[e] > 0}
        evs.update(self.final_events)
        for e in self.streams:
            w = self._waits(e, {k: v for k, v in evs.items() if k != e or e != "pe"})
            if w:
                self.streams[e].append(("wait", w, None, None))
        self.res = {}

    def emit(self):
        nc = self.nc
        handles = {"pe": "tensor", "act": "scalar", "dve": "vector", "pool": "gpsimd", "sp": "sync"}
        with nc.Block() as block:
            for e, hname in handles.items():
                stream = self.streams[e]

                def body(engine, stream=stream, e=e):
                    for kind, waits, fn, ev in stream:
                        for k, v in waits:
                            engine.wait_ge(self.sems[k], v)
                        if kind == "wait":
                            continue
                        ins = fn(engine)
                        if kind == "op":
                            ins.then_inc(self.sems[e], 1)
                        else:
                            ins.then_inc(self.sems[ev[0]], 16)

                getattr(block, hname)(body)


class Arena:
    def __init__(self, big, nbytes):
        self.f = big
        self.b = big.bitcast(BF16)
        self.i = big.bitcast(I32)
        self.n = nbytes
        self.top = 0
        self.marks = []

    def alloc(self, nbytes, dt=F32, shape=None):
        nbytes = (nbytes + 15) // 16 * 16
        o = self.top
        self.top += nbytes
        assert self.top <= self.n, f"SBUF arena overflow {self.top} > {self.n}"
        if dt == F32:
            v = self.f[:, o // 4:(o + nbytes) // 4]
        elif dt == BF16:
            v = self.b[:, o // 2:(o + nbytes) // 2]
        else:
            v = self.i[:, o // 4:(o + nbytes) // 4]
        return v

    def f32(self, *shape):
        n = int(np.prod(shape))
        v = self.alloc(n * 4, F32)[:, 0:n]
        return _shape(v, shape)

    def bf16(self, *shape):
        n = int(np.prod(shape))
        v = self.alloc(n * 2, BF16)[:, 0:n]
        return _shape(v, shape)

    def i32(self, *shape):
        n = int(np.prod(shape))
        v = self.alloc(n * 4, I32)[:, 0:n]
        return _shape(v, shape)

    def mark(self):
        self.marks.append(self.top)

    def release(self):
        self.top = self.marks.pop()


def _shape(v, shape):
    if len(shape) == 1:
        return v
    if len(shape) == 2:
        return v.rearrange("p (a b) -> p a b", a=shape[0])
    if len(shape) == 3:
        return v.rearrange("p (a b c) -> p a b c", a=shape[0], b=shape[1])
    raise ValueError


_uid = [0]


def uid(s):
    _uid[0] += 1
    return f"{s}#{_uid[0]}"


def build_program():
    nc = bass.Bass("TRN2", target_bir_lowering=False)
    di = {}

    def inp(name, shape, dt=F32):
        di[name] = nc.dram_tensor(name, list(shape), dt, kind="ExternalInput").ap()
        return di[name]

    def outp(name, shape, dt=F32):
        di[name] = nc.dram_tensor(name, list(shape), dt, kind="ExternalOutput").ap()
        return di[name]

    def scratch(name, shape, dt=F32):
        di[name] = nc.dram_tensor(name, list(shape), dt, kind="Internal").ap()
        return di[name]

    xp = inp("xp", [SEQ, D]); xs = inp("xs", [128, D])
    cpr = inp("cpr", [128, D]); csr = inp("csr", [128, D])
    cache_k = inp("cache_k", [2560 * 128, 512]); cache_v = inp("cache_v", [2560 * 128, 512])
    ptab = inp("ptab", [1, 256], I32)
    st_re = inp("st_re", [128, 16, 16]); st_im = inp("st_im", [128, 16, 16])
    ada_w = inp("ada_w", [D, 9 * D]); ada_b = inp("ada_b", [1, 9 * D])
    n1 = inp("norm_ffn1", [1, D]); n2 = inp("norm_mix", [1, D]); n3 = inp("norm_ffn2", [1, D]); nfin = inp("final_norm", [1, D])
    f1_in = inp("ffn1_w_in", [D, 2 * DFF]); f1_out = inp("ffn1_w_out", [DFF, D])
    f2_in = inp("ffn2_w_in", [D, 2 * DFF]); f2_out = inp("ffn2_w_out", [DFF, D])
    w_in = inp("w_in", [D, 4096]); sbb_d = inp("sb_bias", [1, 8])
    lam_re = inp("lam_re", [128, 16]); lam_im = inp("lam_im", [128, 16]); logdt = inp("logdt", [128, 16])
    bre_d = inp("bfull_re", [16, 128, 128]); bim_d = inp("bfull_im", [16, 128, 128])
    cre_d = inp("cfull_re", [16, 128, 128]); cim_d = inp("cfull_im", [16, 128, 128])
    dsk_d = inp("ssm_d", [128, 4]); glu_w = inp("glu_w", [512, 512]); glub_d = inp("glu_b", [128, 4])
    wba = inp("w_branch_a", [512, D]); wbb = inp("w_branch_b", [512, D]); w_o = inp("w_out", [D, D])
    ident_d = inp("ident", [128, 128]); iotap_d = inp("iota_p", [128, 1]); iotaj_d = inp("iota_j", [128, 512])
    ltri_d = inp("ltri", [128, 128]); amask_d = inp("amask", [8, 128, 512]); smask_d = inp("smask", [128, 16, 128])
    idxown_d = inp("idx_own", [128, 16], I32); p0p1_d = inp("p0p1", [128, 2]); inp("sb_rows", [128, 1])

    y_p = outp("y_p", [NOWN * 128, D]); y_s = outp("y_s", [128, D])
    kp_o = outp("kp", [SEQ, 512]); vp_o = outp("vp", [SEQ, 512])
    spre_o = outp("sp_re", [128, 16]); spim_o = outp("sp_im", [128, 16])
    ks_o = outp("ks", [128, 512]); vs_o = outp("vs", [128, 512])
    ssre_o = outp("ss_re", [128, 16, 16]); ssim_o = outp("ss_im", [128, 16, 16])

    MOD = scratch("MOD", [2, 128, 9 * D])
    H1 = scratch("H1", [33 * 128, D]); XS2 = scratch("XS2", [33 * 128, D], BF16)
    SGA = scratch("SGA", [17, 128, 1024], BF16); SGB = scratch("SGB", [17, 128, 1024], BF16)
    YT = scratch("YT", [17, 128, 512])
    H2 = scratch("H2", [17 * 128, D]); XS3 = scratch("XS3", [17 * 128, D], BF16)

    with ExitStack() as es:
        sems = [es.enter_context(nc.semaphore(f"s{i}")) for i in range(64)]
        ARENA_BYTES = 207 * 1024
        big = es.enter_context(nc.sbuf_tensor("big", [128, ARENA_BYTES // 4], F32))
        psf = [es.enter_context(nc.psum_tensor(f"ps{i}", [128, 512], F32)) for i in range(8)]
        psb = [p.bitcast(BF16) for p in psf]
        P = Prog(nc, sems)
        A = Arena(big, ARENA_BYTES)

        def PSN(i):
            return f"ps{i}"

        wstage = {"n": 0, "tiles": None, "on": False}
        ident_b = A.bf16(128); ones_b = A.bf16(128); ltri_b = A.bf16(128)
        ones_f = A.f32(128)
        iota_p = A.f32(1); p0p1 = A.f32(2); sbb = A.f32(8); epsb = A.f32(1)
        P.dma("pool", lambda e: e.dma_start(out=ident_b, in_=ident_d), writes=["ident_b"])
        P.dma("pool", lambda e: e.dma_start(out=ltri_b, in_=ltri_d), writes=["ltri_b"])
        P.dma("sp", lambda e: e.dma_start(out=iota_p, in_=iotap_d), writes=["iota_p"])
        P.dma("sp", lambda e: e.dma_start(out=p0p1, in_=p0p1_d), writes=["p0p1"])
        P.dma("sp", lambda e: e.dma_start(out=sbb, in_=sbb_d.partition_broadcast(128)), writes=["sbb"])
        P.op("dve", lambda e: e.memset(ones_b, 1.0), writes=["ones_b"])
        P.op("dve", lambda e: e.memset(ones_f, 1.0), writes=["ones_f"])
        P.op("dve", lambda e: e.memset(epsb, 1e-6), writes=["epsb"])
        CONST = ["ident_b", "ones_b", "ltri_b", "ones_f", "iota_p", "p0p1", "sbb", "epsb"]
        WST_BYTES = 3 * 4096
        wstage["tiles"] = [A.f[:, (ARENA_BYTES - WST_BYTES) // 4 + 1024 * i:(ARENA_BYTES - WST_BYTES) // 4 + 1024 * (i + 1)] for i in range(3)]

        def after_barrier():
            pass


        def rstd_from(x_ap, x_res, junk, ss, rstd, tag):
            P.op("act", lambda e: e.activation(junk, x_ap, AF.Square, accum_out=ss), reads=[x_res], writes=[tag + "junk", tag + "ss"])
            P.op("act", lambda e: e.activation(rstd, ss, AF.Sqrt, scale=1.0 / D, bias=epsb), reads=[tag + "ss"], writes=[tag + "rs0"])
            P.op("dve", lambda e: e.reciprocal(rstd, rstd), reads=[tag + "rs0"], writes=[tag + "rstd"])

        def transpose_block(src_b, src_res, dstT, dst_res, pbank):
            for k in range(8):
                P.op("pe", lambda e, k=k: e.transpose(psb[pbank][:, k * 128:(k + 1) * 128], src_b[:, k * 128:(k + 1) * 128], ident_b),
                     reads=[src_res, "ident_b"], writes=[PSN(pbank)])
            P.op("act", lambda e: e.copy(dstT, psb[pbank][:, 0:1024].rearrange("p (k n) -> p k n", k=8)), reads=[PSN(pbank)], writes=[dst_res])

        def _stage_cast(dst_view, src_view, res, fine):
            if not wstage["on"]:
                P.dma("pool", lambda e: e.dma_start(out=dst_view, in_=src_view), writes=[res])
                return
            i = wstage["n"]; wstage["n"] += 1
            sl = i % len(wstage["tiles"])
            st = wstage["tiles"][sl]
            shp = dst_view.shape
            stv = st[:, 0:shp[1] * shp[2]].rearrange("p (k n) -> p k n", k=shp[1])
            P.dma("sp", lambda e: e.dma_start(out=stv, in_=src_view), writes=[f"wst{sl}"])
            if fine and i % 2 == 1:
                P.op("act", lambda e: e.copy(dst_view, stv), reads=[f"wst{sl}"], writes=[res])
            else:
                P.op("pool", lambda e: e.tensor_copy(dst_view, stv), reads=[f"wst{sl}"], writes=[res])

        def load_w_cols(dst, src, c0, c1, res, step=512, order=None, fine=False):
            if fine:
                step = 128
            starts = order if order is not None else list(range(c0, c1, step))
            for a in starts:
                b = min(a + step, c1)
                _stage_cast(dst[:, :, a - c0:b - c0], src[:, a:b].rearrange("(k p) n -> p k n", p=128), f"{res}:{a - c0}" if fine else res, fine)

        def load_w_rows(dst, src, nchunk, res, step=1, fine=False):
            ncols = dst.shape[2]
            per = max(1, 1024 // ncols) if fine else max(1, 4096 // ncols)
            for a in range(0, nchunk, per):
                b = min(a + per, nchunk)
                _stage_cast(dst[:, a:b, :], src[a * 128:b * 128, :].rearrange("(k p) n -> p k n", p=128), f"{res}:{a}" if fine else res, fine)

        def ffn_mm1(uT, uT_res, W1, actT, tag):
            for f in range(NF):
                pb = 2 + (f % 2)
                for k in range(8):
                    P.op("pe", lambda e, f=f, k=k, pb=pb: e.matmul(psf[pb][:, 0:128], W1[:, k, f * 128:(f + 1) * 128], uT[:, k, :], start=(k == 0), stop=(k == 7)),
                         reads=[uT_res, f"{tag}W1:{f * 128}"], writes=[PSN(pb)])
                for k in range(8):
                    P.op("pe", lambda e, f=f, k=k, pb=pb: e.matmul(psf[pb][:, 128:256], W1[:, k, DFF + f * 128:DFF + (f + 1) * 128], uT[:, k, :], start=(k == 0), stop=(k == 7)),
                         reads=[uT_res, f"{tag}W1:{DFF + f * 128}"], writes=[PSN(pb)])
                sg = sgs[f % 2]
                P.op("act", lambda e, pb=pb, sg=sg: e.activation(sg, psf[pb][:, 0:128], AF.Silu), reads=[PSN(pb)], writes=[f"sg{f % 2}"])
                P.op("dve", lambda e, f=f, pb=pb, sg=sg: e.tensor_tensor(actT[:, f, :], sg, psf[pb][:, 128:256], ALU.mult), reads=[f"sg{f % 2}", PSN(pb)], writes=[f"{tag}actT{f}"])

        def ffn_mm2(actT, W2, tag):
            for half in range(2):
                pb = 4 + half
                for f in range(NF):
                    P.op("pe", lambda e, f=f, half=half, pb=pb: e.matmul(psf[pb][:, 0:512], actT[:, f, :], W2[:, f, half * 512:(half + 1) * 512], start=(f == 0), stop=(f == NF - 1)),
                         reads=[f"{tag}actT{f}", f"{tag}W2:{f}"], writes=[PSN(pb)])

        def ffn_update(hres_in, h_ap, ghalf, tag):
            for half in range(2):
                pb = 4 + half
                hs = h_ap[:, half * 512:(half + 1) * 512]
                tm = tmp512
                P.op("dve", lambda e, half=half, pb=pb, tm=tm: e.tensor_tensor(tm, psf[pb][:, 0:512], ghalf[:, half * 512:(half + 1) * 512], ALU.mult), reads=[PSN(pb), tag + "mod"], writes=["tmp512"])
                P.op("dve", lambda e, hs=hs, tm=tm: e.tensor_tensor(hs, hs, tm, ALU.add), reads=["tmp512", hres_in], writes=[hres_in])

        def norm_mod(x_ap, x_res, Arow, shrow, out_b, out_res, tag):
            rstd_from(x_ap, x_res, junkD, ss1, rstd1, tag)
            P.op("dve", lambda e: e.scalar_tensor_tensor(tmpD, x_ap, rstd1, Arow, ALU.mult, ALU.mult), reads=[x_res, tag + "rstd", tag + "mod"], writes=["tmpD"])
            P.op("dve", lambda e: e.tensor_tensor(out_b, tmpD, shrow, ALU.add), reads=["tmpD", tag + "mod"], writes=[out_res])

        wstage["on"] = False
        A.n = ARENA_BYTES
        A.mark()
        scT = [A.bf16(8, 128), A.bf16(8, 128)]
        ctile = A.f32(D); cb = A.bf16(D)
        for g, src in enumerate((cpr, csr)):
            P.dma("sp", lambda e, src=src: e.dma_start(out=ctile, in_=src), writes=["ctile"])
            P.op("act", lambda e: e.activation(cb, ctile, AF.Silu), reads=["ctile"], writes=["cb"])
            transpose_block(cb, "cb", scT[g], f"scT{g}", 0)
        wA = [A.bf16(8, 512), A.bf16(8, 512)]
        brow = [A.f32(512), A.f32(512)]
        stg = [A.f32(512), A.f32(512)]
        for cg in range(18):
            s = cg % 2
            load_w_cols(wA[s], ada_w, cg * 512, (cg + 1) * 512, f"wA{s}")
            wa_res = [f"wA{s}"]
            P.dma("sp", lambda e, cg=cg, s=s: e.dma_start(out=brow[s][0:1, :], in_=ada_b[0:1, cg * 512:(cg + 1) * 512]), writes=[f"brow{s}"])
            for g in range(2):
                pb = 2 + g
                for k in range(8):
                    P.op("pe", lambda e, k=k, g=g, s=s, pb=pb: e.matmul(psf[pb][:, 0:512], scT[g][:, k, :], wA[s][:, k, :], start=(k == 0), stop=False),
                         reads=[f"scT{g}"] + wa_res, writes=[PSN(pb)])
                P.op("pe", lambda e, s=s, pb=pb: e.matmul(psf[pb][:, 0:512], ones_f[0:1, :], brow[s][0:1, :], start=False, stop=True),
                     reads=["ones_f", f"brow{s}"], writes=[PSN(pb)])
                P.op("act", lambda e, g=g, pb=pb: e.copy(stg[g], psf[pb][:, 0:512]), reads=[PSN(pb)], writes=[f"stg{g}"])
                P.dma("sp", lambda e, g=g, cg=cg: e.dma_start(out=MOD[g, :, cg * 512:(cg + 1) * 512], in_=stg[g]), reads=[f"stg{g}"], writes=[f"MOD{g}"])
        P.barrier()
        A.release()

        def load_mod(dst, g, chunk, res):
            P.dma("sp", lambda e: e.dma_start(out=dst, in_=MOD[g, :, chunk * D:(chunk + 1) * D]), writes=[res])

        def load_row_bcast(dst, src, res):
            P.dma("sp", lambda e: e.dma_start(out=dst, in_=src.partition_broadcast(128)), writes=[res])

        A.mark()
        W1 = A.bf16(8, 2 * DFF); W2 = A.bf16(NF, D)
        load_w_cols(W1, f1_in, 0, 2 * DFF, "f1W1", order=[x for f_ in range(NF) for x in (f_ * 128, DFF + f_ * 128)], fine=True)
        load_w_rows(W2, f1_out, NF, "f1W2", fine=True)
        A1 = A.f32(D); sh1 = A.f32(D); g1h = A.f32(D); A2 = A.f32(D); sh2 = A.f32(D); grow = A.f32(D)
        xt = [A.f32(D), A.f32(D)]
        xsb = [A.bf16(D), A.bf16(D)]; uT = [A.bf16(8, 128), A.bf16(8, 128)]; actT = A.bf16(NF, 128)
        sgs = [A.f32(128), A.f32(128)]
        tmp512 = A.f32(512); tmpD = A.f32(D); junkD = A.bf16(D)
        ss1 = A.f32(1); rstd1 = A.f32(1)
        xs2b = [A.bf16(D), A.bf16(D)]
        for g in range(2):
            load_mod(sh1, g, 0, "f1mod"); load_mod(A1, g, 1, "f1mod"); load_mod(g1h, g, 2, "f1mod")
            load_mod(sh2, g, 3, "f1mod"); load_mod(A2, g, 4, "f1mod")
            load_row_bcast(grow, n1, "grow")
            P.op("dve", lambda e: e.scalar_tensor_tensor(A1, A1, 1.0, grow, ALU.add, ALU.mult), reads=["f1mod", "grow"], writes=["f1mod"])
            P.op("dve", lambda e: e.tensor_scalar(g1h, g1h, 0.5, None, ALU.mult), reads=["f1mod"], writes=["f1mod"])
            load_row_bcast(grow, n2, "grow")
            P.op("dve", lambda e: e.scalar_tensor_tensor(A2, A2, 1.0, grow, ALU.add, ALU.mult), reads=["f1mod", "grow"], writes=["f1mod"])
            blocks = list(range(NBLK)) if g == 0 else [NBLK]

            def prep_norm(b, s):
                src = xp[b * 128:(b + 1) * 128, :] if b < NBLK else xs
                P.dma("sp", lambda e: e.dma_start(out=xt[s], in_=src), writes=[f"x{s}"])
                norm_mod(xt[s], f"x{s}", A1, sh1, xsb[s], f"xsb{s}", "f1")

            def prep_tr(s):
                transpose_block(xsb[s], f"xsb{s}", uT[s], f"uT{s}", 0)

            prep_norm(blocks[0], 0)
            prep_tr(0)
            for i, b in enumerate(blocks):
                s = i % 2
                x = xt[s]; xr = f"x{s}"
                ffn_mm1(uT[s], f"uT{s}", W1, actT, "f1")
                if i + 1 < len(blocks):
                    prep_norm(blocks[i + 1], 1 - s)
                ffn_mm2(actT, W2, "f1")
                if i + 1 < len(blocks):
                    prep_tr(1 - s)
                ffn_update(xr, x, g1h, "f1")
                P.dma("sp", lambda e, x=x, b=b: e.dma_start(out=H1[b * 128:(b + 1) * 128, :], in_=x), reads=[xr], writes=["H1"])
                norm_mod(x, xr, A2, sh2, xs2b[s], f"xs2b{s}", "f1")
                P.dma("sp", lambda e, s=s, b=b: e.dma_start(out=XS2[b * 128:(b + 1) * 128, :], in_=xs2b[s]), reads=[f"xs2b{s}"], writes=["XS2"])
        P.barrier()
        A.release()

        A.mark()
        OT_BYTES = 4 * (NOWN * 128 + 128) * 2
        OT = A.b[:, (ARENA_BYTES - OT_BYTES) // 2:ARENA_BYTES // 2].rearrange("p (a b) -> p a b", a=4)
        QTs = A.bf16(4, 128); KTn = A.bf16(4, 128); VBn = A.f32(512)
        ident_f = A.f32(128)
        P.dma("sp", lambda e: e.dma_start(out=ident_f, in_=ident_d), writes=["ident_f"])
        A.mark()
        KT = A.bf16(4, SEQ)
        VB = A.bf16(NBLK, 512)
        QT = A.bf16(4, NOWN * 128)
        A.mark()
        ST = A.bf16(4, SEQ + 128)
        A.mark()
        WinA = A.bf16(8, 1536)
        load_w_cols(WinA, w_in, 512, 2048, "Win")
        xin = [A.bf16(D), A.bf16(D)]
        u2T = A.bf16(8, 128)
        kf = A.f32(512); vf = A.f32(512)
        idxo = A.i32(16)
        sgat = A.bf16(8, 128); sgbt = A.bf16(8, 128)
        P.dma("sp", lambda e: e.dma_start(out=idxo, in_=idxown_d), writes=["idxo"])
        for b in range(NBLK + 1):
            s = b % 2
            P.dma("sp", lambda e, s=s, b=b: e.dma_start(out=xin[s], in_=XS2[b * 128:(b + 1) * 128, :]), writes=[f"xin{s}"])
            transpose_block(xin[s], f"xin{s}", u2T, "u2T", 0)
            for (c0, dstf, pb, nm) in ((512, kf, 2, "kf"), (1024, vf, 3, "vf")):
                for k in range(8):
                    P.op("pe", lambda e, k=k, c0=c0, pb=pb: e.matmul(psf[pb][:, 0:512], u2T[:, k, :], WinA[:, k, c0 - 512:c0], start=(k == 0), stop=(k == 7)),
                         reads=["u2T", "Win"], writes=[PSN(pb)])
                P.op("act", lambda e, dstf=dstf, pb=pb: e.copy(dstf, psf[pb][:, 0:512]), reads=[PSN(pb)], writes=[nm])
            ko = kp_o[b * 128:(b + 1) * 128, :] if b < NBLK else ks_o
            vo = vp_o[b * 128:(b + 1) * 128, :] if b < NBLK else vs_o
            P.dma("sp", lambda e, ko=ko: e.dma_start(out=ko, in_=kf), reads=["kf"], writes=[uid("ko")])
            P.dma("sp", lambda e, vo=vo: e.dma_start(out=vo, in_=vf), reads=["vf"], writes=[uid("vo")])
            if b < NBLK:
                P.op("pool", lambda e, b=b: e.tensor_copy(VB[:, b, :], vf), reads=["vf"], writes=[f"VB{b}"])
            else:
                P.op("pool", lambda e: e.tensor_copy(VBn, vf), reads=["vf"], writes=["VBn"])
            for (c0, dst, pb, nm) in ((512, KT, 4, "KT"), (1536, ST, 5, "ST")):
                for c in range(4):
                    for k in range(8):
                        P.op("pe", lambda e, k=k, c=c, c0=c0, pb=pb: e.matmul(psf[pb][:, c * 128:(c + 1) * 128], WinA[:, k, c0 - 512 + c * 128:c0 - 512 + (c + 1) * 128], u2T[:, k, :], start=(k == 0), stop=(k == 7)),
                             reads=["u2T", "Win"], writes=[PSN(pb)])
                dd = KTn if (nm == "KT" and b == NBLK) else dst[:, :, b * 128:(b + 1) * 128]
                P.op("act", lambda e, dd=dd, pb=pb: e.copy(dd, psf[pb][:, 0:512].rearrange("p (c n) -> p c n", c=4)),
                     reads=[PSN(pb)], writes=[f"{nm}{b}"])
        P.barrier()
        A.release()
        A.mark()
        WinQ = A.bf16(8, 512); WinG = A.bf16(8, 2048)
        load_w_cols(WinQ, w_in, 0, 512, "Win"); load_w_cols(WinG, w_in, 2048, 4096, "Win")
        xin = [A.bf16(D), A.bf16(D)]
        u2T = A.bf16(8, 128)
        idxo = A.i32(16)
        sgat = A.bf16(8, 128); sgbt = A.bf16(8, 128)
        P.dma("sp", lambda e: e.dma_start(out=idxo, in_=idxown_d), writes=["idxo"])
        for j in range(NOWN + 1):
            s = j % 2
            if j < NOWN:
                P.dma("pool", lambda e, s=s, j=j: e.indirect_dma_start(out=xin[s], out_offset=None, in_=XS2, in_offset=bass.IndirectOffsetOnAxis(ap=idxo[:, j:j + 1], axis=0)),
                      reads=["idxo"], writes=[f"xin{s}"])
            else:
                P.dma("sp", lambda e, s=s: e.dma_start(out=xin[s], in_=XS2[NBLK * 128:(NBLK + 1) * 128, :]), writes=[f"xin{s}"])
            transpose_block(xin[s], f"xin{s}", u2T, "u2T", 0)
            for c in range(4):
                for k in range(8):
                    P.op("pe", lambda e, k=k, c=c: e.matmul(psf[2][:, c * 128:(c + 1) * 128], WinQ[:, k, c * 128:(c + 1) * 128], u2T[:, k, :], start=(k == 0), stop=(k == 7)),
                         reads=["u2T", "Win"], writes=[PSN(2)])
            qd = QT[:, :, j * 128:(j + 1) * 128] if j < NOWN else QTs
            P.op("act", lambda e, qd=qd: e.copy(qd, psf[2][:, 0:512].rearrange("p (c n) -> p c n", c=4)), reads=[PSN(2)], writes=[f"QT{j}"])
            for (c0, dst, scr, nm) in ((2048, sgat, SGA, "sga"), (3072, sgbt, SGB, "sgb")):
                for hh in range(2):
                    pb = 4 + hh
                    for c in range(4):
                        cc = hh * 4 + c
                        for k in range(8):
                            P.op("pe", lambda e, k=k, c=c, cc=cc, c0=c0, pb=pb: e.matmul(psf[pb][:, c * 128:(c + 1) * 128], WinG[:, k, c0 - 2048 + cc * 128:c0 - 2048 + (cc + 1) * 128], u2T[:, k, :], start=(k == 0), stop=(k == 7)),
                                 reads=["u2T", "Win"], writes=[PSN(pb)])
                    P.op("act", lambda e, dst=dst, hh=hh, pb=pb: e.activation(dst[:, hh * 4:(hh + 1) * 4, :], psf[pb][:, 0:512].rearrange("p (c n) -> p c n", c=4), AF.Sigmoid),
                         reads=[PSN(pb)], writes=[nm])
                P.dma("sp", lambda e, dst=dst, scr=scr, j=j: e.dma_start(out=scr[j].rearrange("p (c n) -> p c n", c=8), in_=dst), reads=[nm], writes=[uid("sg")])
        P.barrier()
        A.release()

        wstage["on"] = False
        A.n = ARENA_BYTES
        A.mark()
        def pt16():
            return A.f32(16)
        lr = pt16(); li = pt16(); ldt = pt16(); dt_ = pt16(); rr = pt16(); th = pt16(); cth = pt16(); sth = pt16()
        abre = pt16(); abim = pt16(); nabim = pt16(); zr = pt16(); zi = pt16(); den = pt16(); q1 = pt16(); q2 = pt16(); nr = pt16()
        Fr = pt16(); Fi = pt16(); a512 = pt16(); init_re = pt16(); init_im = pt16(); fin_re = pt16(); fin_im = pt16()
        dsk = A.f32(4)
        Bre = A.bf16(16, 128); Bim = A.bf16(16, 128); Cre = A.bf16(16, 128); Cim = A.bf16(16, 128)
        iotaj = A.f32(512)
        tabs = [[A.f32(512) for _ in range(4)] for _ in range(4)]
        rt = [A.f32(512) for _ in range(4)]
        wi_re = A.f32(512); wi_im = A.f32(512); w_re = A.f32(512); w_im = A.f32(512); u1 = A.f32(512); u2 = A.f32(512)
        xre = A.bf16(4, 512); ximn = A.bf16(4, 512)
        ysb = wi_re; yown = wi_im[:, 0:256].rearrange("p (a n) -> p a n", a=2)
        _o = A.top; ki = A.i32(512); p1 = A.f[:, _o // 4:_o // 4 + 512]; kfl = u2; ang = u1; ang2 = wi_re
        sre0 = A.f32(16, 16); sim0 = A.f32(16, 16); sso_re = A.f32(16, 16); sso_im = A.f32(16, 16)
        bzr = A.f32(128); bzi = A.f32(128); xsr = w_re[:, 0:128]; xsi = w_im[:, 0:128]; st4 = [A.f32(16) for _ in range(4)]
        tq = A.f32(16); tq2 = A.f32(16)
        for dst, src, nm in ((lr, lam_re, "lr"), (li, lam_im, "li"), (ldt, logdt, "ldt"), (dsk, dsk_d, "dsk"), (iotaj, iotaj_d, "iotaj"),
                             (sre0, st_re, "sre0"), (sim0, st_im, "sim0")):
            P.dma("sp", lambda e, dst=dst, src=src: e.dma_start(out=dst, in_=src), writes=[nm])
        for dst, src, nm in ((Bre, bre_d, "Bre"), (Bim, bim_d, "Bim"), (Cre, cre_d, "Cre"), (Cim, cim_d, "Cim")):
            for hlf in range(2):
                P.dma("pool", lambda e, dst=dst, src=src, hlf=hlf: e.dma_start(out=dst[:, hlf * 8:(hlf + 1) * 8, :], in_=src[hlf * 8:(hlf + 1) * 8].rearrange("t p n -> p t n")), writes=[nm])

        def dve(fn, reads, writes):
            P.op("dve", fn, reads=reads, writes=writes)

        dve(lambda e: e.tensor_scalar(Cim, Cim, -1.0, None, ALU.mult), ["Cim"], ["Cim"])

        def sin_of(a_ap, a_res, out, out_res, n, shift=0.0):
            kiv, kfv, av = ki[:, 0:n], kfl[:, 0:n], ang2[:, 0:n]
            dve(lambda e: e.tensor_scalar(av, a_ap, shift, None, ALU.add), [a_res], ["wi_re"])
            dve(lambda e: e.tensor_scalar(kiv, av, 1.0 / TWO_PI, None, ALU.mult), ["wi_re"], ["ki"])
            dve(lambda e: e.tensor_copy(kfv, kiv), ["ki"], ["u2"])
            dve(lambda e: e.scalar_tensor_tensor(av, kfv, -TWO_PI, av, ALU.mult, ALU.add), ["u2", "wi_re"], ["wi_re"])
            dve(lambda e: e.tensor_scalar(av, av, -PI, PI, ALU.max, ALU.min), ["wi_re"], ["wi_re"])
            P.op("act", lambda e: e.activation(out, av, AF.Sin), reads=["wi_re"], writes=[out_res])

        P.op("act", lambda e: e.activation(dt_, ldt, AF.Exp), reads=["ldt"], writes=["dt"])
        dve(lambda e: e.tensor_tensor(q1, lr, dt_, ALU.mult), ["lr", "dt"], ["q1"])
        P.op("act", lambda e: e.activation(rr, q1, AF.Exp), reads=["q1"], writes=["rr"])
        dve(lambda e: e.tensor_tensor(th, li, dt_, ALU.mult), ["li", "dt"], ["th"])
        dve(lambda e: e.tensor_scalar(ki[:, 0:16], th, 1.0 / TWO_PI, None, ALU.mult), ["th"], ["ki"])
        dve(lambda e: e.tensor_copy(kfl[:, 0:16], ki[:, 0:16]), ["ki"], ["u2"])
        dve(lambda e: e.scalar_tensor_tensor(th, kfl[:, 0:16], -TWO_PI, th, ALU.mult, ALU.add), ["u2", "th"], ["th"])
        sin_of(th, "th", sth, "sth", 16)
        sin_of(th, "th", cth, "cth", 16, shift=PI / 2)
        dve(lambda e: e.tensor_tensor(abre, rr, cth, ALU.mult), ["rr", "cth"], ["abre"])
        dve(lambda e: e.tensor_tensor(abim, rr, sth, ALU.mult), ["rr", "sth"], ["abim"])
        dve(lambda e: e.tensor_scalar(nabim, abim, -1.0, None, ALU.mult), ["abim"], ["nabim"])
        dve(lambda e: e.tensor_tensor(den, lr, lr, ALU.mult), ["lr"], ["den"])
        dve(lambda e: e.tensor_tensor(q2, li, li, ALU.mult), ["li"], ["q2"])
        dve(lambda e: e.tensor_tensor(den, den, q2, ALU.add), ["den", "q2"], ["den"])
        dve(lambda e: e.reciprocal(den, den), ["den"], ["den"])
        dve(lambda e: e.tensor_scalar(nr, abre, -1.0, None, ALU.add), ["abre"], ["nr"])
        dve(lambda e: e.tensor_tensor(q1, nr, lr, ALU.mult), ["nr", "lr"], ["q1"])
        dve(lambda e: e.tensor_tensor(q2, abim, li, ALU.mult), ["abim", "li"], ["q2"])
        dve(lambda e: e.tensor_tensor(q1, q1, q2, ALU.add), ["q1", "q2"], ["q1"])
        dve(lambda e: e.tensor_tensor(zr, q1, den, ALU.mult), ["q1", "den"], ["zr"])
        dve(lambda e: e.tensor_tensor(q1, abim, lr, ALU.mult), ["abim", "lr"], ["q1"])
        dve(lambda e: e.tensor_tensor(q2, nr, li, ALU.mult), ["nr", "li"], ["q2"])
        dve(lambda e: e.tensor_tensor(q1, q1, q2, ALU.subtract), ["q1", "q2"], ["q1"])
        dve(lambda e: e.tensor_tensor(zi, q1, den, ALU.mult), ["q1", "den"], ["zi"])
        dve(lambda e: e.tensor_scalar(a512, th, 512.0, None, ALU.mult), ["th"], ["a512"])
        sin_of(a512, "a512", Fi, "Fi", 16)
        sin_of(a512, "a512", Fr, "Fr", 16, shift=PI / 2)

        for cc in range(4):
            for tl in range(4):
                ti = cc * 4 + tl
                c_t, s_t, er_t, ei_t = tabs[tl]
                tr = f"tab{tl}"
                dve(lambda e, ti=ti: e.tensor_scalar(ang, iotaj, th[:, ti:ti + 1], None, ALU.mult), ["iotaj", "th"], ["u1"])
                sin_of(ang, "u1", s_t, tr + "s", 512)
                sin_of(ang, "u1", c_t, tr + "c", 512, shift=PI / 2)
                dve(lambda e, ti=ti, c_t=c_t: e.tensor_scalar(u1, c_t, zr[:, ti:ti + 1], None, ALU.mult), [tr + "c", "zr"], ["u1"])
                dve(lambda e, ti=ti, s_t=s_t, er_t=er_t: e.scalar_tensor_tensor(er_t, s_t, zi[:, ti:ti + 1], u1, ALU.mult, ALU.add), [tr + "s", "zi", "u1"], [tr + "er"])
                dve(lambda e, ti=ti, s_t=s_t: e.tensor_scalar(u1, s_t, zr[:, ti:ti + 1], None, ALU.mult), [tr + "s", "zr"], ["u1"])
                dve(lambda e, ti=ti, c_t=c_t, ei_t=ei_t: e.scalar_tensor_tensor(ei_t, c_t, zi[:, ti:ti + 1], u1, ALU.mult, ALU.subtract), [tr + "c", "zi", "u1"], [tr + "ei"])
                dve(lambda e, ti=ti, tl=tl: e.tensor_scalar(rt[tl], ones_f.to_broadcast([128, 512]) if False else iotaj, 0.0, rr[:, ti:ti + 1], ALU.mult, ALU.add), ["iotaj", "rr"], [f"rt{tl}"])
                dve(lambda e, ti=ti: e.memset(init_re[:, ti:ti + 1], 0.0), [], [f"ire{ti}"])
                dve(lambda e, ti=ti: e.memset(init_im[:, ti:ti + 1], 0.0), [], [f"iim{ti}"])
            for c8 in range(8):
                t0 = c8 * 512
                for tl in range(4):
                    ti = cc * 4 + tl
                    c_t, s_t, er_t, ei_t = tabs[tl]
                    tr = f"tab{tl}"
                    P.op("pe", lambda e, ti=ti, t0=t0, cc=cc: e.matmul(psf[2][:, 0:512], Bre[:, ti, :], ST[:, cc, t0:t0 + 512], start=True, stop=True), reads=["Bre"], writes=[PSN(2)])
                    P.op("pe", lambda e, ti=ti, t0=t0, cc=cc: e.matmul(psf[3][:, 0:512], Bim[:, ti, :], ST[:, cc, t0:t0 + 512], start=True, stop=True), reads=["Bim"], writes=[PSN(3)])
                    dve(lambda e, er_t=er_t: e.tensor_tensor(u1, psf[2][:, 0:512], er_t, ALU.mult), [PSN(2), tr + "er"], ["u1"])
                    dve(lambda e, ei_t=ei_t: e.tensor_tensor(u2, psf[3][:, 0:512], ei_t, ALU.mult), [PSN(3), tr + "ei"], ["u2"])
                    dve(lambda e: e.tensor_tensor(wi_re, u1, u2, ALU.subtract), ["u1", "u2"], ["wi_re"])
                    dve(lambda e, er_t=er_t: e.tensor_tensor(u1, psf[3][:, 0:512], er_t, ALU.mult), [PSN(3), tr + "er"], ["u1"])
                    dve(lambda e, ei_t=ei_t: e.tensor_tensor(u2, psf[2][:, 0:512], ei_t, ALU.mult), [PSN(2), tr + "ei"], ["u2"])
                    dve(lambda e: e.tensor_tensor(wi_im, u1, u2, ALU.add), ["u1", "u2"], ["wi_im"])
                    dve(lambda e, ti=ti, tl=tl: e.tensor_tensor_scan(w_re, rt[tl], wi_re, init_re[:, ti:ti + 1], ALU.mult, ALU.add), [f"rt{tl}", "wi_re", f"ire{ti}"], ["w_re"])
                    dve(lambda e, ti=ti, tl=tl: e.tensor_tensor_scan(w_im, rt[tl], wi_im, init_im[:, ti:ti + 1], ALU.mult, ALU.add), [f"rt{tl}", "wi_im", f"iim{ti}"], ["w_im"])
                    if c8 < 7:
                        dve(lambda e, ti=ti: e.tensor_scalar(tq[:, 0:1], w_im[:, 511:512], Fi[:, ti:ti + 1], None, ALU.mult), ["w_im", "Fi"], ["tq"])
                        dve(lambda e, ti=ti: e.scalar_tensor_tensor(init_re[:, ti:ti + 1], w_re[:, 511:512], Fr[:, ti:ti + 1], tq[:, 0:1], ALU.mult, ALU.subtract), ["w_re", "Fr", "tq"], [f"ire{ti}"])
                        dve(lambda e, ti=ti: e.tensor_scalar(tq[:, 0:1], w_re[:, 511:512], Fi[:, ti:ti + 1], None, ALU.mult), ["w_re", "Fi"], ["tq"])
                        dve(lambda e, ti=ti: e.scalar_tensor_tensor(init_im[:, ti:ti + 1], w_im[:, 511:512], Fr[:, ti:ti + 1], tq[:, 0:1], ALU.mult, ALU.add), ["w_im", "Fr", "tq"], [f"iim{ti}"])
                    dve(lambda e, c_t=c_t: e.tensor_tensor(u1, c_t, w_re, ALU.mult), [tr + "c", "w_re"], ["u1"])
                    dve(lambda e, s_t=s_t: e.tensor_tensor(u2, s_t, w_im, ALU.mult), [tr + "s", "w_im"], ["u2"])
                    dve(lambda e, tl=tl: e.tensor_tensor(xre[:, tl, :], u1, u2, ALU.subtract), ["u1", "u2"], [f"xre{tl}"])
                    if c8 == 7:
                        dve(lambda e, ti=ti: e.tensor_tensor(fin_re[:, ti:ti + 1], u1[:, 511:512], u2[:, 511:512], ALU.subtract), ["u1", "u2"], ["fin_re"])
                    P.op("pool", lambda e, s_t=s_t: e.tensor_tensor(p1, s_t, w_re, ALU.mult), reads=[tr + "s", "w_re"], writes=["ki"])
                    P.op("pool", lambda e, c_t=c_t, tl=tl: e.tensor_tensor(ximn[:, tl, :], c_t, w_im, ALU.mult), reads=[tr + "c", "w_im"], writes=[f"ximn{tl}"])
                    P.op("pool", lambda e, tl=tl: e.tensor_tensor(ximn[:, tl, :], ximn[:, tl, :], p1, ALU.add), reads=["ki", f"ximn{tl}"], writes=[f"ximn{tl}"])
                    if c8 == 7:
                        dve(lambda e, ti=ti, s_t=s_t: e.tensor_tensor(tq[:, 0:1], s_t[:, 511:512], w_re[:, 511:512], ALU.mult), [tr + "s", "w_re"], ["tq"])
                        dve(lambda e, ti=ti, c_t=c_t: e.scalar_tensor_tensor(fin_im[:, ti:ti + 1], c_t[:, 511:512], w_im[:, 511:512], tq[:, 0:1], ALU.mult, ALU.add), [tr + "c", "w_im", "tq"], ["fin_im"])
                for tl in range(4):
                    ti = cc * 4 + tl
                    P.op("pe", lambda e, ti=ti, tl=tl: e.matmul(psf[4][:, 0:512], Cre[:, ti, :], xre[:, tl, :], start=(tl == 0), stop=False), reads=["Cre", f"xre{tl}"], writes=[PSN(4)])
                    P.op("pe", lambda e, ti=ti, tl=tl: e.matmul(psf[4][:, 0:512], Cim[:, ti, :], ximn[:, tl, :], start=False, stop=(tl == 3)), reads=["Cim", f"ximn{tl}"], writes=[PSN(4)])
                dve(lambda e, cc=cc, t0=t0: e.scalar_tensor_tensor(ysb, ST[:, cc, t0:t0 + 512], dsk[:, cc:cc + 1], psf[4][:, 0:512], ALU.mult, ALU.add), ["dsk", PSN(4)], ["wi_re"])
                yv = ysb.rearrange("p (a b n) -> p a b n", a=2, b=2)
                dve(lambda e, yv=yv: e.tensor_scalar(u1[:, 0:256].rearrange("p (a n) -> p a n", a=2), yv[:, :, 0, :], p0p1[:, 0:1], None, ALU.mult), ["wi_re"], ["u1"])
                dve(lambda e, yv=yv: e.scalar_tensor_tensor(yown, yv[:, :, 1, :], p0p1[:, 1:2], u1[:, 0:256].rearrange("p (a n) -> p a n", a=2), ALU.mult, ALU.add), ["wi_re", "u1"], ["wi_im"])
                for jj in range(2):
                    P.dma("sp", lambda e, jj=jj, c8=c8, cc=cc: e.dma_start(out=YT[2 * c8 + jj, :, cc * 128:(cc + 1) * 128], in_=yown[:, jj, :]), reads=["wi_im"], writes=[uid("YT")])
            for tl in range(4):
                ti = cc * 4 + tl
                P.op("pe", lambda e, ti=ti, cc=cc: e.matmul(psf[2][:, 0:128], Bre[:, ti, :], ST[:, cc, SEQ:SEQ + 128], start=True, stop=True), reads=["Bre"], writes=[PSN(2)])
                P.op("pe", lambda e, ti=ti, cc=cc: e.matmul(psf[3][:, 0:128], Bim[:, ti, :], ST[:, cc, SEQ:SEQ + 128], start=True, stop=True), reads=["Bim"], writes=[PSN(3)])
                dve(lambda e, ti=ti: e.tensor_scalar(u1[:, 0:128], psf[3][:, 0:128], zi[:, ti:ti + 1], None, ALU.mult), [PSN(3), "zi"], ["u1"])
                dve(lambda e, ti=ti: e.scalar_tensor_tensor(bzr, psf[2][:, 0:128], zr[:, ti:ti + 1], u1[:, 0:128], ALU.mult, ALU.subtract), [PSN(2), "zr", "u1"], ["bzr"])
                dve(lambda e, ti=ti: e.tensor_scalar(u1[:, 0:128], psf[2][:, 0:128], zi[:, ti:ti + 1], None, ALU.mult), [PSN(2), "zi"], ["u1"])
                dve(lambda e, ti=ti: e.scalar_tensor_tensor(bzi, psf[3][:, 0:128], zr[:, ti:ti + 1], u1[:, 0:128], ALU.mult, ALU.add), [PSN(3), "zr", "u1"], ["bzi"])
                bzr_v = bzr.rearrange("p (n i) -> p n i", i=8); bzi_v = bzi.rearrange("p (n i) -> p n i", i=8)
                xsr_v = xsr.rearrange("p (n i) -> p n i", i=8); xsi_v = xsi.rearrange("p (n i) -> p n i", i=8)
                for i in range(8):
                    pre = sre0[:, ti, :] if i == 0 else xsr_v[:, :, i - 1]
                    pim = sim0[:, ti, :] if i == 0 else xsi_v[:, :, i - 1]
                    rd = ["sre0", "sim0", "w_re", "w_im", "bzr", "bzi", "abre", "abim", "nabim"]
                    dve(lambda e, ti=ti, i=i, pim=pim: e.scalar_tensor_tensor(tq, pim, nabim[:, ti:ti + 1], bzr_v[:, :, i], ALU.mult, ALU.add), rd, ["tq"])
                    dve(lambda e, ti=ti, i=i, pre=pre: e.scalar_tensor_tensor(tq2, pre, abim[:, ti:ti + 1], bzi_v[:, :, i], ALU.mult, ALU.add), rd, ["tq2"])
                    dve(lambda e, ti=ti, i=i, pre=pre: e.scalar_tensor_tensor(xsr_v[:, :, i], pre, abre[:, ti:ti + 1], tq, ALU.mult, ALU.add), rd + ["tq"], ["w_re"])
                    dve(lambda e, ti=ti, i=i, pim=pim: e.scalar_tensor_tensor(xsi_v[:, :, i], pim, abre[:, ti:ti + 1], tq2, ALU.mult, ALU.add), rd + ["tq2"], ["w_im"])
                dve(lambda e, ti=ti: e.tensor_copy(sso_re[:, ti, :], xsr_v[:, :, 7]), ["w_re"], ["sso_re"])
                dve(lambda e, ti=ti: e.tensor_copy(sso_im[:, ti, :], xsi_v[:, :, 7]), ["w_im"], ["sso_im"])
                dve(lambda e, tl=tl: e.tensor_copy(xre[:, tl, 0:128], xsr), ["w_re"], [f"xre{tl}"])
                dve(lambda e, tl=tl: e.tensor_copy(ximn[:, tl, 0:128], xsi), ["w_im"], [f"ximn{tl}"])
            for tl in range(4):
                ti = cc * 4 + tl
                P.op("pe", lambda e, ti=ti, tl=tl: e.matmul(psf[4][:, 0:128], Cre[:, ti, :], xre[:, tl, 0:128], start=(tl == 0), stop=False), reads=["Cre", f"xre{tl}"], writes=[PSN(4)])
                P.op("pe", lambda e, ti=ti, tl=tl: e.matmul(psf[4][:, 0:128], Cim[:, ti, :], ximn[:, tl, 0:128], start=False, stop=(tl == 3)), reads=["Cim", f"ximn{tl}"], writes=[PSN(4)])
            dve(lambda e, cc=cc: e.scalar_tensor_tensor(ysb[:, 0:128], ST[:, cc, SEQ:SEQ + 128], dsk[:, cc:cc + 1], psf[4][:, 0:128], ALU.mult, ALU.add), ["dsk", PSN(4)], ["wi_re"])
            P.dma("sp", lambda e, cc=cc: e.dma_start(out=YT[16, :, cc * 128:(cc + 1) * 128], in_=ysb[:, 0:128]), reads=["wi_re"], writes=[uid("YT")])
        P.dma("sp", lambda e: e.dma_start(out=spre_o, in_=fin_re), reads=["fin_re"], writes=[uid("o")])
        P.dma("sp", lambda e: e.dma_start(out=spim_o, in_=fin_im), reads=["fin_im"], writes=[uid("o")])
        P.dma("sp", lambda e: e.dma_start(out=ssre_o, in_=sso_re), reads=["sso_re"], writes=[uid("o")])
        P.dma("sp", lambda e: e.dma_start(out=ssim_o, in_=sso_im), reads=["sso_im"], writes=[uid("o")])
        P.barrier()
        A.release()
        A.release()

        A.n = ARENA_BYTES - OT_BYTES
        A.mark()
        amask = A.bf16(8, 512)
        for m in range(8):
            P.dma("pool", lambda e, m=m: e.dma_start(out=amask[:, m, :], in_=amask_d[m]), writes=["amask"])
        HC = []
        for hi in range(2):
            HC.append(dict(E=[A.f32(512) for _ in range(3)], SP=[A.bf16(512), A.bf16(512)], SPf=A.f32(512), R=A.bf16(512),
                           T=[A.f32(512), A.f32(512)], W=[A.bf16(512), A.bf16(512)], Wf=A.f32(512), psS=hi, psC=[2 + 2 * hi, 3 + 2 * hi], hi=hi))

        def stA(hc, g, h, kb, k):
            hi = hc["hi"]
            c, hb = h // 2, (h % 2) * 64
            ps_s = hc["psS"]
            E = hc["E"][k % 3]; SP = hc["SP"][k % 2]; SPf = hc["SPf"]
            en = f"E{hi}{k % 3}"; spn = f"SP{hi}{k % 2}"
            P.op("pe", lambda e: e.matmul(psf[ps_s][:, 0:512], KT[hb:hb + 64, c, kb * 128:(kb + 1) * 128], QT[hb:hb + 64, c, g * 512:(g + 1) * 512], start=True, stop=True),
                 reads=[], writes=[PSN(ps_s)])
            P.op("act", lambda e: e.activation(E, psf[ps_s][:, 0:512], AF.Exp, scale=0.125, bias=sbb[:, h:h + 1]), reads=[PSN(ps_s)], writes=[en])
            if kb >= 8 * g:
                m = kb - 8 * g
                P.op("act", lambda e: e.activation(SPf, E, AF.Ln, bias=1.0), reads=[en], writes=[f"SPf{hi}"])
                P.op("dve", lambda e: e.tensor_tensor(SP, SPf, amask[:, m, :], ALU.mult), reads=[f"SPf{hi}", "amask"], writes=[spn])
            else:
                P.op("act", lambda e: e.activation(SP, E, AF.Ln, bias=1.0), reads=[en], writes=[spn])

        def stB(hc, k, first, last):
            hi = hc["hi"]
            ps_c = hc["psC"][k % 2]
            SP = hc["SP"][k % 2]; R = hc["R"]; spn = f"SP{hi}{k % 2}"
            P.op("pe", lambda e: e.matmul(psf[ps_c][:, 0:512], ltri_b, SP, start=True, stop=first), reads=[spn], writes=[PSN(ps_c)])
            if not first:
                P.op("pe", lambda e: e.matmul(psf[ps_c][:, 0:512], ones_b, R, start=False, stop=True), reads=[f"R{hi}"], writes=[PSN(ps_c)])
            if not last:
                if first:
                    P.op("pool", lambda e: e.tensor_copy(R, SP), reads=[spn], writes=[f"R{hi}"])
                else:
                    P.op("pool", lambda e: e.tensor_tensor(R, R, SP, ALU.add), reads=[spn, f"R{hi}"], writes=[f"R{hi}"])

        def stC(hc, g, h, kb, k):
            hi = hc["hi"]
            ps_c = hc["psC"][k % 2]
            E = hc["E"][k % 3]; T = hc["T"][k % 2]; W = hc["W"][k % 2]; Wf = hc["Wf"]
            en = f"E{hi}{k % 3}"; tn = f"T{hi}{k % 2}"; wn = f"W{hi}{k % 2}"
            P.op("act", lambda e: e.activation(T, psf[ps_c][:, 0:512], AF.Exp, scale=-1.0), reads=[PSN(ps_c)], writes=[tn])
            if kb >= 8 * g:
                m = kb - 8 * g
                P.op("dve", lambda e: e.tensor_tensor(Wf, E, T, ALU.mult), reads=[en, tn], writes=[f"Wf{hi}"])
                P.op("dve", lambda e: e.tensor_tensor(W, Wf, amask[:, m, :], ALU.mult), reads=[f"Wf{hi}", "amask"], writes=[wn])
            else:
                P.op("dve", lambda e: e.tensor_tensor(W, E, T, ALU.mult), reads=[en, tn], writes=[wn])

        def stD(hc, h, kb, k, first, last):
            hi = hc["hi"]
            hb = (h % 2) * 64
            W = hc["W"][k % 2]; wn = f"W{hi}{k % 2}"
            P.op("pe", lambda e: e.matmul(psf[6][hb:hb + 64, 0:512], VB[:, kb, h * 64:(h + 1) * 64], W, start=first, stop=last),
                 reads=[wn], writes=[f"psO{hi}"])

        for g in range(4):
            nkb = 8 * (g + 1)
            kbs = list(range(nkb - 1, -1, -1))
            n = len(kbs)
            for hp in range(4):
                hs = [2 * hp, 2 * hp + 1]
                for i in range(n + 2):
                    if 0 <= i - 2 < n:
                        for hi in range(2):
                            stC(HC[hi], g, hs[hi], kbs[i - 2], i - 2)
                    if i < n:
                        for hi in range(2):
                            stA(HC[hi], g, hs[hi], kbs[i], i)
                    if 0 <= i - 1 < n:
                        for hi in range(2):
                            stB(HC[hi], i - 1, i - 1 == 0, i - 1 == n - 1)
                    if 0 <= i - 2 < n:
                        for hi in range(2):
                            stD(HC[hi], hs[hi], kbs[i - 2], i - 2, i - 2 == 0, i - 2 == n - 1)
                for hi in range(2):
                    h = hs[hi]; c, hb = h // 2, (h % 2) * 64
                    P.op("act", lambda e, c=c, hb=hb, g=g: e.copy(OT[hb:hb + 64, c, g * 512:(g + 1) * 512], psf[6][hb:hb + 64, 0:512]), reads=[f"psO{hi}"], writes=[uid("OT")])
        P.barrier()
        A.release()
        A.release()

        A.mark()
        sbrow = A.f32(1); smask = A.f32(16, 128)
        pti = A.i32(256); ptf = A.f32(256); idx = A.i32(256)
        Qbd = A.bf16(4, 16, 16)
        NR = 4
        Kf = A.f32(16, 512); Vf = A.f32(16, 512); Kb = A.bf16(NR, 512); Vb = A.bf16(16, 512)
        KTs = A.bf16(4, 2176)
        NK = 2176
        Es = A.f32(NK); SPs = A.f32(NK); Fs = A.f32(NK); zer = A.f32(NK); Wsb = A.bf16(NK)
        WT = A.bf16(17, 128)
        ntot = A.f32(1); VBnb = A.bf16(512)
        P.dma("sp", lambda e: e.dma_start(out=sbrow, in_=di["sb_rows"]), writes=["sbrow"])
        P.dma("sp", lambda e: e.dma_start(out=smask, in_=smask_d), writes=["smask"])
        P.dma("sp", lambda e: e.dma_start(out=pti, in_=ptab.partition_broadcast(128)), writes=["pti"])
        dve(lambda e: e.tensor_copy(ptf, pti), ["pti"], ["ptf"])
        dve(lambda e: e.tensor_scalar(ptf, ptf, 128.0, iota_p[:, 0:1], ALU.mult, ALU.add), ["ptf"], ["ptf"])
        dve(lambda e: e.tensor_copy(idx, ptf), ["ptf"], ["idx"])
        dve(lambda e: e.memset(zer, 0.0), [], ["zer"])
        dve(lambda e: e.memset(Qbd, 0.0), [], ["Qbd"])
        dve(lambda e: e.tensor_copy(VBnb, VBn), [], ["VBnb"])
        for c in range(4):
            for hh in range(2):
                dve(lambda e, c=c, hh=hh: e.tensor_copy(Qbd[hh * 64:(hh + 1) * 64, c, :, hh * 8:(hh + 1) * 8],
                                                        QTs[hh * 64:(hh + 1) * 64, c, :].rearrange("p (n i) -> p n i", i=8)), ["Qbd"], ["Qbd"])
        dve(lambda e: e.tensor_copy(KTs[:, :, 2048:2176], KTn), [], ["KTnew"])
        gcount = {"k": 0, "v": 0}

        def k_gather(n):
            for pg in range(16):
                col = n * 16 + pg
                sl = pg
                P.dma("pool", lambda e, sl=sl, col=col: e.indirect_dma_start(out=Kf[:, sl, :], out_offset=None, in_=cache_k, in_offset=bass.IndirectOffsetOnAxis(ap=idx[:, col:col + 1], axis=0)),
                      reads=["idx"], writes=[f"Kf{sl}"])

        def v_gather(n):
            for pg in range(16):
                col = n * 16 + pg
                sl = pg
                P.dma("pool", lambda e, sl=sl, col=col: e.indirect_dma_start(out=Vf[:, sl, :], out_offset=None, in_=cache_v, in_offset=bass.IndirectOffsetOnAxis(ap=idx[:, col:col + 1], axis=0)),
                      reads=["idx"], writes=[f"Vf{sl}"])

        def k_process(n):
            for pg in range(16):
                sl = pg % NR
                dve(lambda e, sl=sl, pg=pg: e.tensor_copy(Kb[:, sl, :], Kf[:, pg, :]), [f"Kf{pg}"], [f"Kb{sl}"])
                for c in range(4):
                    P.op("pe", lambda e, sl=sl, c=c: e.transpose(psb[7][:, c * 128:(c + 1) * 128], Kb[:, sl, c * 128:(c + 1) * 128], ident_b), reads=[f"Kb{sl}"], writes=[PSN(7)])
                P.op("act", lambda e, pg=pg: e.copy(KTs[:, :, pg * 128:(pg + 1) * 128], psb[7][:, 0:512].rearrange("p (c n) -> p c n", c=4)), reads=[PSN(7)], writes=[f"KTs{pg}"])

        def v_process(n):
            for pg in range(16):
                P.op("act", lambda e, pg=pg: e.copy(Vb[:, pg, :], Vf[:, pg, :]), reads=[f"Vf{pg}"], writes=[f"Vb{pg}"])

        ktr = [f"KTs{pg}" for pg in range(16)] + ["KTnew"]
        k_gather(0); v_gather(0)
        k_process(0); v_process(0)
        for n in range(16):
            for cb in range(5):
                w = 512 if cb < 4 else 128
                for c in range(4):
                    P.op("pe", lambda e, c=c, cb=cb, w=w, n=n: e.matmul(psf[cb][32 * c:32 * c + 16, 0:w], Qbd[:, c, n, :], KTs[:, c, cb * 512:cb * 512 + w], start=True, stop=True, tile_position=(0, 32 * c)),
                         reads=ktr + ["Qbd"], writes=[PSN(cb)])
                P.op("act", lambda e, cb=cb, w=w: e.activation(Es[:, cb * 512:cb * 512 + w], psf[cb][:, 0:w], AF.Exp, scale=0.125, bias=sbrow), reads=[PSN(cb), "sbrow"], writes=["Es"])
            if n + 1 < 16:
                k_gather(n + 1); v_gather(n + 1)
            P.op("act", lambda e: e.activation(SPs, Es, AF.Ln, bias=1.0), reads=["Es"], writes=["SPs"])
            dve(lambda e, n=n: e.tensor_tensor(SPs[:, 2048:2176], SPs[:, 2048:2176], smask[:, n, :], ALU.mult), ["SPs", "smask"], ["SPs"])
            dve(lambda e: e.tensor_tensor_scan(Fs, zer, SPs, 0.0, ALU.add, ALU.add), ["zer", "SPs"], ["Fs"])
            dve(lambda e: e.tensor_scalar(ntot, Fs[:, NK - 1:NK], -1.0, None, ALU.mult), ["Fs"], ["ntot"])
            dve(lambda e: e.tensor_tensor(Fs, Fs, SPs, ALU.subtract), ["Fs", "SPs"], ["Fs"])
            P.op("act", lambda e: e.activation(SPs, Fs, AF.Exp, bias=ntot), reads=["Fs", "ntot"], writes=["SPs"])
            dve(lambda e: e.tensor_tensor(Wsb[:, 0:2048], Es[:, 0:2048], SPs[:, 0:2048], ALU.mult), ["Es", "SPs"], ["Wsb"])
            dve(lambda e: e.tensor_tensor(Fs[:, 0:128], Es[:, 2048:2176], SPs[:, 2048:2176], ALU.mult), ["Es", "SPs"], ["Fs"])
            dve(lambda e, n=n: e.tensor_tensor(Wsb[:, 2048:2176], Fs[:, 0:128], smask[:, n, :], ALU.mult), ["Fs", "smask"], ["Wsb"])
            if n + 1 < 16:
                k_process(n + 1)
            for blk in range(17):
                P.op("pe", lambda e, blk=blk: e.transpose(psb[5][:, (blk % 8) * 128:(blk % 8 + 1) * 128], Wsb[:, blk * 128:(blk + 1) * 128], ident_b), reads=["Wsb"], writes=[PSN(5)])
                if blk % 8 == 7 or blk == 16:
                    b0 = blk - (blk % 8); nb = blk % 8 + 1
                    P.op("act", lambda e, b0=b0, nb=nb: e.copy(WT[:, b0:b0 + nb, :], psb[5][:, 0:nb * 128].rearrange("p (b n) -> p b n", b=nb)), reads=[PSN(5)], writes=["WT"])
            for c in range(4):
                for blk in range(17):
                    vsrc = Vb[:, blk, c * 128:(c + 1) * 128] if blk < 16 else VBnb[:, c * 128:(c + 1) * 128]
                    P.op("pe", lambda e, c=c, blk=blk, vsrc=vsrc: e.matmul(psf[6][:, c * 16:(c + 1) * 16], vsrc, WT[:, blk, 32 * c:32 * c + 16], start=(blk == 0), stop=(blk == 16)),
                         reads=["WT"] + ([f"Vb{blk}"] if blk < 16 else ["VBnb"]), writes=[PSN(6)])
            for hh in range(2):
                P.op("act", lambda e, hh=hh, n=n: e.copy(OT[hh * 64:(hh + 1) * 64, :, NOWN * 128 + 8 * n:NOWN * 128 + 8 * n + 8],
                                                         psf[6][hh * 64:(hh + 1) * 64, 0:64].rearrange("p (c k) -> p c k", c=4)[:, :, hh * 8:(hh + 1) * 8]), reads=[PSN(6)], writes=[uid("OTs")])
            if n + 1 < 16:
                v_process(n + 1)
        P.barrier()
        A.release()

        A.mark()
        Wa = A.bf16(4, D); Wbb = A.bf16(4, D); Wg = A.bf16(4, 512); Wo = A.bf16(8, D)
        load_w_rows(Wa, wba, 4, "Wa"); load_w_rows(Wbb, wbb, 4, "Wbb"); load_w_rows(Wg, glu_w, 4, "Wg"); load_w_rows(Wo, w_o, 8, "Wo")
        glub = A.f32(4); idxo = A.i32(16)
        P.dma("sp", lambda e: e.dma_start(out=glub, in_=glub_d), writes=["glub"])
        P.dma("sp", lambda e: e.dma_start(out=idxo, in_=idxown_d), writes=["idxo"])
        g2r = A.f32(D); sh3 = A.f32(D); A3 = A.f32(D); grow = A.f32(D)
        yt = A.f32(512); y2 = A.f32(512); zg = A.f32(512); sgm = A.f32(512); zgb = A.bf16(4, 128)
        obT = [A.bf16(4, 128), A.bf16(4, 128)]
        sga = [A.bf16(8, 128), A.bf16(8, 128)]; sgb = [A.bf16(8, 128), A.bf16(8, 128)]
        m1 = A.f32(512); m2 = A.f32(512); mgT = [A.bf16(8, 128), A.bf16(8, 128)]
        h1t = A.f32(D); tmp512 = A.f32(512); tmpD = A.f32(D); junkD = A.bf16(D); ss1 = A.f32(1); rstd1 = A.f32(1)
        xs3b = A.bf16(D)
        blocks5 = list(range(NOWN + 1))
        n5 = len(blocks5)

        def load_p5_mod(g):
            load_mod(g2r, g, 5, "p5mod"); load_mod(sh3, g, 6, "p5mod"); load_mod(A3, g, 7, "p5mod")
            load_row_bcast(grow, n3, "grow")
            dve(lambda e: e.scalar_tensor_tensor(A3, A3, 1.0, grow, ALU.add, ALU.mult), ["p5mod", "grow"], ["p5mod"])

        load_p5_mod(0)
        for i in range(n5 + 2):
            jx, jy, jz = i, i - 1, i - 2
            hasx, hasy, hasz = jx < n5, 0 <= jy < n5, 0 <= jz < n5
            if hasx:
                P.dma("sp", lambda e, j=jx: e.dma_start(out=yt, in_=YT[j]), writes=["yt"])
            if hasy:
                sy = jy % 2
                P.dma("sp", lambda e, j=jy, sy=sy: e.dma_start(out=sga[sy], in_=SGA[j].rearrange("p (c n) -> p c n", c=8)), writes=[f"sga{sy}"])
                P.dma("sp", lambda e, j=jy, sy=sy: e.dma_start(out=sgb[sy], in_=SGB[j].rearrange("p (c n) -> p c n", c=8)), writes=[f"sgb{sy}"])
            if hasz:
                if jz == NOWN:
                    load_p5_mod(1)
                if jz < NOWN:
                    P.dma("pool", lambda e, j=jz: e.indirect_dma_start(out=h1t, out_offset=None, in_=H1, in_offset=bass.IndirectOffsetOnAxis(ap=idxo[:, j:j + 1], axis=0)), reads=["idxo"], writes=["h1t"])
                else:
                    P.dma("sp", lambda e: e.dma_start(out=h1t, in_=H1[NBLK * 128:(NBLK + 1) * 128, :]), writes=["h1t"])
            if hasx:
                dve(lambda e: e.tensor_tensor(y2, yt, yt, ALU.mult), ["yt"], ["y2"])
                dve(lambda e: e.tensor_scalar(y2, y2, 0.044715, 1.0, ALU.mult, ALU.add), ["y2"], ["y2"])
                dve(lambda e: e.tensor_tensor(y2, y2, yt, ALU.mult), ["y2", "yt"], ["y2"])
                P.op("act", lambda e: e.activation(sgm, y2, AF.Sigmoid, scale=1.5957691216057308), reads=["y2"], writes=["sgm"])
                dve(lambda e: e.tensor_tensor(zg, yt, sgm, ALU.mult), ["yt", "sgm"], ["zg"])
                dve(lambda e: e.tensor_copy(zgb, zg.rearrange("p (c n) -> p c n", c=4)), ["zg"], ["zgb"])
            if hasy:
                sy = jy % 2
                for hh in range(2):
                    for c4 in range(4):
                        dc = hh * 4 + c4
                        for k in range(4):
                            P.op("pe", lambda e, hh=hh, c4=c4, dc=dc, k=k, j=jy: e.matmul(psf[2 + hh][:, c4 * 128:(c4 + 1) * 128], Wa[:, k, dc * 128:(dc + 1) * 128], OT[:, k, j * 128:(j + 1) * 128], start=(k == 0), stop=(k == 3)),
                                 reads=["Wa"], writes=[PSN(2 + hh)])
                        for k in range(4):
                            P.op("pe", lambda e, hh=hh, c4=c4, dc=dc, k=k, sy=sy: e.matmul(psf[4 + hh][:, c4 * 128:(c4 + 1) * 128], Wbb[:, k, dc * 128:(dc + 1) * 128], obT[sy][:, k, :], start=(k == 0), stop=(k == 3)),
                                 reads=["Wbb", f"obT{sy}"], writes=[PSN(4 + hh)])
                    dve(lambda e, hh=hh, sy=sy: e.tensor_tensor(m1, psf[2 + hh][:, 0:512], sga[sy][:, hh * 4:(hh + 1) * 4, :].rearrange("p c n -> p (c n)"), ALU.mult), [PSN(2 + hh), f"sga{sy}"], ["m1"])
                    dve(lambda e, hh=hh, sy=sy: e.tensor_tensor(m2, psf[4 + hh][:, 0:512], sgb[sy][:, hh * 4:(hh + 1) * 4, :].rearrange("p c n -> p (c n)"), ALU.mult), [PSN(4 + hh), f"sgb{sy}"], ["m2"])
                    dve(lambda e, hh=hh, sy=sy: e.tensor_tensor(mgT[sy][:, hh * 4:(hh + 1) * 4, :], m1.rearrange("p (c n) -> p c n", c=4), m2.rearrange("p (c n) -> p c n", c=4), ALU.add), ["m1", "m2"], [f"mgT{sy}"])
            if hasz:
                sz = jz % 2
                for half in range(2):
                    for k in range(8):
                        P.op("pe", lambda e, half=half, k=k, sz=sz: e.matmul(psf[6 + half][:, 0:512], mgT[sz][:, k, :], Wo[:, k, half * 512:(half + 1) * 512], start=(k == 0), stop=(k == 7)), reads=[f"mgT{sz}", "Wo"], writes=[PSN(6 + half)])
                    hs = h1t[:, half * 512:(half + 1) * 512]
                    dve(lambda e, half=half: e.tensor_tensor(tmp512, psf[6 + half][:, 0:512], g2r[:, half * 512:(half + 1) * 512], ALU.mult), [PSN(6 + half), "p5mod"], ["tmp512"])
                    dve(lambda e, hs=hs: e.tensor_tensor(hs, hs, tmp512, ALU.add), ["tmp512", "h1t"], ["h1t"])
                P.dma("sp", lambda e, j=jz: e.dma_start(out=H2[j * 128:(j + 1) * 128, :], in_=h1t), reads=["h1t"], writes=[uid("H2")])
                norm_mod(h1t, "h1t", A3, sh3, xs3b, "xs3b", "p5")
                P.dma("sp", lambda e, j=jz: e.dma_start(out=XS3[j * 128:(j + 1) * 128, :], in_=xs3b), reads=["xs3b"], writes=[uid("XS3")])
            if hasx:
                sx = jx % 2
                for c in range(4):
                    for k in range(4):
                        P.op("pe", lambda e, c=c, k=k: e.matmul(psf[1][:, c * 128:(c + 1) * 128], Wg[:, k, c * 128:(c + 1) * 128], zgb[:, k, :], start=(k == 0), stop=(k == 3)), reads=["Wg", "zgb"], writes=[PSN(1)])
                for c in range(4):
                    P.op("act", lambda e, c=c: e.activation(sgm[:, c * 128:(c + 1) * 128], psf[1][:, c * 128:(c + 1) * 128], AF.Sigmoid, bias=glub[:, c:c + 1]), reads=[PSN(1), "glub", "zg"], writes=["sgm"])
                dve(lambda e, sx=sx: e.tensor_tensor(obT[sx], zg.rearrange("p (c n) -> p c n", c=4), sgm.rearrange("p (c n) -> p c n", c=4), ALU.mult), ["zg", "sgm"], [f"obT{sx}"])
        P.barrier()
        A.release()
        A.release()

        wstage["on"] = False
        A.n = ARENA_BYTES
        A.mark()
        W1 = A.bf16(8, 2 * DFF); W2 = A.bf16(NF, D)
        load_w_cols(W1, f2_in, 0, 2 * DFF, "f2W1", order=[x for f_ in range(NF) for x in (f_ * 128, DFF + f_ * 128)], fine=True)
        load_w_rows(W2, f2_out, NF, "f2W2", fine=True)
        g3h = A.f32(D); fnr = A.f32(D)
        ht = [A.f32(D), A.f32(D)]; xin = [A.bf16(D), A.bf16(D)]
        uT = [A.bf16(8, 128), A.bf16(8, 128)]; actT = A.bf16(NF, 128)
        sgs = [A.f32(128), A.f32(128)]
        tmp512 = A.f32(512); junkD = A.bf16(D); ss1 = A.f32(1); rstd1 = A.f32(1)
        yo = [A.f32(D), A.f32(D)]
        load_row_bcast(fnr, nfin, "fnr")
        for g in range(2):
            load_mod(g3h, g, 8, "f2mod")
            dve(lambda e: e.tensor_scalar(g3h, g3h, 0.5, None, ALU.mult), ["f2mod"], ["f2mod"])
            blocks = list(range(NOWN)) if g == 0 else [NOWN]

            def prep5(j, s):
                P.dma("sp", lambda e: e.dma_start(out=ht[s], in_=H2[j * 128:(j + 1) * 128, :]), writes=[f"ht{s}"])
                P.dma("sp", lambda e: e.dma_start(out=xin[s], in_=XS3[j * 128:(j + 1) * 128, :]), writes=[f"xin{s}"])

            prep5(blocks[0], 0)
            transpose_block(xin[0], "xin0", uT[0], "uT0", 0)
            for i, j in enumerate(blocks):
                s = i % 2
                ffn_mm1(uT[s], f"uT{s}", W1, actT, "f2")
                if i + 1 < len(blocks):
                    prep5(blocks[i + 1], 1 - s)
                ffn_mm2(actT, W2, "f2")
                if i + 1 < len(blocks):
                    transpose_block(xin[1 - s], f"xin{1 - s}", uT[1 - s], f"uT{1 - s}", 0)
                ffn_update(f"ht{s}", ht[s], g3h, "f2")
                rstd_from(ht[s], f"ht{s}", junkD, ss1, rstd1, "fin")
                dve(lambda e, s=s: e.scalar_tensor_tensor(yo[s], ht[s], rstd1, fnr, ALU.mult, ALU.mult), [f"ht{s}", "finrstd", "fnr"], [f"yo{s}"])
                dst = y_p[j * 128:(j + 1) * 128, :] if j < NOWN else y_s
                P.dma("sp", lambda e, s=s, dst=dst: e.dma_start(out=dst, in_=yo[s]), reads=[f"yo{s}"], writes=[uid("y")])
        P.barrier()
        P.emit()
    return nc


_CACHE = {}


def _consts(par):
    c = {}
    c["ident"] = np.eye(128, dtype=np.float32)
    c["iota_p"] = np.arange(128, dtype=np.float32).reshape(128, 1)
    c["iota_j"] = np.ascontiguousarray(np.broadcast_to(np.arange(512, dtype=np.float32)[None, :], (128, 512)))
    jj, ss = np.meshgrid(np.arange(128), np.arange(128), indexing="ij")
    c["ltri"] = (jj >= ss).astype(np.float32)
    am = np.zeros((8, 128, 512), np.float32)
    p = np.arange(128)[:, None]
    t = np.arange(128)[None, :]
    for m in range(8):
        for q in range(4):
            am[m, :, q * 128:(q + 1) * 128] = ((m * 128 + p) < ((2 * q + par) * 128 + t)).astype(np.float32)
    c["amask"] = am
    sm = np.zeros((128, 16, 128), np.float32)
    for cc in range(4):
        for hh in range(2):
            for i in range(8):
                row = 32 * cc + 8 * hh + i
                for n in range(16):
                    sm[row, n, 8 * n:8 * n + i] = 1.0
    c["smask"] = sm
    c["idx_own"] = ((2 * np.arange(16)[None, :] + par) * 128 + np.arange(128)[:, None]).astype(np.int32)
    c["p0p1"] = np.ascontiguousarray(np.broadcast_to(np.array([[1.0 - par, float(par)]], np.float32), (128, 2)))
    return c


def kernel(**inp):
    f = lambda k: np.asarray(inp[k])
    if "nc" not in _CACHE:
        _CACHE["nc"] = build_program()
    nc = _CACHE["nc"]
    x_prompt, x_sample = f("x_prompt"), f("x_sample")
    c_prompt, c_sample = f("c_prompt"), f("c_sample")
    ck = np.ascontiguousarray(f("cache_k")[0].reshape(-1, 512)); cv = np.ascontiguousarray(f("cache_v")[0].reshape(-1, 512))
    page_table = f("page_table").astype(np.int32)
    s_re, s_im = f("state_ssm_re")[0], f("state_ssm_im")[0]

    def st_layout(a):
        return np.ascontiguousarray(a.reshape(16, 16, 2, 64).transpose(2, 3, 1, 0).reshape(128, 16, 16))

    def gp_layout(a):
        return np.ascontiguousarray(a.reshape(16, 2, 64).transpose(1, 2, 0).reshape(128, 16))

    b_re, b_im, c_re, c_im = f("ssm_b_re")[0], f("ssm_b_im")[0], f("ssm_c_re")[0], f("ssm_c_im")[0]

    def bfull(b):
        o = np.zeros((16, 128, 128), np.float32)
        for ti in range(16):
            for g2 in range(2):
                r0 = 32 * (ti % 4) + 16 * g2
                o[ti, r0:r0 + 16, 64 * g2:64 * g2 + 64] = b[2 * ti + g2].T
        return o

    def cfull(cm):
        o = np.zeros((16, 128, 128), np.float32)
        for ti in range(16):
            for g2 in range(2):
                c0 = 32 * (ti % 4) + 16 * g2
                o[ti, 64 * g2:64 * g2 + 64, c0:c0 + 16] = cm[2 * ti + g2].T
        return o

    ld = f("ssm_log_dt")[0]
    shared = {
        "cache_k": ck, "cache_v": cv,
        "ada_w": f("ada_w")[0], "ada_b": f("ada_b")[0].reshape(1, -1),
        "norm_ffn1": f("norm_ffn1")[0].reshape(1, -1), "norm_mix": f("norm_mix")[0].reshape(1, -1),
        "norm_ffn2": f("norm_ffn2")[0].reshape(1, -1), "final_norm": f("final_norm").reshape(1, -1),
        "ffn1_w_in": f("ffn1_w_in")[0], "ffn1_w_out": f("ffn1_w_out")[0],
        "ffn2_w_in": f("ffn2_w_in")[0], "ffn2_w_out": f("ffn2_w_out")[0],
        "w_in": f("w_in")[0], "sb_bias": f("sb_bias")[0].reshape(1, 8),
        "lam_re": gp_layout(f("ssm_lambda_re")[0]), "lam_im": gp_layout(f("ssm_lambda_im")[0]),
        "logdt": np.ascontiguousarray(np.broadcast_to(ld.reshape(16, 2).T[:, None, :], (2, 64, 16)).reshape(128, 16)),
        "bfull_re": bfull(b_re), "bfull_im": bfull(b_im), "cfull_re": cfull(c_re), "cfull_im": cfull(c_im),
        "ssm_d": np.ascontiguousarray(f("ssm_d")[0].reshape(4, 128).T), "glu_w": f("glu_w")[0],
        "glu_b": np.ascontiguousarray(f("glu_b")[0].reshape(4, 128).T),
        "w_branch_a": f("w_branch_a")[0], "w_branch_b": f("w_branch_b")[0], "w_out": f("w_out")[0],
    }
    sbr = np.zeros((128, 1), np.float32)
    for cc in range(4):
        for hh in range(2):
            sbr[32 * cc + 8 * hh:32 * cc + 8 * hh + 8, 0] = f("sb_bias")[0, 2 * cc + hh]
    shared["sb_rows"] = sbr
    shared = {k: np.ascontiguousarray(v, dtype=np.float32) for k, v in shared.items()}
    cst = [_consts(0), _consts(1)]
    in_maps = []
    for core in range(8):
        seq, par = core // 2, core % 2
        m = dict(shared)
        m.update(cst[par])
        m["xp"] = np.ascontiguousarray(x_prompt[seq])
        m["xs"] = np.ascontiguousarray(x_sample[16 * core:16 * core + 16].reshape(128, D))
        m["cpr"] = np.ascontiguousarray(np.repeat(c_prompt[seq:seq + 1], 128, axis=0))
        m["csr"] = np.ascontiguousarray(np.repeat(c_sample[16 * core:16 * core + 16], 8, axis=0))
        m["ptab"] = np.ascontiguousarray(page_table[16 * core:16 * core + 16].reshape(1, 256))
        m["st_re"] = st_layout(s_re[16 * core:16 * core + 16]); m["st_im"] = st_layout(s_im[16 * core:16 * core + 16])
        in_maps.append(m)
    res = run_bass_kernel_spmd(nc, in_maps, core_ids=list(range(8))).results
    B = 4
    y_prompt = np.zeros((B, SEQ, D), np.float32); y_sample = np.zeros((128, 8, D), np.float32)
    nkp = np.zeros((1, B, SEQ, 8, 64), np.float32); nvp = np.zeros_like(nkp)
    srp = np.zeros((1, B, 32, 64), np.float32); sip = np.zeros_like(srp)
    nks = np.zeros((1, 128, 8, 8, 64), np.float32); nvs = np.zeros_like(nks)
    srs = np.zeros((1, 128, 32, 64), np.float32); sis = np.zeros_like(srs)

    def gp_back(a):
        return a.reshape(2, 64, 16).transpose(2, 0, 1).reshape(32, 64)

    def st_back(a):
        return a.reshape(2, 64, 16, 16).transpose(3, 2, 0, 1).reshape(16, 32, 64)

    for core in range(8):
        seq, par = core // 2, core % 2
        r = res[core]
        yp = r["y_p"].reshape(16, 128, D)
        y_prompt[seq].reshape(16, 2, 128, D)[:, par] = yp
        y_sample[16 * core:16 * core + 16] = r["y_s"].reshape(16, 8, D)
        if par == 0:
            nkp[0, seq] = r["kp"].reshape(SEQ, 8, 64); nvp[0, seq] = r["vp"].reshape(SEQ, 8, 64)
            srp[0, seq] = gp_back(r["sp_re"]); sip[0, seq] = gp_back(r["sp_im"])
        nks[0, 16 * core:16 * core + 16] = r["ks"].reshape(16, 8, 8, 64); nvs[0, 16 * core:16 * core + 16] = r["vs"].reshape(16, 8, 8, 64)
        srs[0, 16 * core:16 * core + 16] = st_back(r["ss_re"]); sis[0, 16 * core:16 * core + 16] = st_back(r["ss_im"])
    return (y_prompt, y_sample, nkp, nvp, srp, sip, nks, nvs, srs, sis)
```

```python
import numpy as np
from contextlib import ExitStack
import concourse.bass as bass
import concourse.mybir as mybir
from concourse.bass_utils import run_bass_kernel_spmd

F32 = mybir.dt.float32
BF16 = mybir.dt.bfloat16
I32 = mybir.dt.int32
AF = mybir.ActivationFunctionType
ALU = mybir.AluOpType

D = 1024
DFF = 2816
NF = 22
SEQ = 4096
NBLK = 32
NOWN = 16
PI = 3.14159265358979
TWO_PI = 6.283185307179586


import types


def _snap(fn):
    if fn.__closure__ is None:
        return fn
    cells = []
    for c in fn.__closure__:
        try:
            cells.append(types.CellType(c.cell_contents))
        except ValueError:
            cells.append(c)
    return types.FunctionType(fn.__code__, fn.__globals__, fn.__name__, fn.__defaults__, tuple(cells))


class Prog:
    CE = ("pe", "act", "dve", "pool")
    QE = ("sp", "act", "pool")

    def __init__(self, nc, sems, n_dma_sems=(28, 4, 24)):
        self.nc = nc
        self.streams = {e: [] for e in ("pe", "act", "dve", "pool", "sp")}
        self.count = {e: 0 for e in self.CE}
        self.res = {}
        self.waited = {e: {} for e in self.streams}
        self.sems = {}
        it = iter(sems)
        for e in self.CE:
            self.sems[e] = next(it)
        self.dma_pool = {}
        for q, n in zip(self.QE, n_dma_sems):
            self.dma_pool[q] = [[next(it), 0] for _ in range(n)]
        self.dma_rr = {q: 0 for q in self.QE}
        for q in self.QE:
            for i, sv in enumerate(self.dma_pool[q]):
                self.sems[("d", q, i)] = sv[0]
        self.final_events = {}

    def _deps(self, eng, reads, writes):
        evs = {}

        def add(k, v):
            if v > evs.get(k, 0):
                evs[k] = v

        for r in reads:
            st = self.res.get(r)
            if st and st["w"]:
                add(*st["w"])
        for w in writes:
            st = self.res.get(w)
            if st:
                if st["w"]:
                    add(*st["w"])
                for k, v in st["r"].items():
                    if k == eng:
                        continue
                    add(k, v)
        if eng == "pe":
            evs.pop("pe", None)
        return evs

    def _commit(self, ev, reads, writes):
        k, v = ev
        for r in reads:
            st = self.res.setdefault(r, {"w": None, "r": {}})
            if v > st["r"].get(k, 0):
                st["r"][k] = v
        for w in writes:
            self.res[w] = {"w": ev, "r": {}}

    def _waits(self, eng, evs):
        out = []
        wd = self.waited[eng]
        for k, v in evs.items():
            if wd.get(k, 0) >= v:
                continue
            wd[k] = v
            out.append((k, v))
        return out

    def op(self, eng, fn, reads=(), writes=()):
        fn = _snap(fn)
        evs = self._deps(eng, reads, writes)
        waits = self._waits(eng, evs)
        self.count[eng] += 1
        ev = (eng, self.count[eng])
        self.streams[eng].append(("op", waits, fn, ev))
        self._commit(ev, reads, writes)
        return ev

    def dma(self, q, fn, reads=(), writes=()):
        fn = _snap(fn)
        evs = self._deps(None, reads, writes)
        pool = self.dma_pool[q]
        i = self.dma_rr[q]
        self.dma_rr[q] = (i + 1) % len(pool)
        key = ("d", q, i)
        prev = pool[i][1]
        if prev > 0:
            evs[key] = max(evs.get(key, 0), prev)
        waits = self._waits(q, evs)
        pool[i][1] = prev + 16
        ev = (key, prev + 16)
        self.streams[q].append(("dma", waits, fn, ev))
        self._commit(ev, reads, writes)
        self.final_events[key] = prev + 16
        return ev

    def barrier(self):
        evs = {e: self.count[e] for e in self.CE if self.count[e] > 0}
        evs.update(self.final_events)
        for e in self.streams:
            w = self._waits(e, {k: v for k, v in evs.items() if k != e or e != "pe"})
            if w:
                self.streams[e].append(("wait", w, None, None))
        self.res = {}

    def emit(self):
        nc = self.nc
        handles = {"pe": "tensor", "act": "scalar", "dve": "vector", "pool": "gpsimd", "sp": "sync"}
        with nc.Block() as block:
            for e, hname in handles.items():
                stream = self.streams[e]

                def body(engine, stream=stream, e=e):
                    for kind, waits, fn, ev in stream:
                        for k, v in waits:
                            engine.wait_ge(self.sems[k], v)
                        if kind == "wait":
                            continue
                        ins = fn(engine)
                        if kind == "op":
                            ins.then_inc(self.sems[e], 1)
                        else:
                            ins.then_inc(self.sems[ev[0]], 16)

                getattr(block, hname)(body)


class Arena:
    def __init__(self, big, nbytes):
        self.f = big
        self.b = big.bitcast(BF16)
        self.i = big.bitcast(I32)
        self.n = nbytes
        self.top = 0
        self.marks = []

    def alloc(self, nbytes, dt=F32, shape=None):
        nbytes = (nbytes + 15) // 16 * 16
        o = self.top
        self.top += nbytes
        assert self.top <= self.n, f"SBUF arena overflow {self.top} > {self.n}"
        if dt == F32:
            v = self.f[:, o // 4:(o + nbytes) // 4]
        elif dt == BF16:
            v = self.b[:, o // 2:(o + nbytes) // 2]
        else:
            v = self.i[:, o // 4:(o + nbytes) // 4]
        return v

    def f32(self, *shape):
        n = int(np.prod(shape))
        v = self.alloc(n * 4, F32)[:, 0:n]
        return _shape(v, shape)

    def bf16(self, *shape):
        n = int(np.prod(shape))
        v = self.alloc(n * 2, BF16)[:, 0:n]
        return _shape(v, shape)

    def i32(self, *shape):
        n = int(np.prod(shape))
        v = self.alloc(n * 4, I32)[:, 0:n]
        return _shape(v, shape)

    def mark(self):
        self.marks.append(self.top)

    def release(self):
        self.top = self.marks.pop()


def _shape(v, shape):
    if len(shape) == 1:
        return v
    if len(shape) == 2:
        return v.rearrange("p (a b) -> p a b", a=shape[0])
    if len(shape) == 3:
        return v.rearrange("p (a b c) -> p a b c", a=shape[0], b=shape[1])
    raise ValueError


_uid = [0]


def uid(s):
    _uid[0] += 1
    return f"{s}#{_uid[0]}"


def build_program():
    nc = bass.Bass("TRN2", target_bir_lowering=False)
    di = {}

    def inp(name, shape, dt=F32):
        di[name] = nc.dram_tensor(name, list(shape), dt, kind="ExternalInput").ap()
        return di[name]

    def outp(name, shape, dt=F32):
        di[name] = nc.dram_tensor(name, list(shape), dt, kind="ExternalOutput").ap()
        return di[name]

    def scratch(name, shape, dt=F32):
        di[name] = nc.dram_tensor(name, list(shape), dt, kind="Internal").ap()
        return di[name]

    xp = inp("xp", [SEQ, D]); xs = inp("xs", [128, D])
    cpr = inp("cpr", [128, D]); csr = inp("csr", [128, D])
    cache_k = inp("cache_k", [2560 * 128, 512]); cache_v = inp("cache_v", [2560 * 128, 512])
    ptab = inp("ptab", [1, 256], I32)
    st_re = inp("st_re", [128, 16, 16]); st_im = inp("st_im", [128, 16, 16])
    ada_w = inp("ada_w", [D, 9 * D]); ada_b = inp("ada_b", [1, 9 * D])
    n1 = inp("norm_ffn1", [1, D]); n2 = inp("norm_mix", [1, D]); n3 = inp("norm_ffn2", [1, D]); nfin = inp("final_norm", [1, D])
    f1_in = inp("ffn1_w_in", [D, 2 * DFF]); f1_out = inp("ffn1_w_out", [DFF, D])
    f2_in = inp("ffn2_w_in", [D, 2 * DFF]); f2_out = inp("ffn2_w_out", [DFF, D])
    w_in = inp("w_in", [D, 4096]); sbb_d = inp("sb_bias", [1, 8])
    lam_re = inp("lam_re", [128, 16]); lam_im = inp("lam_im", [128, 16]); logdt = inp("logdt", [128, 16])
    bre_d = inp("bfull_re", [16, 128, 128]); bim_d = inp("bfull_im", [16, 128, 128])
    cre_d = inp("cfull_re", [16, 128, 128]); cim_d = inp("cfull_im", [16, 128, 128])
    dsk_d = inp("ssm_d", [128, 4]); glu_w = inp("glu_w", [512, 512]); glub_d = inp("glu_b", [128, 4])
    wba = inp("w_branch_a", [512, D]); wbb = inp("w_branch_b", [512, D]); w_o = inp("w_out", [D, D])
    ident_d = inp("ident", [128, 128]); iotap_d = inp("iota_p", [128, 1]); iotaj_d = inp("iota_j", [128, 512])
    ltri_d = inp("ltri", [128, 128]); amask_d = inp("amask", [8, 128, 512]); smask_d = inp("smask", [128, 16, 128])
    idxown_d = inp("idx_own", [128, 16], I32); p0p1_d = inp("p0p1", [128, 2]); inp("sb_rows", [128, 1])

    y_p = outp("y_p", [NOWN * 128, D]); y_s = outp("y_s", [128, D])
    kp_o = outp("kp", [SEQ, 512]); vp_o = outp("vp", [SEQ, 512])
    spre_o = outp("sp_re", [128, 16]); spim_o = outp("sp_im", [128, 16])
    ks_o = outp("ks", [128, 512]); vs_o = outp("vs", [128, 512])
    ssre_o = outp("ss_re", [128, 16, 16]); ssim_o = outp("ss_im", [128, 16, 16])

    MOD = scratch("MOD", [2, 128, 9 * D])
    H1 = scratch("H1", [33 * 128, D]); XS2 = scratch("XS2", [33 * 128, D], BF16)
    SGA = scratch("SGA", [17, 128, 1024], BF16); SGB = scratch("SGB", [17, 128, 1024], BF16)
    YT = scratch("YT", [17, 128, 512])
    H2 = scratch("H2", [17 * 128, D]); XS3 = scratch("XS3", [17 * 128, D], BF16)

    with ExitStack() as es:
        sems = [es.enter_context(nc.semaphore(f"s{i}")) for i in range(64)]
        ARENA_BYTES = 207 * 1024
        big = es.enter_context(nc.sbuf_tensor("big", [128, ARENA_BYTES // 4], F32))
        psf = [es.enter_context(nc.psum_tensor(f"ps{i}", [128, 512], F32)) for i in range(8)]
        psb = [p.bitcast(BF16) for p in psf]
        P = Prog(nc, sems)
        A = Arena(big, ARENA_BYTES)

        def PSN(i):
            return f"ps{i}"

        wstage = {"n": 0, "tiles": None, "on": False}
        ident_b = A.bf16(128); ones_b = A.bf16(128); ltri_b = A.bf16(128)
        ones_f = A.f32(128)
        iota_p = A.f32(1); p0p1 = A.f32(2); sbb = A.f32(8); epsb = A.f32(1)
        P.dma("pool", lambda e: e.dma_start(out=ident_b, in_=ident_d), writes=["ident_b"])
        P.dma("pool", lambda e: e.dma_start(out=ltri_b, in_=ltri_d), writes=["ltri_b"])
        P.dma("sp", lambda e: e.dma_start(out=iota_p, in_=iotap_d), writes=["iota_p"])
        P.dma("sp", lambda e: e.dma_start(out=p0p1, in_=p0p1_d), writes=["p0p1"])
        P.dma("sp", lambda e: e.dma_start(out=sbb, in_=sbb_d.partition_broadcast(128)), writes=["sbb"])
        P.op("dve", lambda e: e.memset(ones_b, 1.0), writes=["ones_b"])
        P.op("dve", lambda e: e.memset(ones_f, 1.0), writes=["ones_f"])
        P.op("dve", lambda e: e.memset(epsb, 1e-6), writes=["epsb"])
        CONST = ["ident_b", "ones_b", "ltri_b", "ones_f", "iota_p", "p0p1", "sbb", "epsb"]
        WST_BYTES = 3 * 4096
        wstage["tiles"] = [A.f[:, (ARENA_BYTES - WST_BYTES) // 4 + 1024 * i:(ARENA_BYTES - WST_BYTES) // 4 + 1024 * (i + 1)] for i in range(3)]

        def after_barrier():
            pass


        def rstd_from(x_ap, x_res, junk, ss, rstd, tag):
            P.op("act", lambda e: e.activation(junk, x_ap, AF.Square, accum_out=ss), reads=[x_res], writes=[tag + "junk", tag + "ss"])
            P.op("act", lambda e: e.activation(rstd, ss, AF.Sqrt, scale=1.0 / D, bias=epsb), reads=[tag + "ss"], writes=[tag + "rs0"])
            P.op("dve", lambda e: e.reciprocal(rstd, rstd), reads=[tag + "rs0"], writes=[tag + "rstd"])

        def transpose_block(src_b, src_res, dstT, dst_res, pbank):
            for k in range(8):
                P.op("pe", lambda e, k=k: e.transpose(psb[pbank][:, k * 128:(k + 1) * 128], src_b[:, k * 128:(k + 1) * 128], ident_b),
                     reads=[src_res, "ident_b"], writes=[PSN(pbank)])
            P.op("act", lambda e: e.copy(dstT, psb[pbank][:, 0:1024].rearrange("p (k n) -> p k n", k=8)), reads=[PSN(pbank)], writes=[dst_res])

        def _stage_cast(dst_view, src_view, res, fine):
            if not wstage["on"]:
                P.dma("pool", lambda e: e.dma_start(out=dst_view, in_=src_view), writes=[res])
                return
            i = wstage["n"]; wstage["n"] += 1
            sl = i % len(wstage["tiles"])
            st = wstage["tiles"][sl]
            shp = dst_view.shape
            stv = st[:, 0:shp[1] * shp[2]].rearrange("p (k n) -> p k n", k=shp[1])
            P.dma("sp", lambda e: e.dma_start(out=stv, in_=src_view), writes=[f"wst{sl}"])
            if fine and i % 2 == 1:
                P.op("act", lambda e: e.copy(dst_view, stv), reads=[f"wst{sl}"], writes=[res])
            else:
                P.op("pool", lambda e: e.tensor_copy(dst_view, stv), reads=[f"wst{sl}"], writes=[res])

        def load_w_cols(dst, src, c0, c1, res, step=512, order=None, fine=False):
            if fine:
                step = 128
            starts = order if order is not None else list(range(c0, c1, step))
            for a in starts:
                b = min(a + step, c1)
                _stage_cast(dst[:, :, a - c0:b - c0], src[:, a:b].rearrange("(k p) n -> p k n", p=128), f"{res}:{a - c0}" if fine else res, fine)

        def load_w_rows(dst, src, nchunk, res, step=1, fine=False):
            ncols = dst.shape[2]
            per = max(1, 1024 // ncols) if fine else max(1, 4096 // ncols)
            for a in range(0, nchunk, per):
                b = min(a + per, nchunk)
                _stage_cast(dst[:, a:b, :], src[a * 128:b * 128, :].rearrange("(k p) n -> p k n", p=128), f"{res}:{a}" if fine else res, fine)

        def ffn_mm1(uT, uT_res, W1, actT, tag):
            for f in range(NF):
                pb = 2 + (f % 2)
                for k in range(8):
                    P.op("pe", lambda e, f=f, k=k, pb=pb: e.matmul(psf[pb][:, 0:128], W1[:, k, f * 128:(f + 1) * 128], uT[:, k, :], start=(k == 0), stop=(k == 7)),
                         reads=[uT_res, f"{tag}W1:{f * 128}"], writes=[PSN(pb)])
                for k in range(8):
                    P.op("pe", lambda e, f=f, k=k, pb=pb: e.matmul(psf[pb][:, 128:256], W1[:, k, DFF + f * 128:DFF + (f + 1) * 128], uT[:, k, :], start=(k == 0), stop=(k == 7)),
                         reads=[uT_res, f"{tag}W1:{DFF + f * 128}"], writes=[PSN(pb)])
                sg = sgs[f % 2]
                P.op("act", lambda e, pb=pb, sg=sg: e.activation(sg, psf[pb][:, 0:128], AF.Silu), reads=[PSN(pb)], writes=[f"sg{f % 2}"])
                P.op("dve", lambda e, f=f, pb=pb, sg=sg: e.tensor_tensor(actT[:, f, :], sg, psf[pb][:, 128:256], ALU.mult), reads=[f"sg{f % 2}", PSN(pb)], writes=[f"{tag}actT{f}"])

        def ffn_mm2(actT, W2, tag):
            for half in range(2):
                pb = 4 + half
                for f in range(NF):
                    P.op("pe", lambda e, f=f, half=half, pb=pb: e.matmul(psf[pb][:, 0:512], actT[:, f, :], W2[:, f, half * 512:(half + 1) * 512], start=(f == 0), stop=(f == NF - 1)),
                         reads=[f"{tag}actT{f}", f"{tag}W2:{f}"], writes=[PSN(pb)])

        def ffn_update(hres_in, h_ap, ghalf, tag):
            for half in range(2):
                pb = 4 + half
                hs = h_ap[:, half * 512:(half + 1) * 512]
                tm = tmp512
                P.op("dve", lambda e, half=half, pb=pb, tm=tm: e.tensor_tensor(tm, psf[pb][:, 0:512], ghalf[:, half * 512:(half + 1) * 512], ALU.mult), reads=[PSN(pb), tag + "mod"], writes=["tmp512"])
                P.op("dve", lambda e, hs=hs, tm=tm: e.tensor_tensor(hs, hs, tm, ALU.add), reads=["tmp512", hres_in], writes=[hres_in])

        def norm_mod(x_ap, x_res, Arow, shrow, out_b, out_res, tag):
            rstd_from(x_ap, x_res, junkD, ss1, rstd1, tag)
            P.op("dve", lambda e: e.scalar_tensor_tensor(tmpD, x_ap, rstd1, Arow, ALU.mult, ALU.mult), reads=[x_res, tag + "rstd", tag + "mod"], writes=["tmpD"])
            P.op("dve", lambda e: e.tensor_tensor(out_b, tmpD, shrow, ALU.add), reads=["tmpD", tag + "mod"], writes=[out_res])

        wstage["on"] = False
        A.n = ARENA_BYTES
        A.mark()
        scT = [A.bf16(8, 128), A.bf16(8, 128)]
        ctile = A.f32(D); cb = A.bf16(D)
        for g, src in enumerate((cpr, csr)):
            P.dma("sp", lambda e, src=src: e.dma_start(out=ctile, in_=src), writes=["ctile"])
            P.op("act", lambda e: e.activation(cb, ctile, AF.Silu), reads=["ctile"], writes=["cb"])
            transpose_block(cb, "cb", scT[g], f"scT{g}", 0)
        wA = [A.bf16(8, 512), A.bf16(8, 512)]
        brow = [A.f32(512), A.f32(512)]
        stg = [A.f32(512), A.f32(512)]
        for cg in range(18):
            s = cg % 2
            load_w_cols(wA[s], ada_w, cg * 512, (cg + 1) * 512, f"wA{s}")
            wa_res = [f"wA{s}"]
            P.dma("sp", lambda e, cg=cg, s=s: e.dma_start(out=brow[s][0:1, :], in_=ada_b[0:1, cg * 512:(cg + 1) * 512]), writes=[f"brow{s}"])
            for g in range(2):
                pb = 2 + g
                for k in range(8):
                    P.op("pe", lambda e, k=k, g=g, s=s, pb=pb: e.matmul(psf[pb][:, 0:512], scT[g][:, k, :], wA[s][:, k, :], start=(k == 0), stop=False),
                         reads=[f"scT{g}"] + wa_res, writes=[PSN(pb)])
                P.op("pe", lambda e, s=s, pb=pb: e.matmul(psf[pb][:, 0:512], ones_f[0:1, :], brow[s][0:1, :], start=False, stop=True),
                     reads=["ones_f", f"brow{s}"], writes=[PSN(pb)])
                P.op("act", lambda e, g=g, pb=pb: e.copy(stg[g], psf[pb][:, 0:512]), reads=[PSN(pb)], writes=[f"stg{g}"])
                P.dma("sp", lambda e, g=g, cg=cg: e.dma_start(out=MOD[g, :, cg * 512:(cg + 1) * 512], in_=stg[g]), reads=[f"stg{g}"], writes=[f"MOD{g}"])
        P.barrier()
        A.release()

        def load_mod(dst, g, chunk, res):
            P.dma("sp", lambda e: e.dma_start(out=dst, in_=MOD[g, :, chunk * D:(chunk + 1) * D]), writes=[res])

        def load_row_bcast(dst, src, res):
            P.dma("sp", lambda e: e.dma_start(out=dst, in_=src.partition_broadcast(128)), writes=[res])

        A.mark()
        W1 = A.bf16(8, 2 * DFF); W2 = A.bf16(NF, D)
        load_w_cols(W1, f1_in, 0, 2 * DFF, "f1W1", order=[x for f_ in range(NF) for x in (f_ * 128, DFF + f_ * 128)], fine=True)
        load_w_rows(W2, f1_out, NF, "f1W2", fine=True)
        A1 = A.f32(D); sh1 = A.f32(D); g1h = A.f32(D); A2 = A.f32(D); sh2 = A.f32(D); grow = A.f32(D)
        xt = [A.f32(D), A.f32(D)]
        xsb = [A.bf16(D), A.bf16(D)]; uT = [A.bf16(8, 128), A.bf16(8, 128)]; actT = A.bf16(NF, 128)
        sgs = [A.f32(128), A.f32(128)]
        tmp512 = A.f32(512); tmpD = A.f32(D); junkD = A.bf16(D)
        ss1 = A.f32(1); rstd1 = A.f32(1)
        xs2b = [A.bf16(D), A.bf16(D)]
        for g in range(2):
            load_mod(sh1, g, 0, "f1mod"); load_mod(A1, g, 1, "f1mod"); load_mod(g1h, g, 2, "f1mod")
            load_mod(sh2, g, 3, "f1mod"); load_mod(A2, g, 4, "f1mod")
            load_row_bcast(grow, n1, "grow")
            P.op("dve", lambda e: e.scalar_tensor_tensor(A1, A1, 1.0, grow, ALU.add, ALU.mult), reads=["f1mod", "grow"], writes=["f1mod"])
            P.op("dve", lambda e: e.tensor_scalar(g1h, g1h, 0.5, None, ALU.mult), reads=["f1mod"], writes=["f1mod"])
            load_row_bcast(grow, n2, "grow")
            P.op("dve", lambda e: e.scalar_tensor_tensor(A2, A2, 1.0, grow, ALU.add, ALU.mult), reads=["f1mod", "grow"], writes=["f1mod"])
            blocks = list(range(NBLK)) if g == 0 else [NBLK]

            def prep_norm(b, s):
                src = xp[b * 128:(b + 1) * 128, :] if b < NBLK else xs
                P.dma("sp", lambda e: e.dma_start(out=xt[s], in_=src), writes=[f"x{s}"])
                norm_mod(xt[s], f"x{s}", A1, sh1, xsb[s], f"xsb{s}", "f1")

            def prep_tr(s):
                transpose_block(xsb[s], f"xsb{s}", uT[s], f"uT{s}", 0)

            prep_norm(blocks[0], 0)
            prep_tr(0)
            for i, b in enumerate(blocks):
                s = i % 2
                x = xt[s]; xr = f"x{s}"
                ffn_mm1(uT[s], f"uT{s}", W1, actT, "f1")
                if i + 1 < len(blocks):
                    prep_norm(blocks[i + 1], 1 - s)
                ffn_mm2(actT, W2, "f1")
                if i + 1 < len(blocks):
                    prep_tr(1 - s)
                ffn_update(xr, x, g1h, "f1")
                P.dma("sp", lambda e, x=x, b=b: e.dma_start(out=H1[b * 128:(b + 1) * 128, :], in_=x), reads=[xr], writes=["H1"])
                norm_mod(x, xr, A2, sh2, xs2b[s], f"xs2b{s}", "f1")
                P.dma("sp", lambda e, s=s, b=b: e.dma_start(out=XS2[b * 128:(b + 1) * 128, :], in_=xs2b[s]), reads=[f"xs2b{s}"], writes=["XS2"])
        P.barrier()
        A.release()

        A.mark()
        OT_BYTES = 4 * (NOWN * 128 + 128) * 2
        OT = A.b[:, (ARENA_BYTES - OT_BYTES) // 2:ARENA_BYTES // 2].rearrange("p (a b) -> p a b", a=4)
        QTs = A.bf16(4, 128); KTn = A.bf16(4, 128); VBn = A.f32(512)
        ident_f = A.f32(128)
        P.dma("sp", lambda e: e.dma_start(out=ident_f, in_=ident_d), writes=["ident_f"])
        A.mark()
        KT = A.bf16(4, SEQ)
        VB = A.bf16(NBLK, 512)
        QT = A.bf16(4, NOWN * 128)
        A.mark()
        ST = A.bf16(4, SEQ + 128)
        A.mark()
        WinA = A.bf16(8, 1536)
        load_w_cols(WinA, w_in, 512, 2048, "Win")
        WinQ = A.bf16(8, 512); WinG = A.bf16(8, 2048)
        load_w_cols(WinQ, w_in, 0, 512, "WinQ"); load_w_cols(WinG, w_in, 2048, 4096, "WinG")
        xin = [A.bf16(D), A.bf16(D)]
        u2T = A.bf16(8, 128)
        kf = A.f32(512); vf = A.f32(512)
        idxo = A.i32(16)
        sgat = A.bf16(8, 128); sgbt = A.bf16(8, 128)
        P.dma("sp", lambda e: e.dma_start(out=idxo, in_=idxown_d), writes=["idxo"])
        for b in range(NBLK + 1):
            s = b % 2
            P.dma("sp", lambda e, s=s, b=b: e.dma_start(out=xin[s], in_=XS2[b * 128:(b + 1) * 128, :]), writes=[f"xin{s}"])
            transpose_block(xin[s], f"xin{s}", u2T, "u2T", 0)
            for (c0, dstf, pb, nm) in ((512, kf, 2, "kf"), (1024, vf, 3, "vf")):
                for k in range(8):
                    P.op("pe", lambda e, k=k, c0=c0, pb=pb: e.matmul(psf[pb][:, 0:512], u2T[:, k, :], WinA[:, k, c0 - 512:c0], start=(k == 0), stop=(k == 7)),
                         reads=["u2T", "Win"], writes=[PSN(pb)])
                P.op("act", lambda e, dstf=dstf, pb=pb: e.copy(dstf, psf[pb][:, 0:512]), reads=[PSN(pb)], writes=[nm])
            ko = kp_o[b * 128:(b + 1) * 128, :] if b < NBLK else ks_o
            vo = vp_o[b * 128:(b + 1) * 128, :] if b < NBLK else vs_o
            P.dma("sp", lambda e, ko=ko: e.dma_start(out=ko, in_=kf), reads=["kf"], writes=[uid("ko")])
            P.dma("sp", lambda e, vo=vo: e.dma_start(out=vo, in_=vf), reads=["vf"], writes=[uid("vo")])
            if b < NBLK:
                P.op("pool", lambda e, b=b: e.tensor_copy(VB[:, b, :], vf), reads=["vf"], writes=[f"VB{b}"])
            else:
                P.op("pool", lambda e: e.tensor_copy(VBn, vf), reads=["vf"], writes=["VBn"])
            for (c0, dst, pb, nm) in ((512, KT, 4, "KT"), (1536, ST, 5, "ST")):
                for c in range(4):
                    for k in range(8):
                        P.op("pe", lambda e, k=k, c=c, c0=c0, pb=pb: e.matmul(psf[pb][:, c * 128:(c + 1) * 128], WinA[:, k, c0 - 512 + c * 128:c0 - 512 + (c + 1) * 128], u2T[:, k, :], start=(k == 0), stop=(k == 7)),
                             reads=["u2T", "Win"], writes=[PSN(pb)])
                dd = KTn if (nm == "KT" and b == NBLK) else dst[:, :, b * 128:(b + 1) * 128]
                P.op("act", lambda e, dd=dd, pb=pb: e.copy(dd, psf[pb][:, 0:512].rearrange("p (c n) -> p c n", c=4)),
                     reads=[PSN(pb)], writes=[f"{nm}{b}"])
        for j in range(NOWN + 1):
            s = j % 2
            if j < NOWN:
                P.dma("pool", lambda e, s=s, j=j: e.indirect_dma_start(out=xin[s], out_offset=None, in_=XS2, in_offset=bass.IndirectOffsetOnAxis(ap=idxo[:, j:j + 1], axis=0)),
                      reads=["idxo"], writes=[f"xin{s}"])
            else:
                P.dma("sp", lambda e, s=s: e.dma_start(out=xin[s], in_=XS2[NBLK * 128:(NBLK + 1) * 128, :]), writes=[f"xin{s}"])
            transpose_block(xin[s], f"xin{s}", u2T, "u2T", 0)
            for c in range(4):
                for k in range(8):
                    P.op("pe", lambda e, k=k, c=c: e.matmul(psf[2][:, c * 128:(c + 1) * 128], WinQ[:, k, c * 128:(c + 1) * 128], u2T[:, k, :], start=(k == 0), stop=(k == 7)),
                         reads=["u2T", "WinQ"], writes=[PSN(2)])
            qd = QT[:, :, j * 128:(j + 1) * 128] if j < NOWN else QTs
            P.op("act", lambda e, qd=qd: e.copy(qd, psf[2][:, 0:512].rearrange("p (c n) -> p c n", c=4)), reads=[PSN(2)], writes=[f"QT{j}"])
            for (c0, dst, scr, nm) in ((2048, sgat, SGA, "sga"), (3072, sgbt, SGB, "sgb")):
                for hh in range(2):
                    pb = 4 + hh
                    for c in range(4):
                        cc = hh * 4 + c
                        for k in range(8):
                            P.op("pe", lambda e, k=k, c=c, cc=cc, c0=c0, pb=pb: e.matmul(psf[pb][:, c * 128:(c + 1) * 128], WinG[:, k, c0 - 2048 + cc * 128:c0 - 2048 + (cc + 1) * 128], u2T[:, k, :], start=(k == 0), stop=(k == 7)),
                                 reads=["u2T", "WinG"], writes=[PSN(pb)])
                    P.op("act", lambda e, dst=dst, hh=hh, pb=pb: e.activation(dst[:, hh * 4:(hh + 1) * 4, :], psf[pb][:, 0:512].rearrange("p (c n) -> p c n", c=4), AF.Sigmoid),
                         reads=[PSN(pb)], writes=[nm])
                P.dma("sp", lambda e, dst=dst, scr=scr, j=j: e.dma_start(out=scr[j].rearrange("p (c n) -> p c n", c=8), in_=dst), reads=[nm], writes=[uid("sg")])
        P.barrier()
        A.release()

        wstage["on"] = False
        A.n = ARENA_BYTES
        A.mark()
        def pt16():
            return A.f32(16)
        lr = pt16(); li = pt16(); ldt = pt16(); dt_ = pt16(); rr = pt16(); th = pt16(); cth = pt16(); sth = pt16()
        abre = pt16(); abim = pt16(); nabim = pt16(); zr = pt16(); zi = pt16(); den = pt16(); q1 = pt16(); q2 = pt16(); nr = pt16()
        Fr = pt16(); Fi = pt16(); a512 = pt16(); init_re = pt16(); init_im = pt16(); fin_re = pt16(); fin_im = pt16()
        dsk = A.f32(4)
        Bre = A.bf16(16, 128); Bim = A.bf16(16, 128); Cre = A.bf16(16, 128); Cim = A.bf16(16, 128)
        iotaj = A.f32(512)
        tabs = [[A.f32(512) for _ in range(4)] for _ in range(4)]
        rt = [A.f32(512) for _ in range(4)]
        wi_re = A.f32(512); wi_im = A.f32(512); w_re = A.f32(512); w_im = A.f32(512); u1 = A.f32(512); u2 = A.f32(512)
        xre = A.bf16(4, 512); ximn = A.bf16(4, 512)
        ysb = wi_re; yown = wi_im[:, 0:256].rearrange("p (a n) -> p a n", a=2)
        _o = A.top; ki = A.i32(512); p1 = A.f[:, _o // 4:_o // 4 + 512]; kfl = u2; ang = u1; ang2 = wi_re
        sre0 = A.f32(16, 16); sim0 = A.f32(16, 16); sso_re = A.f32(16, 16); sso_im = A.f32(16, 16)
        bzr = A.f32(128); bzi = A.f32(128); xsr = w_re[:, 0:128]; xsi = w_im[:, 0:128]; st4 = [A.f32(16) for _ in range(4)]
        tq = A.f32(16); tq2 = A.f32(16)
        for dst, src, nm in ((lr, lam_re, "lr"), (li, lam_im, "li"), (ldt, logdt, "ldt"), (dsk, dsk_d, "dsk"), (iotaj, iotaj_d, "iotaj"),
                             (sre0, st_re, "sre0"), (sim0, st_im, "sim0")):
            P.dma("sp", lambda e, dst=dst, src=src: e.dma_start(out=dst, in_=src), writes=[nm])
        for dst, src, nm in ((Bre, bre_d, "Bre"), (Bim, bim_d, "Bim"), (Cre, cre_d, "Cre"), (Cim, cim_d, "Cim")):
            for hlf in range(2):
                P.dma("pool", lambda e, dst=dst, src=src, hlf=hlf: e.dma_start(out=dst[:, hlf * 8:(hlf + 1) * 8, :], in_=src[hlf * 8:(hlf + 1) * 8].rearrange("t p n -> p t n")), writes=[nm])

        def dve(fn, reads, writes):
            P.op("dve", fn, reads=reads, writes=writes)

        dve(lambda e: e.tensor_scalar(Cim, Cim, -1.0, None, ALU.mult), ["Cim"], ["Cim"])

        def sin_of(a_ap, a_res, out, out_res, n, shift=0.0):
            kiv, kfv, av = ki[:, 0:n], kfl[:, 0:n], ang2[:, 0:n]
            dve(lambda e: e.tensor_scalar(av, a_ap, shift, None, ALU.add), [a_res], ["wi_re"])
            dve(lambda e: e.tensor_scalar(kiv, av, 1.0 / TWO_PI, None, ALU.mult), ["wi_re"], ["ki"])
            dve(lambda e: e.tensor_copy(kfv, kiv), ["ki"], ["u2"])
            dve(lambda e: e.scalar_tensor_tensor(av, kfv, -TWO_PI, av, ALU.mult, ALU.add), ["u2", "wi_re"], ["wi_re"])
            dve(lambda e: e.tensor_scalar(av, av, -PI, PI, ALU.max, ALU.min), ["wi_re"], ["wi_re"])
            P.op("act", lambda e: e.activation(out, av, AF.Sin), reads=["wi_re"], writes=[out_res])

        P.op("act", lambda e: e.activation(dt_, ldt, AF.Exp), reads=["ldt"], writes=["dt"])
        dve(lambda e: e.tensor_tensor(q1, lr, dt_, ALU.mult), ["lr", "dt"], ["q1"])
        P.op("act", lambda e: e.activation(rr, q1, AF.Exp), reads=["q1"], writes=["rr"])
        dve(lambda e: e.tensor_tensor(th, li, dt_, ALU.mult), ["li", "dt"], ["th"])
        dve(lambda e: e.tensor_scalar(ki[:, 0:16], th, 1.0 / TWO_PI, None, ALU.mult), ["th"], ["ki"])
        dve(lambda e: e.tensor_copy(kfl[:, 0:16], ki[:, 0:16]), ["ki"], ["u2"])
        dve(lambda e: e.scalar_tensor_tensor(th, kfl[:, 0:16], -TWO_PI, th, ALU.mult, ALU.add), ["u2", "th"], ["th"])
        sin_of(th, "th", sth, "sth", 16)
        sin_of(th, "th", cth, "cth", 16, shift=PI / 2)
        dve(lambda e: e.tensor_tensor(abre, rr, cth, ALU.mult), ["rr", "cth"], ["abre"])
        dve(lambda e: e.tensor_tensor(abim, rr, sth, ALU.mult), ["rr", "sth"], ["abim"])
        dve(lambda e: e.tensor_scalar(nabim, abim, -1.0, None, ALU.mult), ["abim"], ["nabim"])
        dve(lambda e: e.tensor_tensor(den, lr, lr, ALU.mult), ["lr"], ["den"])
        dve(lambda e: e.tensor_tensor(q2, li, li, ALU.mult), ["li"], ["q2"])
        dve(lambda e: e.tensor_tensor(den, den, q2, ALU.add), ["den", "q2"], ["den"])
        dve(lambda e: e.reciprocal(den, den), ["den"], ["den"])
        dve(lambda e: e.tensor_scalar(nr, abre, -1.0, None, ALU.add), ["abre"], ["nr"])
        dve(lambda e: e.tensor_tensor(q1, nr, lr, ALU.mult), ["nr", "lr"], ["q1"])
        dve(lambda e: e.tensor_tensor(q2, abim, li, ALU.mult), ["abim", "li"], ["q2"])
        dve(lambda e: e.tensor_tensor(q1, q1, q2, ALU.add), ["q1", "q2"], ["q1"])
        dve(lambda e: e.tensor_tensor(zr, q1, den, ALU.mult), ["q1", "den"], ["zr"])
        dve(lambda e: e.tensor_tensor(q1, abim, lr, ALU.mult), ["abim", "lr"], ["q1"])
        dve(lambda e: e.tensor_tensor(q2, nr, li, ALU.mult), ["nr", "li"], ["q2"])
        dve(lambda e: e.tensor_tensor(q1, q1, q2, ALU.subtract), ["q1", "q2"], ["q1"])
        dve(lambda e: e.tensor_tensor(zi, q1, den, ALU.mult), ["q1", "den"], ["zi"])
        dve(lambda e: e.tensor_scalar(a512, th, 512.0, None, ALU.mult), ["th"], ["a512"])
        sin_of(a512, "a512", Fi, "Fi", 16)
        sin_of(a512, "a512", Fr, "Fr", 16, shift=PI / 2)

        for cc in range(4):
            for tl in range(4):
                ti = cc * 4 + tl
                c_t, s_t, er_t, ei_t = tabs[tl]
                tr = f"tab{tl}"
                dve(lambda e, ti=ti: e.tensor_scalar(ang, iotaj, th[:, ti:ti + 1], None, ALU.mult), ["iotaj", "th"], ["u1"])
                sin_of(ang, "u1", s_t, tr + "s", 512)
                sin_of(ang, "u1", c_t, tr + "c", 512, shift=PI / 2)
                dve(lambda e, ti=ti, c_t=c_t: e.tensor_scalar(u1, c_t, zr[:, ti:ti + 1], None, ALU.mult), [tr + "c", "zr"], ["u1"])
                dve(lambda e, ti=ti, s_t=s_t, er_t=er_t: e.scalar_tensor_tensor(er_t, s_t, zi[:, ti:ti + 1], u1, ALU.mult, ALU.add), [tr + "s", "zi", "u1"], [tr + "er"])
                dve(lambda e, ti=ti, s_t=s_t: e.tensor_scalar(u1, s_t, zr[:, ti:ti + 1], None, ALU.mult), [tr + "s", "zr"], ["u1"])
                dve(lambda e, ti=ti, c_t=c_t, ei_t=ei_t: e.scalar_tensor_tensor(ei_t, c_t, zi[:, ti:ti + 1], u1, ALU.mult, ALU.subtract), [tr + "c", "zi", "u1"], [tr + "ei"])
                dve(lambda e, ti=ti, tl=tl: e.tensor_scalar(rt[tl], ones_f.to_broadcast([128, 512]) if False else iotaj, 0.0, rr[:, ti:ti + 1], ALU.mult, ALU.add), ["iotaj", "rr"], [f"rt{tl}"])
                dve(lambda e, ti=ti: e.memset(init_re[:, ti:ti + 1], 0.0), [], [f"ire{ti}"])
                dve(lambda e, ti=ti: e.memset(init_im[:, ti:ti + 1], 0.0), [], [f"iim{ti}"])
            for c8 in range(8):
                t0 = c8 * 512
                for tl in range(4):
                    ti = cc * 4 + tl
                    c_t, s_t, er_t, ei_t = tabs[tl]
                    tr = f"tab{tl}"
                    P.op("pe", lambda e, ti=ti, t0=t0, cc=cc: e.matmul(psf[2][:, 0:512], Bre[:, ti, :], ST[:, cc, t0:t0 + 512], start=True, stop=True), reads=["Bre"], writes=[PSN(2)])
                    P.op("pe", lambda e, ti=ti, t0=t0, cc=cc: e.matmul(psf[3][:, 0:512], Bim[:, ti, :], ST[:, cc, t0:t0 + 512], start=True, stop=True), reads=["Bim"], writes=[PSN(3)])
                    dve(lambda e, er_t=er_t: e.tensor_tensor(u1, psf[2][:, 0:512], er_t, ALU.mult), [PSN(2), tr + "er"], ["u1"])
                    dve(lambda e, ei_t=ei_t: e.tensor_tensor(u2, psf[3][:, 0:512], ei_t, ALU.mult), [PSN(3), tr + "ei"], ["u2"])
                    dve(lambda e: e.tensor_tensor(wi_re, u1, u2, ALU.subtract), ["u1", "u2"], ["wi_re"])
                    dve(lambda e, er_t=er_t: e.tensor_tensor(u1, psf[3][:, 0:512], er_t, ALU.mult), [PSN(3), tr + "er"], ["u1"])
                    dve(lambda e, ei_t=ei_t: e.tensor_tensor(u2, psf[2][:, 0:512], ei_t, ALU.mult), [PSN(2), tr + "ei"], ["u2"])
                    dve(lambda e: e.tensor_tensor(wi_im, u1, u2, ALU.add), ["u1", "u2"], ["wi_im"])
                    dve(lambda e, ti=ti, tl=tl: e.tensor_tensor_scan(w_re, rt[tl], wi_re, init_re[:, ti:ti + 1], ALU.mult, ALU.add), [f"rt{tl}", "wi_re", f"ire{ti}"], ["w_re"])
                    dve(lambda e, ti=ti, tl=tl: e.tensor_tensor_scan(w_im, rt[tl], wi_im, init_im[:, ti:ti + 1], ALU.mult, ALU.add), [f"rt{tl}", "wi_im", f"iim{ti}"], ["w_im"])
                    if c8 < 7:
                        dve(lambda e, ti=ti: e.tensor_scalar(tq[:, 0:1], w_im[:, 511:512], Fi[:, ti:ti + 1], None, ALU.mult), ["w_im", "Fi"], ["tq"])
                        dve(lambda e, ti=ti: e.scalar_tensor_tensor(init_re[:, ti:ti + 1], w_re[:, 511:512], Fr[:, ti:ti + 1], tq[:, 0:1], ALU.mult, ALU.subtract), ["w_re", "Fr", "tq"], [f"ire{ti}"])
                        dve(lambda e, ti=ti: e.tensor_scalar(tq[:, 0:1], w_re[:, 511:512], Fi[:, ti:ti + 1], None, ALU.mult), ["w_re", "Fi"], ["tq"])
                        dve(lambda e, ti=ti: e.scalar_tensor_tensor(init_im[:, ti:ti + 1], w_im[:, 511:512], Fr[:, ti:ti + 1], tq[:, 0:1], ALU.mult, ALU.add), ["w_im", "Fr", "tq"], [f"iim{ti}"])
                    dve(lambda e, c_t=c_t: e.tensor_tensor(u1, c_t, w_re, ALU.mult), [tr + "c", "w_re"], ["u1"])
                    dve(lambda e, s_t=s_t: e.tensor_tensor(u2, s_t, w_im, ALU.mult), [tr + "s", "w_im"], ["u2"])
                    dve(lambda e, tl=tl: e.tensor_tensor(xre[:, tl, :], u1, u2, ALU.subtract), ["u1", "u2"], [f"xre{tl}"])
                    if c8 == 7:
                        dve(lambda e, ti=ti: e.tensor_tensor(fin_re[:, ti:ti + 1], u1[:, 511:512], u2[:, 511:512], ALU.subtract), ["u1", "u2"], ["fin_re"])
                    P.op("pool", lambda e, s_t=s_t: e.tensor_tensor(p1, s_t, w_re, ALU.mult), reads=[tr + "s", "w_re"], writes=["ki"])
                    P.op("pool", lambda e, c_t=c_t, tl=tl: e.tensor_tensor(ximn[:, tl, :], c_t, w_im, ALU.mult), reads=[tr + "c", "w_im"], writes=[f"ximn{tl}"])
                    P.op("pool", lambda e, tl=tl: e.tensor_tensor(ximn[:, tl, :], ximn[:, tl, :], p1, ALU.add), reads=["ki", f"ximn{tl}"], writes=[f"ximn{tl}"])
                    if c8 == 7:
                        dve(lambda e, ti=ti, s_t=s_t: e.tensor_tensor(tq[:, 0:1], s_t[:, 511:512], w_re[:, 511:512], ALU.mult), [tr + "s", "w_re"], ["tq"])
                        dve(lambda e, ti=ti, c_t=c_t: e.scalar_tensor_tensor(fin_im[:, ti:ti + 1], c_t[:, 511:512], w_im[:, 511:512], tq[:, 0:1], ALU.mult, ALU.add), [tr + "c", "w_im", "tq"], ["fin_im"])
                for tl in range(4):
                    ti = cc * 4 + tl
                    P.op("pe", lambda e, ti=ti, tl=tl: e.matmul(psf[4][:, 0:512], Cre[:, ti, :], xre[:, tl, :], start=(tl == 0), stop=False), reads=["Cre", f"xre{tl}"], writes=[PSN(4)])
                    P.op("pe", lambda e, ti=ti, tl=tl: e.matmul(psf[4][:, 0:512], Cim[:, ti, :], ximn[:, tl, :], start=False, stop=(tl == 3)), reads=["Cim", f"ximn{tl}"], writes=[PSN(4)])
                dve(lambda e, cc=cc, t0=t0: e.scalar_tensor_tensor(ysb, ST[:, cc, t0:t0 + 512], dsk[:, cc:cc + 1], psf[4][:, 0:512], ALU.mult, ALU.add), ["dsk", PSN(4)], ["wi_re"])
                yv = ysb.rearrange("p (a b n) -> p a b n", a=2, b=2)
                dve(lambda e, yv=yv: e.tensor_scalar(u1[:, 0:256].rearrange("p (a n) -> p a n", a=2), yv[:, :, 0, :], p0p1[:, 0:1], None, ALU.mult), ["wi_re"], ["u1"])
                dve(lambda e, yv=yv: e.scalar_tensor_tensor(yown, yv[:, :, 1, :], p0p1[:, 1:2], u1[:, 0:256].rearrange("p (a n) -> p a n", a=2), ALU.mult, ALU.add), ["wi_re", "u1"], ["wi_im"])
                for jj in range(2):
                    P.dma("sp", lambda e, jj=jj, c8=c8, cc=cc: e.dma_start(out=YT[2 * c8 + jj, :, cc * 128:(cc + 1) * 128], in_=yown[:, jj, :]), reads=["wi_im"], writes=[uid("YT")])
            for tl in range(4):
                ti = cc * 4 + tl
                P.op("pe", lambda e, ti=ti, cc=cc: e.matmul(psf[2][:, 0:128], Bre[:, ti, :], ST[:, cc, SEQ:SEQ + 128], start=True, stop=True), reads=["Bre"], writes=[PSN(2)])
                P.op("pe", lambda e, ti=ti, cc=cc: e.matmul(psf[3][:, 0:128], Bim[:, ti, :], ST[:, cc, SEQ:SEQ + 128], start=True, stop=True), reads=["Bim"], writes=[PSN(3)])
                dve(lambda e, ti=ti: e.tensor_scalar(u1[:, 0:128], psf[3][:, 0:128], zi[:, ti:ti + 1], None, ALU.mult), [PSN(3), "zi"], ["u1"])
                dve(lambda e, ti=ti: e.scalar_tensor_tensor(bzr, psf[2][:, 0:128], zr[:, ti:ti + 1], u1[:, 0:128], ALU.mult, ALU.subtract), [PSN(2), "zr", "u1"], ["bzr"])
                dve(lambda e, ti=ti: e.tensor_scalar(u1[:, 0:128], psf[2][:, 0:128], zi[:, ti:ti + 1], None, ALU.mult), [PSN(2), "zi"], ["u1"])
                dve(lambda e, ti=ti: e.scalar_tensor_tensor(bzi, psf[3][:, 0:128], zr[:, ti:ti + 1], u1[:, 0:128], ALU.mult, ALU.add), [PSN(3), "zr", "u1"], ["bzi"])
                bzr_v = bzr.rearrange("p (n i) -> p n i", i=8); bzi_v = bzi.rearrange("p (n i) -> p n i", i=8)
                xsr_v = xsr.rearrange("p (n i) -> p n i", i=8); xsi_v = xsi.rearrange("p (n i) -> p n i", i=8)
                for i in range(8):
                    pre = sre0[:, ti, :] if i == 0 else xsr_v[:, :, i - 1]
                    pim = sim0[:, ti, :] if i == 0 else xsi_v[:, :, i - 1]
                    rd = ["sre0", "sim0", "w_re", "w_im", "bzr", "bzi", "abre", "abim", "nabim"]
                    dve(lambda e, ti=ti, i=i, pim=pim: e.scalar_tensor_tensor(tq, pim, nabim[:, ti:ti + 1], bzr_v[:, :, i], ALU.mult, ALU.add), rd, ["tq"])
                    dve(lambda e, ti=ti, i=i, pre=pre: e.scalar_tensor_tensor(tq2, pre, abim[:, ti:ti + 1], bzi_v[:, :, i], ALU.mult, ALU.add), rd, ["tq2"])
                    dve(lambda e, ti=ti, i=i, pre=pre: e.scalar_tensor_tensor(xsr_v[:, :, i], pre, abre[:, ti:ti + 1], tq, ALU.mult, ALU.add), rd + ["tq"], ["w_re"])
                    dve(lambda e, ti=ti, i=i, pim=pim: e.scalar_tensor_tensor(xsi_v[:, :, i], pim, abre[:, ti:ti + 1], tq2, ALU.mult, ALU.add), rd + ["tq2"], ["w_im"])
                dve(lambda e, ti=ti: e.tensor_copy(sso_re[:, ti, :], xsr_v[:, :, 7]), ["w_re"], ["sso_re"])
                dve(lambda e, ti=ti: e.tensor_copy(sso_im[:, ti, :], xsi_v[:, :, 7]), ["w_im"], ["sso_im"])
                dve(lambda e, tl=tl: e.tensor_copy(xre[:, tl, 0:128], xsr), ["w_re"], [f"xre{tl}"])
                dve(lambda e, tl=tl: e.tensor_copy(ximn[:, tl, 0:128], xsi), ["w_im"], [f"ximn{tl}"])
            for tl in range(4):
                ti = cc * 4 + tl
                P.op("pe", lambda e, ti=ti, tl=tl: e.matmul(psf[4][:, 0:128], Cre[:, ti, :], xre[:, tl, 0:128], start=(tl == 0), stop=False), reads=["Cre", f"xre{tl}"], writes=[PSN(4)])
                P.op("pe", lambda e, ti=ti, tl=tl: e.matmul(psf[4][:, 0:128], Cim[:, ti, :], ximn[:, tl, 0:128], start=False, stop=(tl == 3)), reads=["Cim", f"ximn{tl}"], writes=[PSN(4)])
            dve(lambda e, cc=cc: e.scalar_tensor_tensor(ysb[:, 0:128], ST[:, cc, SEQ:SEQ + 128], dsk[:, cc:cc + 1], psf[4][:, 0:128], ALU.mult, ALU.add), ["dsk", PSN(4)], ["wi_re"])
            P.dma("sp", lambda e, cc=cc: e.dma_start(out=YT[16, :, cc * 128:(cc + 1) * 128], in_=ysb[:, 0:128]), reads=["wi_re"], writes=[uid("YT")])
        P.dma("sp", lambda e: e.dma_start(out=spre_o, in_=fin_re), reads=["fin_re"], writes=[uid("o")])
        P.dma("sp", lambda e: e.dma_start(out=spim_o, in_=fin_im), reads=["fin_im"], writes=[uid("o")])
        P.dma("sp", lambda e: e.dma_start(out=ssre_o, in_=sso_re), reads=["sso_re"], writes=[uid("o")])
        P.dma("sp", lambda e: e.dma_start(out=ssim_o, in_=sso_im), reads=["sso_im"], writes=[uid("o")])
        P.barrier()
        A.release()
        A.release()

        A.n = ARENA_BYTES - OT_BYTES
        A.mark()
        amask = A.bf16(8, 512)
        for m in range(8):
            P.dma("pool", lambda e, m=m: e.dma_start(out=amask[:, m, :], in_=amask_d[m]), writes=["amask"])
        HC = []
        for hi in range(2):
            HC.append(dict(E=[A.f32(512) for _ in range(3)], SP=[A.bf16(512), A.bf16(512)], SPf=A.f32(512), R=A.bf16(512),
                           T=[A.f32(512), A.f32(512)], W=[A.bf16(512), A.bf16(512)], Wf=A.f32(512), psS=hi, psC=[2 + 2 * hi, 3 + 2 * hi], hi=hi))

        def stA(hc, g, h, kb, k):
            hi = hc["hi"]
            c, hb = h // 2, (h % 2) * 64
            ps_s = hc["psS"]
            E = hc["E"][k % 3]; SP = hc["SP"][k % 2]; SPf = hc["SPf"]
            en = f"E{hi}{k % 3}"; spn = f"SP{hi}{k % 2}"
            P.op("pe", lambda e: e.matmul(psf[ps_s][:, 0:512], KT[hb:hb + 64, c, kb * 128:(kb + 1) * 128], QT[hb:hb + 64, c, g * 512:(g + 1) * 512], start=True, stop=True),
                 reads=[], writes=[PSN(ps_s)])
            P.op("act", lambda e: e.activation(E, psf[ps_s][:, 0:512], AF.Exp, scale=0.125, bias=sbb[:, h:h + 1]), reads=[PSN(ps_s)], writes=[en])
            if kb >= 8 * g:
                m = kb - 8 * g
                P.op("act", lambda e: e.activation(SPf, E, AF.Ln, bias=1.0), reads=[en], writes=[f"SPf{hi}"])
                P.op("dve", lambda e: e.tensor_tensor(SP, SPf, amask[:, m, :], ALU.mult), reads=[f"SPf{hi}", "amask"], writes=[spn])
            else:
                P.op("act", lambda e: e.activation(SP, E, AF.Ln, bias=1.0), reads=[en], writes=[spn])

        def stB(hc, k, first, last):
            hi = hc["hi"]
            ps_c = hc["psC"][k % 2]
            SP = hc["SP"][k % 2]; R = hc["R"]; spn = f"SP{hi}{k % 2}"
            P.op("pe", lambda e: e.matmul(psf[ps_c][:, 0:512], ltri_b, SP, start=True, stop=first), reads=[spn], writes=[PSN(ps_c)])
            if not first:
                P.op("pe", lambda e: e.matmul(psf[ps_c][:, 0:512], ones_b, R, start=False, stop=True), reads=[f"R{hi}"], writes=[PSN(ps_c)])
            if not last:
                if first:
                    P.op("pool", lambda e: e.tensor_copy(R, SP), reads=[spn], writes=[f"R{hi}"])
                else:
                    P.op("pool", lambda e: e.tensor_tensor(R, R, SP, ALU.add), reads=[spn, f"R{hi}"], writes=[f"R{hi}"])

        def stC(hc, g, h, kb, k):
            hi = hc["hi"]
            ps_c = hc["psC"][k % 2]
            E = hc["E"][k % 3]; T = hc["T"][k % 2]; W = hc["W"][k % 2]; Wf = hc["Wf"]
            en = f"E{hi}{k % 3}"; tn = f"T{hi}{k % 2}"; wn = f"W{hi}{k % 2}"
            P.op("act", lambda e: e.activation(T, psf[ps_c][:, 0:512], AF.Exp, scale=-1.0), reads=[PSN(ps_c)], writes=[tn])
            if kb >= 8 * g:
                m = kb - 8 * g
                P.op("dve", lambda e: e.tensor_tensor(Wf, E, T, ALU.mult), reads=[en, tn], writes=[f"Wf{hi}"])
                P.op("dve", lambda e: e.tensor_tensor(W, Wf, amask[:, m, :], ALU.mult), reads=[f"Wf{hi}", "amask"], writes=[wn])
            else:
                P.op("dve", lambda e: e.tensor_tensor(W, E, T, ALU.mult), reads=[en, tn], writes=[wn])

        def stD(hc, h, kb, k, first, last):
            hi = hc["hi"]
            hb = (h % 2) * 64
            W = hc["W"][k % 2]; wn = f"W{hi}{k % 2}"
            P.op("pe", lambda e: e.matmul(psf[6][hb:hb + 64, 0:512], VB[:, kb, h * 64:(h + 1) * 64], W, start=first, stop=last),
                 reads=[wn], writes=[f"psO{hi}"])

        for g in range(4):
            nkb = 8 * (g + 1)
            kbs = list(range(nkb - 1, -1, -1))
            n = len(kbs)
            for hp in range(4):
                hs = [2 * hp, 2 * hp + 1]
                for i in range(n + 2):
                    if 0 <= i - 2 < n:
                        for hi in range(2):
                            stC(HC[hi], g, hs[hi], kbs[i - 2], i - 2)
                    if i < n:
                        for hi in range(2):
                            stA(HC[hi], g, hs[hi], kbs[i], i)
                    if 0 <= i - 1 < n:
                        for hi in range(2):
                            stB(HC[hi], i - 1, i - 1 == 0, i - 1 == n - 1)
                    if 0 <= i - 2 < n:
                        for hi in range(2):
                            stD(HC[hi], hs[hi], kbs[i - 2], i - 2, i - 2 == 0, i - 2 == n - 1)
                for hi in range(2):
                    h = hs[hi]; c, hb = h // 2, (h % 2) * 64
                    P.op("act", lambda e, c=c, hb=hb, g=g: e.copy(OT[hb:hb + 64, c, g * 512:(g + 1) * 512], psf[6][hb:hb + 64, 0:512]), reads=[f"psO{hi}"], writes=[uid("OT")])
        P.barrier()
        A.release()
        A.release()

        A.mark()
        sbrow = A.f32(1); smask = A.f32(16, 128)
        pti = A.i32(256); ptf = A.f32(256); idx = A.i32(256)
        Qbd = A.bf16(4, 16, 16)
        NR = 4
        Kf = A.f32(16, 512); Vf = A.f32(16, 512); Kb = A.bf16(NR, 512); Vb = A.bf16(16, 512)
        KTs = A.bf16(4, 2176)
        NK = 2176
        Es = A.f32(NK); SPs = A.f32(NK); Fs = A.f32(NK); zer = A.f32(NK); Wsb = A.bf16(NK)
        WT = A.bf16(17, 128)
        ntot = A.f32(1); VBnb = A.bf16(512)
        P.dma("sp", lambda e: e.dma_start(out=sbrow, in_=di["sb_rows"]), writes=["sbrow"])
        P.dma("sp", lambda e: e.dma_start(out=smask, in_=smask_d), writes=["smask"])
        P.dma("sp", lambda e: e.dma_start(out=pti, in_=ptab.partition_broadcast(128)), writes=["pti"])
        dve(lambda e: e.tensor_copy(ptf, pti), ["pti"], ["ptf"])
        dve(lambda e: e.tensor_scalar(ptf, ptf, 128.0, iota_p[:, 0:1], ALU.mult, ALU.add), ["ptf"], ["ptf"])
        dve(lambda e: e.tensor_copy(idx, ptf), ["ptf"], ["idx"])
        dve(lambda e: e.memset(zer, 0.0), [], ["zer"])
        dve(lambda e: e.memset(Qbd, 0.0), [], ["Qbd"])
        dve(lambda e: e.tensor_copy(VBnb, VBn), [], ["VBnb"])
        for c in range(4):
            for hh in range(2):
                dve(lambda e, c=c, hh=hh: e.tensor_copy(Qbd[hh * 64:(hh + 1) * 64, c, :, hh * 8:(hh + 1) * 8],
                                                        QTs[hh * 64:(hh + 1) * 64, c, :].rearrange("p (n i) -> p n i", i=8)), ["Qbd"], ["Qbd"])
        dve(lambda e: e.tensor_copy(KTs[:, :, 2048:2176], KTn), [], ["KTnew"])
        gcount = {"k": 0, "v": 0}

        def k_gather(n):
            for pg in range(16):
                col = n * 16 + pg
                sl = pg
                P.dma("pool", lambda e, sl=sl, col=col: e.indirect_dma_start(out=Kf[:, sl, :], out_offset=None, in_=cache_k, in_offset=bass.IndirectOffsetOnAxis(ap=idx[:, col:col + 1], axis=0)),
                      reads=["idx"], writes=[f"Kf{sl}"])

        def v_gather(n):
            for pg in range(16):
                col = n * 16 + pg
                sl = pg
                P.dma("pool", lambda e, sl=sl, col=col: e.indirect_dma_start(out=Vf[:, sl, :], out_offset=None, in_=cache_v, in_offset=bass.IndirectOffsetOnAxis(ap=idx[:, col:col + 1], axis=0)),
                      reads=["idx"], writes=[f"Vf{sl}"])

        def k_process(n):
            for pg in range(16):
                sl = pg % NR
                dve(lambda e, sl=sl, pg=pg: e.tensor_copy(Kb[:, sl, :], Kf[:, pg, :]), [f"Kf{pg}"], [f"Kb{sl}"])
                for c in range(4):
                    P.op("pe", lambda e, sl=sl, c=c: e.transpose(psb[7][:, c * 128:(c + 1) * 128], Kb[:, sl, c * 128:(c + 1) * 128], ident_b), reads=[f"Kb{sl}"], writes=[PSN(7)])
                P.op("act", lambda e, pg=pg: e.copy(KTs[:, :, pg * 128:(pg + 1) * 128], psb[7][:, 0:512].rearrange("p (c n) -> p c n", c=4)), reads=[PSN(7)], writes=[f"KTs{pg}"])

        def v_process(n):
            for pg in range(16):
                P.op("act", lambda e, pg=pg: e.copy(Vb[:, pg, :], Vf[:, pg, :]), reads=[f"Vf{pg}"], writes=[f"Vb{pg}"])

        ktr = [f"KTs{pg}" for pg in range(16)] + ["KTnew"]
        k_gather(0); v_gather(0)
        k_process(0); v_process(0)
        for n in range(16):
            for cb in range(5):
                w = 512 if cb < 4 else 128
                for c in range(4):
                    P.op("pe", lambda e, c=c, cb=cb, w=w, n=n: e.matmul(psf[cb][32 * c:32 * c + 16, 0:w], Qbd[:, c, n, :], KTs[:, c, cb * 512:cb * 512 + w], start=True, stop=True, tile_position=(0, 32 * c)),
                         reads=ktr + ["Qbd"], writes=[PSN(cb)])
                P.op("act", lambda e, cb=cb, w=w: e.activation(Es[:, cb * 512:cb * 512 + w], psf[cb][:, 0:w], AF.Exp, scale=0.125, bias=sbrow), reads=[PSN(cb), "sbrow"], writes=["Es"])
            if n + 1 < 16:
                k_gather(n + 1); v_gather(n + 1)
            P.op("act", lambda e: e.activation(SPs, Es, AF.Ln, bias=1.0), reads=["Es"], writes=["SPs"])
            dve(lambda e, n=n: e.tensor_tensor(SPs[:, 2048:2176], SPs[:, 2048:2176], smask[:, n, :], ALU.mult), ["SPs", "smask"], ["SPs"])
            dve(lambda e: e.tensor_tensor_scan(Fs, zer, SPs, 0.0, ALU.add, ALU.add), ["zer", "SPs"], ["Fs"])
            dve(lambda e: e.tensor_scalar(ntot, Fs[:, NK - 1:NK], -1.0, None, ALU.mult), ["Fs"], ["ntot"])
            dve(lambda e: e.tensor_tensor(Fs, Fs, SPs, ALU.subtract), ["Fs", "SPs"], ["Fs"])
            P.op("act", lambda e: e.activation(SPs, Fs, AF.Exp, bias=ntot), reads=["Fs", "ntot"], writes=["SPs"])
            dve(lambda e: e.tensor_tensor(Wsb[:, 0:2048], Es[:, 0:2048], SPs[:, 0:2048], ALU.mult), ["Es", "SPs"], ["Wsb"])
            dve(lambda e: e.tensor_tensor(Fs[:, 0:128], Es[:, 2048:2176], SPs[:, 2048:2176], ALU.mult), ["Es", "SPs"], ["Fs"])
            dve(lambda e, n=n: e.tensor_tensor(Wsb[:, 2048:2176], Fs[:, 0:128], smask[:, n, :], ALU.mult), ["Fs", "smask"], ["Wsb"])
            if n + 1 < 16:
                k_process(n + 1)
            for blk in range(17):
                P.op("pe", lambda e, blk=blk: e.transpose(psb[5][:, (blk % 8) * 128:(blk % 8 + 1) * 128], Wsb[:, blk * 128:(blk + 1) * 128], ident_b), reads=["Wsb"], writes=[PSN(5)])
                if blk % 8 == 7 or blk == 16:
                    b0 = blk - (blk % 8); nb = blk % 8 + 1
                    P.op("act", lambda e, b0=b0, nb=nb: e.copy(WT[:, b0:b0 + nb, :], psb[5][:, 0:nb * 128].rearrange("p (b n) -> p b n", b=nb)), reads=[PSN(5)], writes=["WT"])
            for c in range(4):
                for blk in range(17):
                    vsrc = Vb[:, blk, c * 128:(c + 1) * 128] if blk < 16 else VBnb[:, c * 128:(c + 1) * 128]
                    P.op("pe", lambda e, c=c, blk=blk, vsrc=vsrc: e.matmul(psf[6][:, c * 16:(c + 1) * 16], vsrc, WT[:, blk, 32 * c:32 * c + 16], start=(blk == 0), stop=(blk == 16)),
                         reads=["WT"] + ([f"Vb{blk}"] if blk < 16 else ["VBnb"]), writes=[PSN(6)])
            for hh in range(2):
                P.op("act", lambda e, hh=hh, n=n: e.copy(OT[hh * 64:(hh + 1) * 64, :, NOWN * 128 + 8 * n:NOWN * 128 + 8 * n + 8],
                                                         psf[6][hh * 64:(hh + 1) * 64, 0:64].rearrange("p (c k) -> p c k", c=4)[:, :, hh * 8:(hh + 1) * 8]), reads=[PSN(6)], writes=[uid("OTs")])
            if n + 1 < 16:
                v_process(n + 1)
        P.barrier()
        A.release()

        A.mark()
        Wa = A.bf16(4, D); Wbb = A.bf16(4, D); Wg = A.bf16(4, 512); Wo = A.bf16(8, D)
        load_w_rows(Wa, wba, 4, "Wa"); load_w_rows(Wbb, wbb, 4, "Wbb"); load_w_rows(Wg, glu_w, 4, "Wg"); load_w_rows(Wo, w_o, 8, "Wo")
        glub = A.f32(4); idxo = A.i32(16)
        P.dma("sp", lambda e: e.dma_start(out=glub, in_=glub_d), writes=["glub"])
        P.dma("sp", lambda e: e.dma_start(out=idxo, in_=idxown_d), writes=["idxo"])
        g2r = A.f32(D); sh3 = A.f32(D); A3 = A.f32(D); grow = A.f32(D)
        yt = A.f32(512); y2 = A.f32(512); zg = A.f32(512); sgm = A.f32(512); zgb = A.bf16(4, 128)
        obT = [A.bf16(4, 128), A.bf16(4, 128)]
        sga = [A.bf16(8, 128), A.bf16(8, 128)]; sgb = [A.bf16(8, 128), A.bf16(8, 128)]
        m1 = A.f32(512); m2 = A.f32(512); mgT = [A.bf16(8, 128), A.bf16(8, 128)]
        h1t = A.f32(D); tmp512 = A.f32(512); tmpD = A.f32(D); junkD = A.bf16(D); ss1 = A.f32(1); rstd1 = A.f32(1)
        xs3b = A.bf16(D)
        blocks5 = list(range(NOWN + 1))
        n5 = len(blocks5)

        def load_p5_mod(g):
            load_mod(g2r, g, 5, "p5mod"); load_mod(sh3, g, 6, "p5mod"); load_mod(A3, g, 7, "p5mod")
            load_row_bcast(grow, n3, "grow")
            dve(lambda e: e.scalar_tensor_tensor(A3, A3, 1.0, grow, ALU.add, ALU.mult), ["p5mod", "grow"], ["p5mod"])

        load_p5_mod(0)
        for i in range(n5 + 2):
            jx, jy, jz = i, i - 1, i - 2
            hasx, hasy, hasz = jx < n5, 0 <= jy < n5, 0 <= jz < n5
            if hasx:
                P.dma("sp", lambda e, j=jx: e.dma_start(out=yt, in_=YT[j]), writes=["yt"])
            if hasy:
                sy = jy % 2
                P.dma("sp", lambda e, j=jy, sy=sy: e.dma_start(out=sga[sy], in_=SGA[j].rearrange("p (c n) -> p c n", c=8)), writes=[f"sga{sy}"])
                P.dma("sp", lambda e, j=jy, sy=sy: e.dma_start(out=sgb[sy], in_=SGB[j].rearrange("p (c n) -> p c n", c=8)), writes=[f"sgb{sy}"])
            if hasz:
                if jz == NOWN:
                    load_p5_mod(1)
                if jz < NOWN:
                    P.dma("pool", lambda e, j=jz: e.indirect_dma_start(out=h1t, out_offset=None, in_=H1, in_offset=bass.IndirectOffsetOnAxis(ap=idxo[:, j:j + 1], axis=0)), reads=["idxo"], writes=["h1t"])
                else:
                    P.dma("sp", lambda e: e.dma_start(out=h1t, in_=H1[NBLK * 128:(NBLK + 1) * 128, :]), writes=["h1t"])
            if hasx:
                dve(lambda e: e.tensor_tensor(y2, yt, yt, ALU.mult), ["yt"], ["y2"])
                dve(lambda e: e.tensor_scalar(y2, y2, 0.044715, 1.0, ALU.mult, ALU.add), ["y2"], ["y2"])
                dve(lambda e: e.tensor_tensor(y2, y2, yt, ALU.mult), ["y2", "yt"], ["y2"])
                P.op("act", lambda e: e.activation(sgm, y2, AF.Sigmoid, scale=1.5957691216057308), reads=["y2"], writes=["sgm"])
                dve(lambda e: e.tensor_tensor(zg, yt, sgm, ALU.mult), ["yt", "sgm"], ["zg"])
                dve(lambda e: e.tensor_copy(zgb, zg.rearrange("p (c n) -> p c n", c=4)), ["zg"], ["zgb"])
            if hasy:
                sy = jy % 2
                for hh in range(2):
                    for c4 in range(4):
                        dc = hh * 4 + c4
                        for k in range(4):
                            P.op("pe", lambda e, hh=hh, c4=c4, dc=dc, k=k, j=jy: e.matmul(psf[2 + hh][:, c4 * 128:(c4 + 1) * 128], Wa[:, k, dc * 128:(dc + 1) * 128], OT[:, k, j * 128:(j + 1) * 128], start=(k == 0), stop=(k == 3)),
                                 reads=["Wa"], writes=[PSN(2 + hh)])
                        for k in range(4):
                            P.op("pe", lambda e, hh=hh, c4=c4, dc=dc, k=k, sy=sy: e.matmul(psf[4 + hh][:, c4 * 128:(c4 + 1) * 128], Wbb[:, k, dc * 128:(dc + 1) * 128], obT[sy][:, k, :], start=(k == 0), stop=(k == 3)),
                                 reads=["Wbb", f"obT{sy}"], writes=[PSN(4 + hh)])
                    dve(lambda e, hh=hh, sy=sy: e.tensor_tensor(m1, psf[2 + hh][:, 0:512], sga[sy][:, hh * 4:(hh + 1) * 4, :].rearrange("p c n -> p (c n)"), ALU.mult), [PSN(2 + hh), f"sga{sy}"], ["m1"])
                    dve(lambda e, hh=hh, sy=sy: e.tensor_tensor(m2, psf[4 + hh][:, 0:512], sgb[sy][:, hh * 4:(hh + 1) * 4, :].rearrange("p c n -> p (c n)"), ALU.mult), [PSN(4 + hh), f"sgb{sy}"], ["m2"])
                    dve(lambda e, hh=hh, sy=sy: e.tensor_tensor(mgT[sy][:, hh * 4:(hh + 1) * 4, :], m1.rearrange("p (c n) -> p c n", c=4), m2.rearrange("p (c n) -> p c n", c=4), ALU.add), ["m1", "m2"], [f"mgT{sy}"])
            if hasz:
                sz = jz % 2
                for half in range(2):
                    for k in range(8):
                        P.op("pe", lambda e, half=half, k=k, sz=sz: e.matmul(psf[6 + half][:, 0:512], mgT[sz][:, k, :], Wo[:, k, half * 512:(half + 1) * 512], start=(k == 0), stop=(k == 7)), reads=[f"mgT{sz}", "Wo"], writes=[PSN(6 + half)])
                    hs = h1t[:, half * 512:(half + 1) * 512]
                    dve(lambda e, half=half: e.tensor_tensor(tmp512, psf[6 + half][:, 0:512], g2r[:, half * 512:(half + 1) * 512], ALU.mult), [PSN(6 + half), "p5mod"], ["tmp512"])
                    dve(lambda e, hs=hs: e.tensor_tensor(hs, hs, tmp512, ALU.add), ["tmp512", "h1t"], ["h1t"])
                P.dma("sp", lambda e, j=jz: e.dma_start(out=H2[j * 128:(j + 1) * 128, :], in_=h1t), reads=["h1t"], writes=[uid("H2")])
                norm_mod(h1t, "h1t", A3, sh3, xs3b, "xs3b", "p5")
                P.dma("sp", lambda e, j=jz: e.dma_start(out=XS3[j * 128:(j + 1) * 128, :], in_=xs3b), reads=["xs3b"], writes=[uid("XS3")])
            if hasx:
                sx = jx % 2
                for c in range(4):
                    for k in range(4):
                        P.op("pe", lambda e, c=c, k=k: e.matmul(psf[1][:, c * 128:(c + 1) * 128], Wg[:, k, c * 128:(c + 1) * 128], zgb[:, k, :], start=(k == 0), stop=(k == 3)), reads=["Wg", "zgb"], writes=[PSN(1)])
                for c in range(4):
                    P.op("act", lambda e, c=c: e.activation(sgm[:, c * 128:(c + 1) * 128], psf[1][:, c * 128:(c + 1) * 128], AF.Sigmoid, bias=glub[:, c:c + 1]), reads=[PSN(1), "glub", "zg"], writes=["sgm"])
                dve(lambda e, sx=sx: e.tensor_tensor(obT[sx], zg.rearrange("p (c n) -> p c n", c=4), sgm.rearrange("p (c n) -> p c n", c=4), ALU.mult), ["zg", "sgm"], [f"obT{sx}"])
        P.barrier()
        A.release()
        A.release()

        wstage["on"] = False
        A.n = ARENA_BYTES
        A.mark()
        W1 = A.bf16(8, 2 * DFF); W2 = A.bf16(NF, D)
        load_w_cols(W1, f2_in, 0, 2 * DFF, "f2W1", order=[x for f_ in range(NF) for x in (f_ * 128, DFF + f_ * 128)], fine=True)
        load_w_rows(W2, f2_out, NF, "f2W2", fine=True)
        g3h = A.f32(D); fnr = A.f32(D)
        ht = [A.f32(D), A.f32(D)]; xin = [A.bf16(D), A.bf16(D)]
        uT = [A.bf16(8, 128), A.bf16(8, 128)]; actT = A.bf16(NF, 128)
        sgs = [A.f32(128), A.f32(128)]
        tmp512 = A.f32(512); junkD = A.bf16(D); ss1 = A.f32(1); rstd1 = A.f32(1)
        yo = [A.f32(D), A.f32(D)]
        load_row_bcast(fnr, nfin, "fnr")
        for g in range(2):
            load_mod(g3h, g, 8, "f2mod")
            dve(lambda e: e.tensor_scalar(g3h, g3h, 0.5, None, ALU.mult), ["f2mod"], ["f2mod"])
            blocks = list(range(NOWN)) if g == 0 else [NOWN]

            def prep5(j, s):
                P.dma("sp", lambda e: e.dma_start(out=ht[s], in_=H2[j * 128:(j + 1) * 128, :]), writes=[f"ht{s}"])
                P.dma("sp", lambda e: e.dma_start(out=xin[s], in_=XS3[j * 128:(j + 1) * 128, :]), writes=[f"xin{s}"])

            prep5(blocks[0], 0)
            transpose_block(xin[0], "xin0", uT[0], "uT0", 0)
            for i, j in enumerate(blocks):
                s = i % 2
                ffn_mm1(uT[s], f"uT{s}", W1, actT, "f2")
                if i + 1 < len(blocks):
                    prep5(blocks[i + 1], 1 - s)
                ffn_mm2(actT, W2, "f2")
                if i + 1 < len(blocks):
                    transpose_block(xin[1 - s], f"xin{1 - s}", uT[1 - s], f"uT{1 - s}", 0)
                ffn_update(f"ht{s}", ht[s], g3h, "f2")
                rstd_from(ht[s], f"ht{s}", junkD, ss1, rstd1, "fin")
                dve(lambda e, s=s: e.scalar_tensor_tensor(yo[s], ht[s], rstd1, fnr, ALU.mult, ALU.mult), [f"ht{s}", "finrstd", "fnr"], [f"yo{s}"])
                dst = y_p[j * 128:(j + 1) * 128, :] if j < NOWN else y_s
                P.dma("sp", lambda e, s=s, dst=dst: e.dma_start(out=dst, in_=yo[s]), reads=[f"yo{s}"], writes=[uid("y")])
        P.barrier()
        P.emit()
    return nc


_CACHE = {}


def _consts(par):
    c = {}
    c["ident"] = np.eye(128, dtype=np.float32)
    c["iota_p"] = np.arange(128, dtype=np.float32).reshape(128, 1)
    c["iota_j"] = np.ascontiguousarray(np.broadcast_to(np.arange(512, dtype=np.float32)[None, :], (128, 512)))
    jj, ss = np.meshgrid(np.arange(128), np.arange(128), indexing="ij")
    c["ltri"] = (jj >= ss).astype(np.float32)
    am = np.zeros((8, 128, 512), np.float32)
    p = np.arange(128)[:, None]
    t = np.arange(128)[None, :]
    for m in range(8):
        for q in range(4):
            am[m, :, q * 128:(q + 1) * 128] = ((m * 128 + p) < ((2 * q + par) * 128 + t)).astype(np.float32)
    c["amask"] = am
    sm = np.zeros((128, 16, 128), np.float32)
    for cc in range(4):
        for hh in range(2):
            for i in range(8):
                row = 32 * cc + 8 * hh + i
                for n in range(16):
                    sm[row, n, 8 * n:8 * n + i] = 1.0
    c["smask"] = sm
    c["idx_own"] = ((2 * np.arange(16)[None, :] + par) * 128 + np.arange(128)[:, None]).astype(np.int32)
    c["p0p1"] = np.ascontiguousarray(np.broadcast_to(np.array([[1.0 - par, float(par)]], np.float32), (128, 2)))
    return c


def kernel(**inp):
    f = lambda k: np.asarray(inp[k])
    if "nc" not in _CACHE:
        _CACHE["nc"] = build_program()
    nc = _CACHE["nc"]
    x_prompt, x_sample = f("x_prompt"), f("x_sample")
    c_prompt, c_sample = f("c_prompt"), f("c_sample")
    ck = np.ascontiguousarray(f("cache_k")[0].reshape(-1, 512)); cv = np.ascontiguousarray(f("cache_v")[0].reshape(-1, 512))
    page_table = f("page_table").astype(np.int32)
    s_re, s_im = f("state_ssm_re")[0], f("state_ssm_im")[0]

    def st_layout(a):
        return np.ascontiguousarray(a.reshape(16, 16, 2, 64).transpose(2, 3, 1, 0).reshape(128, 16, 16))

    def gp_layout(a):
        return np.ascontiguousarray(a.reshape(16, 2, 64).transpose(1, 2, 0).reshape(128, 16))

    b_re, b_im, c_re, c_im = f("ssm_b_re")[0], f("ssm_b_im")[0], f("ssm_c_re")[0], f("ssm_c_im")[0]

    def bfull(b):
        o = np.zeros((16, 128, 128), np.float32)
        for ti in range(16):
            for g2 in range(2):
                r0 = 32 * (ti % 4) + 16 * g2
                o[ti, r0:r0 + 16, 64 * g2:64 * g2 + 64] = b[2 * ti + g2].T
        return o

    def cfull(cm):
        o = np.zeros((16, 128, 128), np.float32)
        for ti in range(16):
            for g2 in range(2):
                c0 = 32 * (ti % 4) + 16 * g2
                o[ti, 64 * g2:64 * g2 + 64, c0:c0 + 16] = cm[2 * ti + g2].T
        return o

    ld = f("ssm_log_dt")[0]
    shared = {
        "cache_k": ck, "cache_v": cv,
        "ada_w": f("ada_w")[0], "ada_b": f("ada_b")[0].reshape(1, -1),
        "norm_ffn1": f("norm_ffn1")[0].reshape(1, -1), "norm_mix": f("norm_mix")[0].reshape(1, -1),
        "norm_ffn2": f("norm_ffn2")[0].reshape(1, -1), "final_norm": f("final_norm").reshape(1, -1),
        "ffn1_w_in": f("ffn1_w_in")[0], "ffn1_w_out": f("ffn1_w_out")[0],
        "ffn2_w_in": f("ffn2_w_in")[0], "ffn2_w_out": f("ffn2_w_out")[0],
        "w_in": f("w_in")[0], "sb_bias": f("sb_bias")[0].reshape(1, 8),
        "lam_re": gp_layout(f("ssm_lambda_re")[0]), "lam_im": gp_layout(f("ssm_lambda_im")[0]),
        "logdt": np.ascontiguousarray(np.broadcast_to(ld.reshape(16, 2).T[:, None, :], (2, 64, 16)).reshape(128, 16)),
        "bfull_re": bfull(b_re), "bfull_im": bfull(b_im), "cfull_re": cfull(c_re), "cfull_im": cfull(c_im),
        "ssm_d": np.ascontiguousarray(f("ssm_d")[0].reshape(4, 128).T), "glu_w": f("glu_w")[0],
        "glu_b": np.ascontiguousarray(f("glu_b")[0].reshape(4, 128).T),
        "w_branch_a": f("w_branch_a")[0], "w_branch_b": f("w_branch_b")[0], "w_out": f("w_out")[0],
    }
    sbr = np.zeros((128, 1), np.float32)
    for cc in range(4):
        for hh in range(2):
            sbr[32 * cc + 8 * hh:32 * cc + 8 * hh + 8, 0] = f("sb_bias")[0, 2 * cc + hh]
    shared["sb_rows"] = sbr
    shared = {k: np.ascontiguousarray(v, dtype=np.float32) for k, v in shared.items()}
    cst = [_consts(0), _consts(1)]
    in_maps = []
    for core in range(8):
        seq, par = core // 2, core % 2
        m = dict(shared)
        m.update(cst[par])
        m["xp"] = np.ascontiguousarray(x_prompt[seq])
        m["xs"] = np.ascontiguousarray(x_sample[16 * core:16 * core + 16].reshape(128, D))
        m["cpr"] = np.ascontiguousarray(np.repeat(c_prompt[seq:seq + 1], 128, axis=0))
        m["csr"] = np.ascontiguousarray(np.repeat(c_sample[16 * core:16 * core + 16], 8, axis=0))
        m["ptab"] = np.ascontiguousarray(page_table[16 * core:16 * core + 16].reshape(1, 256))
        m["st_re"] = st_layout(s_re[16 * core:16 * core + 16]); m["st_im"] = st_layout(s_im[16 * core:16 * core + 16])
        in_maps.append(m)
    res = run_bass_kernel_spmd(nc, in_maps, core_ids=list(range(8))).results
    B = 4
    y_prompt = np.zeros((B, SEQ, D), np.float32); y_sample = np.zeros((128, 8, D), np.float32)
    nkp = np.zeros((1, B, SEQ, 8, 64), np.float32); nvp = np.zeros_like(nkp)
    srp = np.zeros((1, B, 32, 64), np.float32); sip = np.zeros_like(srp)
    nks = np.zeros((1, 128, 8, 8, 64), np.float32); nvs = np.zeros_like(nks)
    srs = np.zeros((1, 128, 32, 64), np.float32); sis = np.zeros_like(srs)

    def gp_back(a):
        return a.reshape(2, 64, 16).transpose(2, 0, 1).reshape(32, 64)

    def st_back(a):
        return a.reshape(2, 64, 16, 16).transpose(3, 2, 0, 1).reshape(16, 32, 64)

    for core in range(8):
        seq, par = core // 2, core % 2
        r = res[core]
        yp = r["y_p"].reshape(16, 128, D)
        y_prompt[seq].reshape(16, 2, 128, D)[:, par] = yp
        y_sample[16 * core:16 * core + 16] = r["y_s"].reshape(16, 8, D)
        if par == 0:
            nkp[0, seq] = r["kp"].reshape(SEQ, 8, 64); nvp[0, seq] = r["vp"].reshape(SEQ, 8, 64)
            srp[0, seq] = gp_back(r["sp_re"]); sip[0, seq] = gp_back(r["sp_im"])
        nks[0, 16 * core:16 * core + 16] = r["ks"].reshape(16, 8, 8, 64); nvs[0, 16 * core:16 * core + 16] = r["vs"].reshape(16, 8, 8, 64)
        srs[0, 16 * core:16 * core + 16] = st_back(r["ss_re"]); sis[0, 16 * core:16 * core + 16] = st_back(r["ss_im"])
    return (y_prompt, y_sample, nkp, nvp, srp, sip, nks, nvs, srs, sis)
```
